# Optimizing a Trainium2 kernel written in Bass

```python
import math
import jax, jax.numpy as jnp
from jax import lax
import numpy as np

D_MODEL = 2048
BATCH = 2
SEQ = 16384
DEPTH = 4

D_MIX = D_MODEL
GROUP_W = D_MIX // 4
D_FF = 5632
NORM_EPS = 1e-6
MASK_VALUE = -1e30
HG_HEADS = 4
HG_EXPAND = GROUP_W // HG_HEADS
HG_HEAD_V = GROUP_W // HG_HEADS
HG_CHUNK = 64
ATT_HEADS = 8
ATT_HEAD_DIM = GROUP_W // ATT_HEADS
ATT_PATTERNS = ((128, 1), (512, 4), (2048, 16))
ATT_BLOCK = 128
SSM_HEAD_DIM = 64
SSM_HEADS = GROUP_W // SSM_HEAD_DIM
SSM_GROUPS = 2
SSM_STATE = 128
SSM_CONV = 4
SSM_CHUNK = 64
SSM_CONV_DIM = GROUP_W + 2 * SSM_GROUPS * SSM_STATE
LRU_BLOCKS = 8
LRU_BLOCK_W = GROUP_W // LRU_BLOCKS
LRU_CONV = 4
LRU_C = 8.0
IN_WIDTHS = (GROUP_W, GROUP_W, GROUP_W, GROUP_W,
             GROUP_W, GROUP_W, GROUP_W,
             GROUP_W, SSM_CONV_DIM, SSM_HEADS,
             GROUP_W, GROUP_W)
N_IN = sum(IN_WIDTHS)

kernel_name = "hybrid_parallel_hgrn2_dilattn_ssd_rglru"


def rmsnorm(x, w):
    xf = x.astype(jnp.float32)
    y = xf * lax.rsqrt(jnp.mean(xf * xf, axis=-1, keepdims=True) + NORM_EPS)
    return (y * w.astype(jnp.float32)).astype(x.dtype)


def swiglu(x, w_gate, w_up, w_down):
    return (jax.nn.silu(x @ w_gate) * (x @ w_up)) @ w_down


def causal_dwconv(x, w, b):
    width, ch = w.shape
    y = lax.conv_general_dilated(x, w[:, None, :].astype(x.dtype), window_strides=(1,),
                                 padding=[(width - 1, 0)],
                                 dimension_numbers=("NWC", "WIO", "NWC"),
                                 feature_group_count=ch)
    return y + b.astype(x.dtype)


def hgrn2_chunk_scan(q, k, v, log_f):
    bsz, seq, heads, dk = q.shape
    dv = v.shape[-1]
    c = HG_CHUNK
    nc = seq // c

    def to_chunks(t):
        return t.reshape(bsz, nc, c, heads, t.shape[-1]).transpose(1, 0, 3, 2, 4)

    mask = jnp.tril(jnp.ones((c, c), dtype=bool))[:, :, None]

    def step(state, inp):
        qc, kc, vc, gc = inp
        b = jnp.cumsum(gc, axis=2)
        o_inter = jnp.einsum("bhtk,bhkv->bhtv", qc * jnp.exp(b), state)
        diff = b[:, :, :, None, :] - b[:, :, None, :, :]
        decay = jnp.where(mask, jnp.exp(jnp.where(mask, diff, 0.0)), 0.0)
        scores = jnp.einsum("bhtk,bhsk,bhtsk->bhts", qc, kc, decay)
        o = o_inter + jnp.einsum("bhts,bhsv->bhtv", scores, vc)
        b_last = b[:, :, -1:, :]
        state = (jnp.exp(b_last[:, :, 0, :])[..., None] * state
                 + jnp.einsum("bhsk,bhsv->bhkv", kc * jnp.exp(b_last - b), vc))
        return state, o

    init = jnp.zeros((bsz, heads, dk, dv), jnp.float32)
    _, o = lax.scan(step, init, (to_chunks(q), to_chunks(k), to_chunks(v), to_chunks(log_f)))
    return o.transpose(1, 0, 3, 2, 4).reshape(bsz, seq, heads, dv)


def hgrn2_mixer(q_raw, f_raw, i_raw, g_raw, lower_bound, norm_w):
    bsz, seq, _ = q_raw.shape
    out_dtype = q_raw.dtype

    def heads(t):
        return t.astype(jnp.float32).reshape(bsz, seq, HG_HEADS, -1)

    q = jax.nn.silu(heads(q_raw))
    z = heads(f_raw)
    lb = lower_bound.astype(jnp.float32).reshape(HG_HEADS, HG_EXPAND)
    f = lb + (1.0 - lb) * jax.nn.sigmoid(z)
    log_f = jnp.log(f)
    k = (1.0 - lb) * jax.nn.sigmoid(-z)
    v = heads(i_raw)
    o = hgrn2_chunk_scan(q, k, v, log_f)
    o = rmsnorm(o, norm_w.reshape(HG_HEADS, HG_HEAD_V)) * jax.nn.silu(heads(g_raw))
    return o.reshape(bsz, seq, GROUP_W).astype(out_dtype)


def dilated_pattern(q, k, v, dil, span):
    bsz, seq, heads, hd = q.shape
    qb_len = ATT_BLOCK
    unit = dil * qb_len
    seq_p = -(-seq // unit) * unit
    m_len = seq_p // dil
    nb = m_len // qb_len
    pad = ((0, 0), (0, seq_p - seq), (0, 0), (0, 0))

    def to_blocks(t):
        return (jnp.pad(t, pad).reshape(bsz, m_len, dil, heads, hd)
                .transpose(0, 2, 1, 3, 4).reshape(bsz, dil, nb, qb_len, heads, hd))

    def with_prev(t):
        prev = jnp.pad(t[:, :, :-1], ((0, 0), (0, 0), (1, 0), (0, 0), (0, 0), (0, 0)))
        return jnp.concatenate([prev, t], axis=3)

    qb = to_blocks(q)
    kk = with_prev(to_blocks(k))
    vv = with_prev(to_blocks(v))
    s = jnp.einsum("brnqhe,brnkhe->brnhqk", qb, kk) * (hd ** -0.5)
    iq = jnp.arange(qb_len)[:, None]
    ik = jnp.arange(2 * qb_len)[None, :]
    rel = qb_len + iq - ik
    blk = jnp.arange(nb)[:, None, None]
    valid = ((rel >= 0) & (rel <= span) & (blk * qb_len + ik[None] - qb_len >= 0))[:, None]
    s = jnp.where(valid, s, MASK_VALUE)
    m = jnp.max(s, axis=-1)
    p = jnp.where(valid, jnp.exp(s - m[..., None]), 0.0)
    den = jnp.sum(p, axis=-1)
    num = jnp.einsum("brnhqk,brnkhe->brnqhe", p, vv)
    num = (num.reshape(bsz, dil, m_len, heads, hd).transpose(0, 2, 1, 3, 4)
           .reshape(bsz, seq_p, heads, hd)[:, :seq])

    def stat_back(t):
        return (t.transpose(0, 1, 2, 4, 3).reshape(bsz, dil, m_len, heads)
                .transpose(0, 2, 1, 3).reshape(bsz, seq_p, heads)[:, :seq])

    return num, stat_back(m), stat_back(den)


def dilated_attention(q_raw, k_raw, v_raw, norm_w):
    bsz, seq, _ = q_raw.shape

    def heads(t):
        return t.astype(jnp.float32).reshape(bsz, seq, ATT_HEADS, ATT_HEAD_DIM)

    q, k, v = heads(q_raw), heads(k_raw), heads(v_raw)
    outs = [dilated_pattern(q, k, v, dil, win // dil) for (win, dil) in ATT_PATTERNS]
    nums = jnp.stack([o[0] for o in outs])
    ms = jnp.stack([o[1] for o in outs])
    dens = jnp.stack([o[2] for o in outs])
    wts = jnp.exp(ms - jnp.max(ms, axis=0))
    o = jnp.sum(wts[..., None] * nums, axis=0) / jnp.sum(wts * dens, axis=0)[..., None]
    return rmsnorm(o.reshape(bsz, seq, GROUP_W), norm_w).astype(q_raw.dtype)


def ssd_chunked(xdt, adt, bm, cm):
    bsz, seq, heads, hp = xdt.shape
    lc = SSM_CHUNK
    nc = seq // lc
    g = SSM_GROUPS
    r = heads // g
    xc = xdt.reshape(bsz, nc, lc, g, r, hp)
    a = adt.reshape(bsz, nc, lc, g, r).transpose(0, 3, 4, 1, 2)
    bc = bm.reshape(bsz, nc, lc, g, SSM_STATE)
    cc = cm.reshape(bsz, nc, lc, g, SSM_STATE)
    a_cs = jnp.cumsum(a, axis=-1)
    mask = jnp.tril(jnp.ones((lc, lc), dtype=bool))
    seg = a_cs[..., :, None] - a_cs[..., None, :]
    lmat = jnp.where(mask, jnp.exp(jnp.where(mask, seg, 0.0)), 0.0)
    cb = jnp.einsum("bclgn,bcsgn->bgcls", cc, bc)
    y_diag = jnp.einsum("bgcls,bgrcls,bcsgrp->bclgrp", cb, lmat, xc)
    decay_states = jnp.exp(a_cs[..., -1:] - a_cs)
    states = jnp.einsum("bcsgn,bgrcs,bcsgrp->bcgrpn", bc, decay_states, xc)
    chunk_decay = jnp.exp(a_cs[..., -1])

    def step(carry, inp):
        st, dec = inp
        return dec[..., None, None] * carry + st, carry

    init = jnp.zeros((bsz, g, r, hp, SSM_STATE), jnp.float32)
    _, prev = lax.scan(step, init, (states.transpose(1, 0, 2, 3, 4, 5),
                                    chunk_decay.transpose(3, 0, 1, 2)))
    prev = prev.transpose(1, 0, 2, 3, 4, 5)
    y_off = jnp.einsum("bclgn,bcgrpn,bgrcl->bclgrp", cc, prev, jnp.exp(a_cs))
    return (y_diag + y_off).reshape(bsz, seq, heads, hp)


def mamba2_mixer(z, xbc, dt_raw, conv_w, conv_b, dt_bias, a_log, d_skip, norm_w):
    bsz, seq, _ = z.shape
    xbc = jax.nn.silu(causal_dwconv(xbc, conv_w, conv_b)).astype(jnp.float32)
    xs, bm, cm = jnp.split(xbc, [GROUP_W, GROUP_W + SSM_GROUPS * SSM_STATE], axis=-1)
    x = xs.reshape(bsz, seq, SSM_HEADS, SSM_HEAD_DIM)
    bm = bm.reshape(bsz, seq, SSM_GROUPS, SSM_STATE)
    cm = cm.reshape(bsz, seq, SSM_GROUPS, SSM_STATE)
    dt = jax.nn.softplus(dt_raw.astype(jnp.float32) + dt_bias.astype(jnp.float32))
    a = -jnp.exp(a_log.astype(jnp.float32))
    y = ssd_chunked(x * dt[..., None], dt * a, bm, cm)
    y = y + d_skip.astype(jnp.float32)[:, None] * x
    y = y.reshape(bsz, seq, GROUP_W) * jax.nn.silu(z.astype(jnp.float32))
    y = rmsnorm(y.reshape(bsz, seq, SSM_GROUPS, GROUP_W // SSM_GROUPS),
                norm_w.reshape(SSM_GROUPS, GROUP_W // SSM_GROUPS))
    return y.reshape(bsz, seq, GROUP_W).astype(z.dtype)


def _lin_combine(e1, e2):
    a1, b1 = e1
    a2, b2 = e2
    return a1 * a2, a2 * b1 + b2


def rglru_mixer(xb, gb, conv_w, conv_b, w_a, b_a, w_x, b_x, a_param, norm_w):
    bsz, seq, _ = xb.shape
    xc = causal_dwconv(xb, conv_w, conv_b).astype(jnp.float32)
    xh = xc.reshape(bsz, seq, LRU_BLOCKS, LRU_BLOCK_W)
    r = jax.nn.sigmoid(jnp.einsum("bshi,hij->bshj", xh, w_a.astype(jnp.float32)) + b_a.astype(jnp.float32))
    i = jax.nn.sigmoid(jnp.einsum("bshi,hij->bshj", xh, w_x.astype(jnp.float32)) + b_x.astype(jnp.float32))
    log_a = -LRU_C * r * jax.nn.softplus(-a_param.astype(jnp.float32).reshape(LRU_BLOCKS, LRU_BLOCK_W))
    a = jnp.exp(log_a)
    bterm = jnp.sqrt(jnp.maximum(-jnp.expm1(2.0 * log_a), 0.0)) * (i * xh)
    _, h = lax.associative_scan(_lin_combine, (a, bterm), axis=1)
    h = h.reshape(bsz, seq, GROUP_W) * jax.nn.gelu(gb.astype(jnp.float32))
    return rmsnorm(h, norm_w).astype(xb.dtype)


def setup_inputs(seed: int = 0) -> dict:
    key = jax.random.key(seed)
    ks = iter(jax.random.split(key, 40))
    f32 = jnp.float32

    def nrm(shape, scale):
        return jax.random.normal(next(ks), shape, f32) * scale

    def gain(shape):
        return 1.0 + nrm(shape, 0.02)

    L = DEPTH
    x = nrm((BATCH, SEQ, D_MODEL), 1.0)
    ffn1_norm = gain((L, D_MODEL))
    ffn1_w_gate = nrm((L, D_MODEL, D_FF), D_MODEL ** -0.5)
    ffn1_w_up = nrm((L, D_MODEL, D_FF), D_MODEL ** -0.5)
    ffn1_w_down = nrm((L, D_FF, D_MODEL), D_FF ** -0.5)
    mix_norm = gain((L, D_MODEL))
    w_in = nrm((L, D_MODEL, N_IN), D_MODEL ** -0.5)
    w_out = nrm((L, D_MIX, D_MODEL), D_MIX ** -0.5)
    hgrn_lb_logits = nrm((L, GROUP_W), 0.5)
    hgrn_norm = gain((L, GROUP_W))
    attn_norm = gain((L, GROUP_W))
    ssm_conv_w = nrm((L, SSM_CONV, SSM_CONV_DIM), SSM_CONV ** -0.5)
    ssm_conv_b = nrm((L, SSM_CONV_DIM), 0.01)
    dt0 = jnp.exp(jax.random.uniform(next(ks), (L, SSM_HEADS), f32,
                                     minval=math.log(1e-3), maxval=math.log(1e-1)))
    ssm_dt_bias = dt0 + jnp.log(-jnp.expm1(-dt0))
    ssm_a_log = jnp.log(jax.random.uniform(next(ks), (L, SSM_HEADS), f32, minval=1.0, maxval=16.0))
    ssm_d = gain((L, SSM_HEADS))
    ssm_norm = gain((L, GROUP_W))
    lru_conv_w = nrm((L, LRU_CONV, GROUP_W), LRU_CONV ** -0.5)
    lru_conv_b = nrm((L, GROUP_W), 0.01)
    lru_w_a = nrm((L, LRU_BLOCKS, LRU_BLOCK_W, LRU_BLOCK_W), LRU_BLOCK_W ** -0.5)
    lru_b_a = nrm((L, LRU_BLOCKS, LRU_BLOCK_W), 0.01)
    lru_w_x = nrm((L, LRU_BLOCKS, LRU_BLOCK_W, LRU_BLOCK_W), LRU_BLOCK_W ** -0.5)
    lru_b_x = nrm((L, LRU_BLOCKS, LRU_BLOCK_W), 0.01)
    a_c = jax.random.uniform(next(ks), (L, GROUP_W), f32, minval=0.9, maxval=0.999)
    s = a_c ** (1.0 / LRU_C)
    lru_a_param = jnp.log(s) - jnp.log1p(-s)
    lru_norm = gain((L, GROUP_W))
    ffn2_norm = gain((L, D_MODEL))
    ffn2_w_gate = nrm((L, D_MODEL, D_FF), D_MODEL ** -0.5)
    ffn2_w_up = nrm((L, D_MODEL, D_FF), D_MODEL ** -0.5)
    ffn2_w_down = nrm((L, D_FF, D_MODEL), D_FF ** -0.5)
    final_norm = gain((D_MODEL,))
    return {"x": x, "ffn1_norm": ffn1_norm, "ffn1_w_gate": ffn1_w_gate, "ffn1_w_up": ffn1_w_up,
            "ffn1_w_down": ffn1_w_down, "mix_norm": mix_norm, "w_in": w_in, "w_out": w_out,
            "hgrn_lb_logits": hgrn_lb_logits, "hgrn_norm": hgrn_norm, "attn_norm": attn_norm,
            "ssm_conv_w": ssm_conv_w, "ssm_conv_b": ssm_conv_b, "ssm_dt_bias": ssm_dt_bias,
            "ssm_a_log": ssm_a_log, "ssm_d": ssm_d, "ssm_norm": ssm_norm,
            "lru_conv_w": lru_conv_w, "lru_conv_b": lru_conv_b, "lru_w_a": lru_w_a, "lru_b_a": lru_b_a,
            "lru_w_x": lru_w_x, "lru_b_x": lru_b_x, "lru_a_param": lru_a_param, "lru_norm": lru_norm,
            "ffn2_norm": ffn2_norm, "ffn2_w_gate": ffn2_w_gate, "ffn2_w_up": ffn2_w_up,
            "ffn2_w_down": ffn2_w_down, "final_norm": final_norm}


def reference(x, ffn1_norm, ffn1_w_gate, ffn1_w_up, ffn1_w_down, mix_norm, w_in, w_out,
              hgrn_lb_logits, hgrn_norm, attn_norm, ssm_conv_w, ssm_conv_b, ssm_dt_bias,
              ssm_a_log, ssm_d, ssm_norm, lru_conv_w, lru_conv_b, lru_w_a, lru_b_a,
              lru_w_x, lru_b_x, lru_a_param, lru_norm, ffn2_norm, ffn2_w_gate, ffn2_w_up,
              ffn2_w_down, final_norm):
    split_points = np.cumsum(np.array(IN_WIDTHS))[:-1].tolist()
    lb_logits = hgrn_lb_logits.astype(jnp.float32)
    lb_e = jnp.exp(lb_logits - jnp.max(lb_logits, axis=0, keepdims=True))
    lb_p = lb_e / jnp.sum(lb_e, axis=0, keepdims=True)
    lower_bounds = jnp.cumsum(lb_p, axis=0) - lb_p[0]
    for l in range(DEPTH):
        x = x + 0.5 * swiglu(rmsnorm(x, ffn1_norm[l]), ffn1_w_gate[l], ffn1_w_up[l], ffn1_w_down[l])
        h = rmsnorm(x, mix_norm[l])
        proj = h @ w_in[l]
        (a_q, a_f, a_i, a_g, b_q, b_k, b_v, c_z, c_xbc, c_dt, d_x, d_g) = jnp.split(proj, split_points, axis=-1)
        y_a = hgrn2_mixer(a_q, a_f, a_i, a_g, lower_bounds[l], hgrn_norm[l])
        y_b = dilated_attention(b_q, b_k, b_v, attn_norm[l])
        y_c = mamba2_mixer(c_z, c_xbc, c_dt, ssm_conv_w[l], ssm_conv_b[l], ssm_dt_bias[l],
                           ssm_a_log[l], ssm_d[l], ssm_norm[l])
        y_d = rglru_mixer(d_x, d_g, lru_conv_w[l], lru_conv_b[l], lru_w_a[l], lru_b_a[l],
                          lru_w_x[l], lru_b_x[l], lru_a_param[l], lru_norm[l])
        y = jnp.concatenate([y_a, y_b, y_c, y_d], axis=-1).astype(x.dtype)
        x = x + y @ w_out[l]
        x = x + 0.5 * swiglu(rmsnorm(x, ffn2_norm[l]), ffn2_w_gate[l], ffn2_w_up[l], ffn2_w_down[l])
    return rmsnorm(x, final_norm)
```

```python
import numpy as np
import concourse.bass as bass
import concourse.mybir as mybir
from concourse.bass_utils import run_bass_kernel_spmd

F32, BF16 = mybir.dt.float32, mybir.dt.bfloat16
AF = mybir.ActivationFunctionType
ALU = mybir.AluOpType

D = 2048
DFF = 5632
NIN = 6152
T = 512
KC = 16
FCH = 44
EPS = 1e-6
ENGS = ("pe", "act", "dve", "pool", "sp")


class Sched:
    def __init__(self):
        self.ops = []
        self.cnt = {e: 0 for e in ENGS}
        self.dcnt = {}
        self.last = None

    def op(self, eng, fn, waits=()):
        w = [x for x in waits if x is not None]
        if self.last is not None:
            w.append(self.last)
        self.cnt[eng] += 1
        tok = ("c_" + eng, self.cnt[eng])
        self.ops.append((eng, fn, w, tok, 1))
        self.last = tok
        return tok

    def dma(self, q, fn, dsem, waits=()):
        w = [x for x in waits if x is not None]
        if self.last is not None:
            w.append(self.last)
        self.dcnt[dsem] = self.dcnt.get(dsem, 0) + 16
        tok = (dsem, self.dcnt[dsem])
        self.ops.append((q, fn, w, tok, 16))
        return tok


MIX = 'ABCD'
DBG = False
DBGMAP = {}
LAST = None


def build(S, DEPTH, dbg=None):
    NT = S // T
    nc = bass.Bass("TRN2", target_bir_lowering=False)

    def din(name, shape, dt=F32):
        return nc.dram_tensor(name, list(shape), dt, kind="ExternalInput").ap()

    x_in = din("x", [S, D])
    out = nc.dram_tensor("out", [S, D], F32, kind="ExternalOutput").ap()
    dbgout = nc.dram_tensor("dbg", [128, 8192], F32, kind="ExternalOutput").ap() if DBG else None
    dbgpos = [0]
    dbgmap = {}
    wsrc = {}
    for nm, shp in (("ffn1_w_gate", [DEPTH, D, DFF]), ("ffn1_w_up", [DEPTH, D, DFF]),
                    ("ffn1_w_down", [DEPTH, DFF, D]), ("ffn2_w_gate", [DEPTH, D, DFF]),
                    ("ffn2_w_up", [DEPTH, D, DFF]), ("ffn2_w_down", [DEPTH, DFF, D]),
                    ("w_in", [DEPTH, D, NIN]), ("w_out", [DEPTH, D, D])):
        wsrc[nm] = din(nm, shp)
    consts = din("consts", [128, 512])
    pvec = din("pvec", [128, 64 * DEPTH + 16])
    pvm_in = din("pvm", [128, 128 * DEPTH])
    lbd_in = din("lbd", [DEPTH, 128, 8, 128])
    amask_in = din("amask", [128, 17 * 128])
    anw_in = din("anw", [DEPTH, 128, 512])
    ssmb_in = din("ssmb", [DEPTH, 128, 528])
    lbl_in = din("lbl", [128, DEPTH * 4])

    xs = nc.dram_tensor("xs", [KC, 128, S], F32).ap()
    wb = {}
    for l in range(DEPTH):
        for f in ("ffn1", "ffn2"):
            wb[(l, f, "g")] = nc.dram_tensor(f"wb_{l}_{f}_g", [22, 128, KC * 256], BF16).ap()
            wb[(l, f, "u")] = nc.dram_tensor(f"wb_{l}_{f}_u", [22, 128, KC * 256], BF16).ap()
            wb[(l, f, "d")] = nc.dram_tensor(f"wb_{l}_{f}_d", [16, 128, FCH * 128], BF16).ap()
        wb[(l, "in")] = nc.dram_tensor(f"wb_{l}_in", [24, 128, KC * 256], BF16).ap()
        wb[(l, "dt")] = nc.dram_tensor(f"wb_{l}_dt", [128, KC * 8], BF16).ap()
        wb[(l, "out")] = nc.dram_tensor(f"wb_{l}_out", [16, 128, KC * 128], BF16).ap()

    off = [16640]

    def sb(name, shape, dt, at=None):
        nbytes = int(np.prod(shape[1:])) * (4 if dt == F32 else 2)
        if at is None:
            o = off[0]
            off[0] += (nbytes + 63) // 64 * 64
        else:
            o = at
        return nc.alloc_sbuf_tensor_at(name, list(shape), dt, offset=o)

    XT = sb("XT", [128, KC, T], F32)
    HT = sb("HT", [128, KC, T], BF16)
    arena0 = off[0]
    AT = sb("AT", [128, FCH, T], BF16)
    WGU_OFF = off[0]
    WGU = [sb(f"WGU{i}", [128, KC, 256], BF16) for i in range(4)]
    WD = [nc.alloc_sbuf_tensor_at(f"WD{i}", [128, FCH, 128], BF16, offset=WGU_OFF + i * 16384) for i in range(2)]
    WDBASE = off[0]
    TBSIZE = 14336
    off[0] += TBSIZE
    CON = sb("CON", [128, 512], F32)
    PV = sb("PV", [128, 64 * DEPTH + 16], F32)
    RSTD = sb("RSTD", [128, T], F32)
    SG = sb("SG", [128, T], F32)
    assert off[0] <= 229344, off[0]
    STG = nc.alloc_sbuf_tensor_at("STG", [128, 4, D], F32, offset=arena0)
    SQ = nc.alloc_sbuf_tensor_at("SQ", [128, KC, T], BF16, offset=arena0)
    CVI = [nc.alloc_sbuf_tensor_at(f"CVI{i}", [128, 2048], F32, offset=arena0 + i * 8192) for i in range(2)]
    CVO = [nc.alloc_sbuf_tensor_at(f"CVO{i}", [128, 2048], BF16, offset=arena0 + 16384 + i * 4096) for i in range(2)]

    IDF = CON[:, 0:128]
    ONESB = None

    ps = [nc.alloc_psum_tensor(f"ps{i}", [128, 512], F32) for i in range(8)]

    sc = Sched()

    def load_consts():
        t1 = sc.dma("sp", lambda e: e.dma_start(out=CON[:], in_=consts[:, :]), "d_misc")
        t2 = sc.dma("sp", lambda e: e.dma_start(out=PV[:], in_=pvec[:, :]), "d_misc")
        t2 = sc.dma("sp", lambda e: e.dma_start(out=PVM[:], in_=pvm_in[:, :]), "d_misc")
        return t2

    ONESB_t = sb("ONESB", [128, 128], BF16)
    IDB_t = sb("IDB", [128, 128], BF16)

    def convert_all():
        it = [0]
        store_tok = [None, None]

        def conv_block(src_ap, dst_ap, ncols, nsl, sw):
            i = it[0] % 2
            it[0] += 1
            ld = sc.dma("sp", lambda e: e.dma_start(out=CVI[i][:, 0:ncols], in_=src_ap), f"d_cvi{i}")
            eng = ("dve", "pool", "act")[it[0] % 3]

            def cast(e):
                if eng == "act":
                    return e.copy(out=CVO[i][:, 0:ncols], in_=CVI[i][:, 0:ncols])
                return e.tensor_copy(out=CVO[i][:, 0:ncols], in_=CVI[i][:, 0:ncols])
            sc.op(eng, cast, waits=[ld, store_tok[i]])
            store_tok[i] = sc.dma(
                "sp", lambda e: e.dma_start(out=dst_ap, in_=CVO[i][:, 0:ncols].rearrange("p (s c) -> p s c", c=sw)),
                f"d_cvo{i}")

        for l in range(DEPTH):
            for f in ("ffn1", "ffn2"):
                for nm, key in ((f + "_w_gate", "g"), (f + "_w_up", "u")):
                    src = wsrc[nm]
                    dst = wb[(l, f, key)].rearrange("s p (k c) -> s p k c", c=256)
                    for kc in range(KC):
                        for cb in range(0, DFF, 2048):
                            ncols = min(2048, DFF - cb)
                            s0 = cb // 256
                            nsl = ncols // 256
                            conv_block(src[l, kc * 128:(kc + 1) * 128, cb:cb + ncols],
                                       dst[s0:s0 + nsl, :, kc, :].rearrange("s p c -> p s c"), ncols, nsl, 256)
                src = wsrc[f + "_w_down"]
                dst = wb[(l, f, "d")].rearrange("s p (k c) -> s p k c", c=128)
                for kc in range(FCH):
                    conv_block(src[l, kc * 128:(kc + 1) * 128, :],
                               dst[:, :, kc, :].rearrange("s p c -> p s c"), 2048, 16, 128)
            src = wsrc["w_in"]
            dst = wb[(l, "in")].rearrange("s p (k c) -> s p k c", c=256)
            for kc in range(KC):
                for (c0, d0, ncols) in ((0, 0, 2048), (2048, 2048, 2048), (4096, 4096, 1024), (5128, 5120, 1024)):
                    s0 = d0 // 256
                    nsl = ncols // 256
                    conv_block(src[l, kc * 128:(kc + 1) * 128, c0:c0 + ncols],
                               dst[s0:s0 + nsl, :, kc, :].rearrange("s p c -> p s c"), ncols, nsl, 256)
                dstd = wb[(l, "dt")].rearrange("p (k c) -> p k c", c=8)
                conv_block(src[l, kc * 128:(kc + 1) * 128, 5120:5128], dstd[:, kc:kc + 1, :], 8, 1, 8)
            src = wsrc["w_out"]
            dst = wb[(l, "out")].rearrange("s p (k c) -> s p k c", c=128)
            for kc in range(KC):
                conv_block(src[l, kc * 128:(kc + 1) * 128, :],
                           dst[:, :, kc, :].rearrange("s p c -> p s c"), 2048, 16, 128)
        sc.op("dve", lambda e: e.tensor_copy(out=SG[:, 0:8], in_=SG[:, 8:16]), waits=[store_tok[0], store_tok[1]])

    def load_x_tile(l, ti):
        t0 = ti * T
        if l == 0:
            ld = sc.dma("sp", lambda e: e.dma_start(
                out=STG[:], in_=x_in[t0:t0 + T, :].rearrange("(c p) d -> p c d", p=128)), "d_x")
            first = True
            for kc in range(KC):
                def tr(e, kc=kc):
                    ins = None
                    for c in range(4):
                        ins = e.transpose(ps[kc % 2][:, c * 128:(c + 1) * 128], STG[:, c, kc * 128:(kc + 1) * 128], IDF)
                    return ins
                sc.op("pe", tr, waits=[ld] if first else [])
                first = False
                sc.op("act" if kc % 2 else "dve",
                      (lambda e, kc=kc: e.copy(out=XT[:, kc, :], in_=ps[kc % 2][:])) if kc % 2 else
                      (lambda e, kc=kc: e.tensor_copy(out=XT[:, kc, :], in_=ps[kc % 2][:])))
        else:
            ld = sc.dma("sp", lambda e: e.dma_start(
                out=XT[:], in_=xs[:, :, t0:t0 + T].rearrange("k p t -> p k t")), "d_x")
            sc.op("dve", lambda e: e.tensor_copy(out=SG[:, 0:8], in_=SG[:, 8:16]), waits=[ld])

    def store_x_tile(ti):
        t0 = ti * T
        st = sc.dma("sp", lambda e: e.dma_start(
            out=xs[:, :, t0:t0 + T].rearrange("k p t -> p k t"), in_=XT[:]), "d_xst")
        sc.op("dve", lambda e: e.tensor_copy(out=SG[:, 0:8], in_=SG[:, 8:16]), waits=[st])

    def rmsnorm_to(dst_fn, wcol):
        sc.op("act", lambda e: e.activation(out=SQ[:], in_=XT[:], func=AF.Square))

        def mm(e):
            ins = None
            for kc in range(KC):
                ins = e.matmul(ps[2][:], ONESB_t[:], SQ[:, kc, :], start=(kc == 0), stop=(kc == KC - 1))
            return ins
        sc.op("pe", mm)

        sc.op("dve", lambda e: e.tensor_scalar(out=RSTD[:], in0=ps[2][:], scalar1=1.0 / D, scalar2=EPS, op0=ALU.mult, op1=ALU.add))
        sc.op("act", lambda e: e.sqrt(out=RSTD[:], in_=RSTD[:]))
        sc.op("dve", lambda e: e.reciprocal(out=RSTD[:], in_=RSTD[:]))

        def hh(e):
            ins = None
            for kc in range(KC):
                ins = e.scalar_tensor_tensor(out=dst_fn(kc), in0=XT[:, kc, :], scalar=PV[:, wcol + kc:wcol + kc + 1],
                                             in1=RSTD[:], op0=ALU.mult, op1=ALU.mult)
            return ins
        sc.op("dve", hh)

    wgu_free = [None] * 4
    wd_free = [None] * 2

    def ffn(l, f, wcol):
        rmsnorm_to(lambda kc: HT[:, kc, :], wcol)
        wg, wu, wd = wb[(l, f, "g")], wb[(l, f, "u")], wb[(l, f, "d")]
        loads = {}

        def issue_gu(s):
            b = (s % 2) * 2
            tg = sc.dma("sp", lambda e: e.dma_start(out=WGU[b][:].rearrange("p k c -> p (k c)"), in_=wg[s]), f"d_wgu{b}")
            tu = sc.dma("sp", lambda e: e.dma_start(out=WGU[b + 1][:].rearrange("p k c -> p (k c)"), in_=wu[s]), f"d_wgu{b + 1}")
            loads[s] = (tg, tu)
        issue_gu(0)
        for s in range(22):
            if s + 1 < 22:
                issue_gu(s + 1)
            b = (s % 2) * 2
            for j in range(2):
                fc = s * 2 + j

                def mm(e, b=b, j=j):
                    ins = None
                    for kc in range(KC):
                        ins = e.matmul(ps[0][:], WGU[b][:, kc, j * 128:(j + 1) * 128], HT[:, kc, :],
                                       start=(kc == 0), stop=(kc == KC - 1))
                    for kc in range(KC):
                        ins = e.matmul(ps[1][:], WGU[b + 1][:, kc, j * 128:(j + 1) * 128], HT[:, kc, :],
                                       start=(kc == 0), stop=(kc == KC - 1))
                    return ins
                sc.op("pe", mm, waits=list(loads[s]) if j == 0 else [])
                sc.op("act", lambda e: e.activation(out=SG[:], in_=ps[0][:], func=AF.Silu))
                sc.op("dve", lambda e, fc=fc: e.tensor_tensor(out=AT[:, fc, :], in0=SG[:], in1=ps[1][:], op=ALU.mult))
        dl = {}

        def issue_d(oc):
            b = oc % 2
            dl[oc] = sc.dma("sp", lambda e: e.dma_start(out=WD[b][:].rearrange("p k c -> p (k c)"), in_=wd[oc]), f"d_wd{b}")
        issue_d(0)
        for oc in range(16):
            if oc + 1 < 16:
                issue_d(oc + 1)
            b = oc % 2

            def mm(e, b=b):
                ins = None
                for k in range(FCH):
                    ins = e.matmul(ps[3][:], WD[b][:, k, :], AT[:, k, :], start=(k == 0), stop=(k == FCH - 1))
                return ins
            sc.op("pe", mm, waits=[dl[oc]])
            sc.op("dve", lambda e, oc=oc: e.scalar_tensor_tensor(out=XT[:, oc, :], in0=ps[3][:], scalar=0.5,
                                                                 in1=XT[:, oc, :], op0=ALU.mult, op1=ALU.add))

    def final_store(ti):
        t0 = ti * T
        FN = nc.alloc_sbuf_tensor_at(f"FN{ti}", [128, T], F32, offset=arena0 + 32768)
        sc.op("act", lambda e: e.activation(out=HT[:], in_=XT[:], func=AF.Square))

        def mm(e):
            ins = None
            for kc in range(KC):
                ins = e.matmul(ps[2][:], ONESB_t[:], HT[:, kc, :], start=(kc == 0), stop=(kc == KC - 1))
            return ins
        sc.op("pe", mm)

        sc.op("dve", lambda e: e.tensor_scalar(out=RSTD[:], in0=ps[2][:], scalar1=1.0 / D, scalar2=EPS, op0=ALU.mult, op1=ALU.add))
        sc.op("act", lambda e: e.sqrt(out=RSTD[:], in_=RSTD[:]))
        sc.op("dve", lambda e: e.reciprocal(out=RSTD[:], in_=RSTD[:]))
        wcol = 64 * DEPTH
        for kc in range(KC):
            sc.op("dve", lambda e, kc=kc: e.scalar_tensor_tensor(out=FN[:], in0=XT[:, kc, :], scalar=PV[:, wcol + kc:wcol + kc + 1],
                                                                 in1=RSTD[:], op0=ALU.mult, op1=ALU.mult))

            def tr(e, kc=kc):
                ins = None
                for c in range(4):
                    ins = e.transpose(ps[4][:, c * 128:(c + 1) * 128], FN[:, c * 128:(c + 1) * 128], IDF)
                return ins
            sc.op("pe", tr)
            sc.op("act", lambda e, kc=kc: e.copy(out=STG[:, :, kc * 128:(kc + 1) * 128],
                                                 in_=ps[4][:].rearrange("p (c t) -> p c t", c=4)))
        st = sc.dma("sp", lambda e: e.dma_start(out=out[t0:t0 + T, :].rearrange("(c p) d -> p c d", p=128), in_=STG[:]), "d_out")
        sc.op("dve", lambda e: e.tensor_copy(out=SG[:, 0:8], in_=SG[:, 8:16]), waits=[st])


    PVM = sb("PVM", [128, 128 * DEPTH], F32)
    LBD = sb("LBD", [128, 2, 4, 128], BF16)
    HALO_D = sb("HALO_D", [128, 4, 4], F32)
    HPREV = sb("HPREV", [128, 4], F32)
    assert off[0] <= 229344, off[0]
    wdbase = 16640 + 0
    uid = [0]

    class Tmp:
        def __init__(self, base, size):
            self.base, self.size, self.o, self.n = base, size, 0, 0
        def get(self, shape, dt):
            nbytes = int(np.prod(shape[1:])) * (4 if dt == F32 else 2)
            nbytes = (nbytes + 63) // 64 * 64
            assert self.o + nbytes <= self.size, (self.o, nbytes, self.size)
            self.n += 1
            uid[0] += 1
            t = nc.alloc_sbuf_tensor_at(f"tmp{uid[0]}", list(shape), dt, offset=self.base + self.o)
            self.o += nbytes
            return t
    YT = nc.alloc_sbuf_tensor_at("YT", [128, KC, T], BF16, offset=arena0)
    wd_off = [None]

    win_i = [0]

    def win_slab(l, s):
        b = win_i[0] % 4
        win_i[0] += 1
        tok = sc.dma("sp", lambda e: e.dma_start(out=WGU[b][:].rearrange("p k c -> p (k c)"), in_=wb[(l, "in")][s]), f"d_wgu{b}")
        return b, tok

    def proj_fm(b, j, pst, tok=None):
        def mm(e):
            ins = None
            for kc in range(KC):
                ins = e.matmul(pst[:], WGU[b][:, kc, j * 128:(j + 1) * 128], HT[:, kc, :], start=(kc == 0), stop=(kc == KC - 1))
            return ins
        sc.op("pe", mm, waits=[tok])

    def mixer_lru(l, ti, TA, TB):
        pc = 128 * l
        YD = TA.get([128, 4, T], F32)
        XB = TB.get([128, T + 4], F32)
        XC = TB.get([128, T], F32)
        XCB = TB.get([128, T], BF16)
        R = TB.get([128, T], F32)
        I = TB.get([128, T], F32)
        A = TB.get([128, T], F32)
        G = TB.get([128, T], F32)
        slabs = {}
        for j in range(4):
            sx = 20 + j // 2
            sg = 22 + j // 2
            if j % 2 == 0:
                slabs["x"] = win_slab(l, sx)
                slabs["g"] = win_slab(l, sg)
            bx, tx = slabs["x"]
            bg, tg = slabs["g"]
            proj_fm(bx, j % 2, ps[0], tx)
            sc.op("act", lambda e: e.copy(out=XB[:, 3:T + 3], in_=ps[0][:]))
            sc.op("dve", lambda e, j=j: e.tensor_copy(out=XB[:, 0:3], in_=HALO_D[:, j, 0:3]))
            sc.op("dve", lambda e, j=j: e.tensor_copy(out=HALO_D[:, j, 0:3], in_=XB[:, T:T + 3]))

            def conv(e, j=j):
                e.tensor_scalar(out=XC[:], in0=XB[:, 0:T], scalar1=PVM[:, pc + j * 4:pc + j * 4 + 1],
                                scalar2=PVM[:, pc + 16 + j:pc + 17 + j], op0=ALU.mult, op1=ALU.add)
                ins = None
                for tp in range(1, 4):
                    ins = e.scalar_tensor_tensor(out=XC[:], in0=XB[:, tp:tp + T], scalar=PVM[:, pc + j * 4 + tp:pc + j * 4 + tp + 1],
                                                 in1=XC[:], op0=ALU.mult, op1=ALU.add)
                return ins
            sc.op("dve", conv)
            sc.op("act", lambda e: e.copy(out=XCB[:], in_=XC[:]))
            sc.op("pe", lambda e, j=j: e.matmul(ps[1][:], LBD[:, 0, j, :], XCB[:], start=True, stop=True))
            sc.op("act", lambda e, j=j: e.activation(out=R[:], in_=ps[1][:], func=AF.Sigmoid, bias=PVM[:, pc + 20 + j:pc + 21 + j]))
            sc.op("pe", lambda e, j=j: e.matmul(ps[1][:], LBD[:, 1, j, :], XCB[:], start=True, stop=True))
            sc.op("act", lambda e, j=j: e.activation(out=I[:], in_=ps[1][:], func=AF.Sigmoid, bias=PVM[:, pc + 24 + j:pc + 25 + j]))
            sc.op("act", lambda e, j=j: e.activation(out=A[:], in_=R[:], func=AF.Exp, scale=PVM[:, pc + 40 + j:pc + 41 + j]))

            def bt(e):
                e.tensor_tensor(out=R[:], in0=A[:], in1=A[:], op=ALU.mult)
                e.tensor_scalar(out=R[:], in0=R[:], scalar1=-1.0, scalar2=1.0, op0=ALU.mult, op1=ALU.add)
                return e.tensor_scalar(out=R[:], in0=R[:], scalar1=0.0, scalar2=None, op0=ALU.max)
            sc.op("dve", bt)
            sc.op("act", lambda e: e.sqrt(out=R[:], in_=R[:]))

            def bt2(e):
                e.tensor_tensor(out=I[:], in0=I[:], in1=XC[:], op=ALU.mult)
                return e.tensor_tensor(out=I[:], in0=I[:], in1=R[:], op=ALU.mult)
            sc.op("dve", bt2)
            sc.op("dve", lambda e, j=j: e.tensor_tensor_scan(out=XC[:], data0=A[:], data1=I[:], initial=HPREV[:, j:j + 1],
                                                              op0=ALU.mult, op1=ALU.add))
            sc.op("dve", lambda e, j=j: e.tensor_copy(out=HPREV[:, j:j + 1], in_=XC[:, T - 1:T]))
            proj_fm(bg, j % 2, ps[0], tg)
            sc.op("act", lambda e: e.activation(out=G[:], in_=ps[0][:], func=AF.Gelu))
            sc.op("dve", lambda e, j=j: e.tensor_tensor(out=YD[:, j, :], in0=XC[:], in1=G[:], op=ALU.mult))
        group_norm_to_yt(YD, [0, 1, 2, 3], 12, pc + 32, TA)

    def group_norm_to_yt(YD, chunks, ybase, wcol, TB):
        n = len(chunks)
        SQm = TB.get([128, n, T], BF16)
        RS = TB.get([128, T], F32)
        sc.op("act", lambda e: e.activation(out=SQm[:], in_=YD[:, chunks[0]:chunks[0] + n, :], func=AF.Square))

        def mm(e):
            ins = None
            for i in range(n):
                ins = e.matmul(ps[2][:], ONESB_t[:], SQm[:, i, :], start=(i == 0), stop=(i == n - 1))
            return ins
        sc.op("pe", mm)
        sc.op("dve", lambda e: e.tensor_scalar(out=RS[:], in0=ps[2][:], scalar1=1.0 / (128 * n), scalar2=EPS, op0=ALU.mult, op1=ALU.add))
        sc.op("act", lambda e: e.sqrt(out=RS[:], in_=RS[:]))
        sc.op("dve", lambda e: e.reciprocal(out=RS[:], in_=RS[:]))

        def hh(e):
            ins = None
            for i, c in enumerate(chunks):
                ins = e.scalar_tensor_tensor(out=YT[:, ybase + i, :], in0=YD[:, c, :], scalar=PVM[:, wcol + i:wcol + i + 1],
                                             in1=RS[:], op0=ALU.mult, op1=ALU.mult)
            return ins
        sc.op("dve", hh)


    MASK = sb("MASK", [128, 17, 128], BF16)
    KTH = sb("KTH", [128, 4, 2560], BF16)
    VH = sb("VH", [128, 20, 8, 72], BF16)
    WBA = sb("WBA", [128, 512], F32)
    assert off[0] <= 229344, off[0]

    def attn_init():
        ld = sc.dma("sp", lambda e: e.dma_start(out=STG[:, 0, :].rearrange("p (a b) -> p a b", b=128)[:, 0:17, :] if False else XT[:, 0:5, :].rearrange("p a b -> p (a b)")[:, 0:17 * 128], in_=amask_in[:, :]), "d_misc")
        sc.op("dve", lambda e: e.tensor_copy(out=MASK[:].rearrange("p a b -> p (a b)"), in_=XT[:, 0:5, :].rearrange("p a b -> p (a b)")[:, 0:17 * 128]), waits=[ld])
        sc.op("dve", lambda e: e.memset(VH[:], 1.0))

    def dump(name, ap, ncols):
        if not DBG:
            return
        DSTG = nc.alloc_sbuf_tensor_at(f"DSTG{dbgpos[0]}", [128, 2048], F32, offset=arena0 + 16384)
        c0 = dbgpos[0]
        dbgpos[0] += ncols
        DBGMAP[name] = (c0, ncols)
        sc.op("dve", lambda e: e.tensor_copy(out=DSTG[:, 0:ncols], in_=ap))
        st = sc.dma("sp", lambda e: e.dma_start(out=dbgout[:, c0:c0 + ncols], in_=DSTG[:, 0:ncols]), "d_dbg")
        sc.op("dve", lambda e: e.tensor_copy(out=SG[:, 0:8], in_=SG[:, 8:16]), waits=[st])

    def mixer_attn(l, ti, TA, TB):
        pc = 128 * l
        QZ = TA.get([128, 8, T], BF16)
        E = TB.get([128, 512], BF16)
        P = TB.get([128, 4, 128], BF16)
        OT = TB.get([128, 512], F32)
        JK = TB.get([128, 512], F32)
        YM = TB.get([128, 512], F32)
        SS = TB.get([128, 16], F32)
        bslot = (ti * 4) % 20
        NB = (1, 2, 6, 7)
        for j in range(4):
            if j % 2 == 0:
                sq_ = win_slab(l, 8 + j // 2)
                sk_ = win_slab(l, 10 + j // 2)
            proj_fm(sq_[0], j % 2, ps[0], sq_[1])
            if j == 0:
                sc.op("dve", lambda e: e.memset(QZ[:], 0.0))
            sc.op("act", lambda e, j=j: e.mul(out=QZ[0:64, 2 * j, :], in_=ps[0][0:64, :], mul=0.125))
            sc.op("act", lambda e, j=j: e.mul(out=QZ[64:128, 2 * j + 1, :], in_=ps[0][64:128, :], mul=0.125))
            proj_fm(sk_[0], j % 2, ps[0], sk_[1])
            sc.op("act", lambda e, j=j: e.copy(out=KTH[:, j, bslot * 128:bslot * 128 + T], in_=ps[0][:]))
        v0 = win_slab(l, 12)
        v1 = win_slab(l, 13)
        for c in range(4):
            def mm(e, c=c):
                ins = None
                for half, (b, _) in enumerate((v0, v1)):
                    for kc in range(KC):
                        ins = e.matmul(ps[0][:, half * 256:(half + 1) * 256], HT[:, kc, c * 128:(c + 1) * 128], WGU[b][:, kc, :],
                                       start=(kc == 0), stop=(kc == KC - 1))
                return ins
            sc.op("pe", mm, waits=[v0[1], v1[1]] if c == 0 else [])
            sc.op("act", lambda e, c=c: e.copy(out=VH[:, bslot + c, :, 0:64], in_=ps[0][:].rearrange("p (h d) -> p h d", d=64)))
        import os
        STAGE = int(os.environ.get('ATT_STAGE', '9'))
        for c in range(4 if STAGE >= 2 else 0):
            g = ti * 4 + c
            nd = min(16, g) + 1
            for quad in range(2):
                for dl in range(nd):
                    ks = (g - dl) % 20

                    def smm(e, c=c, quad=quad, ks=ks):
                        ins = None
                        for hh in range(4):
                            h = quad * 4 + hh
                            j, pb = h // 2, (h % 2) * 64
                            if os.environ.get('ATT_PB0'):
                                pb = 0
                            ins = e.matmul(ps[4][:, hh * 128:(hh + 1) * 128], KTH[:, j, ks * 128:(ks + 1) * 128],
                                           QZ[:, h, c * 128:(c + 1) * 128], start=True, stop=True)
                        return ins
                    sc.op("pe", smm)
                    if os.environ.get('ATT_MMONLY'):
                        continue
                    sc.op("act", lambda e: e.activation(out=E[:], in_=ps[4][:], func=AF.Exp))

                    def msk(e, dl=dl):
                        ins = None
                        for hh in range(4):
                            ins = e.tensor_tensor(out=P[:, hh, :], in0=E[:, hh * 128:(hh + 1) * 128], in1=MASK[:, dl, :], op=ALU.mult)
                        return ins
                    sc.op("dve", msk)

                    def nmm(e, quad=quad, ks=ks, dl=dl, nd=nd):
                        ins = None
                        for hh in range(4):
                            h = quad * 4 + hh
                            ins = e.matmul(ps[NB[hh]][:, 0:65], P[:, hh, :], VH[:, ks, h, 0:65],
                                           start=(dl == 0), stop=(dl == nd - 1))
                        return ins
                    if STAGE >= 3:
                        sc.op("pe", nmm)

                def rcp(e):
                    ins = None
                    for hh in range(4):
                        ins = e.reciprocal(out=SS[:, hh:hh + 1], in_=ps[NB[hh]][:, 64:65])
                    return ins
                if STAGE >= 4:
                    sc.op("dve", rcp)

                def fin(e, quad=quad):
                    ins = None
                    for hh in range(4):
                        h = quad * 4 + hh
                        ins = e.tensor_scalar(out=OT[:, h * 64:(h + 1) * 64], in0=ps[NB[hh]][:, 0:64], scalar1=SS[:, hh:hh + 1],
                                              scalar2=None, op0=ALU.mult)
                    return ins
                if STAGE >= 4:
                    sc.op("dve", fin)
            if STAGE < 5:
                continue
            if ti == 0 and c == 0 and l == 0:
                dump("QZ0", QZ[:, 0, 0:128], 128)
                dump("QZ1", QZ[:, 1, 0:128], 128)
                dump("K0", KTH[:, 0, 0:128], 128)
                dump("V0", VH[:, 0, 0, :], 72)
                dump("V1", VH[:, 0, 1, :], 72)
                dump("E", E[:], 512)
                dump("P", P[:].rearrange("p a b -> p (a b)"), 512)
                dump("OT", OT[:], 512)
                dump("MASK0", MASK[:, 0, :], 128)
                dump("HT", HT[:, :, 0:128], 2048)
            sc.op("act", lambda e: e.activation(out=JK[:], in_=OT[:], func=AF.Square))
            sc.op("dve", lambda e: e.reduce_sum(out=SS[:, 8:9], in_=JK[:], axis=mybir.AxisListType.X))
            sc.op("dve", lambda e: e.tensor_scalar(out=SS[:, 8:9], in0=SS[:, 8:9], scalar1=1.0 / 512, scalar2=EPS, op0=ALU.mult, op1=ALU.add))
            sc.op("act", lambda e: e.sqrt(out=SS[:, 8:9], in_=SS[:, 8:9]))
            sc.op("dve", lambda e: e.reciprocal(out=SS[:, 8:9], in_=SS[:, 8:9]))
            sc.op("dve", lambda e: e.scalar_tensor_tensor(out=YM[:], in0=OT[:], scalar=SS[:, 8:9], in1=WBA[:], op0=ALU.mult, op1=ALU.mult))

            def tr(e):
                ins = None
                for j in range(4):
                    ins = e.transpose(ps[5][:, j * 128:(j + 1) * 128], YM[:, j * 128:(j + 1) * 128], IDF)
                return ins
            sc.op("pe", tr)
            sc.op("act", lambda e, c=c: e.copy(out=YT[:, 4:8, c * 128:(c + 1) * 128], in_=ps[5][:].rearrange("p (j t) -> p j t", t=128)))

    SSMB = sb("SSMB", [128, 528], F32)
    AB = sb("AB", [128, 8], F32)
    WDT = sb("WDT", [128, KC, 8], BF16)
    ST_C = sb("ST_C", [128, 512], F32)
    STB_C = sb("STB_C", [128, 512], BF16)
    HALO_C = sb("HALO_C", [128, 8, 4], F32)
    assert off[0] <= 229344, off[0]
    ONESF = CON[:, 128:256]
    TRI = CON[:, 256:384]
    NEGM = CON[:, 384:512]

    def ssd_layer_init(l):
        ld = sc.dma("sp", lambda e: e.dma_start(out=SSMB[:], in_=ssmb_in[l]), "d_misc")
        ld2 = sc.dma("sp", lambda e: e.dma_start(out=WDT[:].rearrange("p k c -> p (k c)"), in_=wb[(l, "dt")]), "d_misc")
        sc.op("act", lambda e: e.activation(out=AB[:], in_=SSMB[:, 520:528], func=AF.Exp), waits=[ld, ld2])
        sc.op("dve", lambda e: e.tensor_scalar(out=AB[:], in0=AB[:], scalar1=-1.0, scalar2=None, op0=ALU.mult))
        sc.op("dve", lambda e: e.memset(ST_C[:], 0.0))
        sc.op("dve", lambda e: e.memset(STB_C[:], 0.0))
        sc.op("dve", lambda e: e.memset(HALO_C[:], 0.0))

    def mixer_ssd(l, ti, TA, TB):
        pc = 128 * l
        XB = TB.get([128, T + 4], F32)
        XC = TB.get([128, T], F32)
        XF = TA.get([128, 6, T], F32)
        BCT = TA.get([128, 4, T], BF16)
        YC = TA.get([128, 4, T], F32)
        XTM = TB.get([128, 512], F32)
        XDT = TB.get([128, 512], BF16)
        XDS = TA.get([128, 512], BF16)
        BTM = TB.get([128, 2, 128], BF16)
        SM = TB.get([128, 64], F32)
        LM = TB.get([128, 4, 128], F32)
        WW = TB.get([128, 8, 128], BF16)
        YTM = TB.get([128, 512], F32)
        TMP = TA.get([128, 512], F32)
        ADTB = TMP[:].rearrange("p (a b) -> p a b", b=128)
        for j in range(8):
            if j % 2 == 0:
                sl = win_slab(l, 16 + j // 2)
            proj_fm(sl[0], j % 2, ps[0], sl[1])
            sc.op("act", lambda e: e.copy(out=XB[:, 3:T + 3], in_=ps[0][:]))
            sc.op("dve", lambda e, j=j: e.tensor_copy(out=XB[:, 0:3], in_=HALO_C[:, j, 0:3]))
            sc.op("dve", lambda e, j=j: e.tensor_copy(out=HALO_C[:, j, 0:3], in_=XB[:, T:T + 3]))

            def conv(e, j=j):
                e.tensor_scalar(out=XC[:], in0=XB[:, 0:T], scalar1=PVM[:, pc + 48 + j * 4:pc + 49 + j * 4],
                                scalar2=PVM[:, pc + 80 + j:pc + 81 + j], op0=ALU.mult, op1=ALU.add)
                ins = None
                for tp in range(1, 4):
                    ins = e.scalar_tensor_tensor(out=XC[:], in0=XB[:, tp:tp + T], scalar=PVM[:, pc + 48 + j * 4 + tp:pc + 49 + j * 4 + tp],
                                                 in1=XC[:], op0=ALU.mult, op1=ALU.add)
                return ins
            sc.op("dve", conv)
            if j < 6:
                sc.op("act", lambda e, j=j: e.activation(out=XF[:, j, :], in_=XC[:], func=AF.Silu))
                if j >= 4:
                    sc.op("dve", lambda e, j=j: e.tensor_copy(out=BCT[:, j - 4, :], in_=XF[:, j, :]))
            else:
                sc.op("act", lambda e, j=j: e.activation(out=BCT[:, j - 4, :], in_=XC[:], func=AF.Silu))
        for c in range(4):
            cs = slice(c * 128, (c + 1) * 128)
            def trx(e, cs=cs):
                ins = None
                for j in range(4):
                    ins = e.transpose(ps[7][:, j * 128:(j + 1) * 128], XF[:, j, cs], IDF)
                return ins
            sc.op("pe", trx)
            sc.op("act", lambda e: e.copy(out=XTM[:], in_=ps[7][:]))

            def trb(e, cs=cs):
                ins = None
                for g in range(2):
                    ins = e.transpose(ps[7][:, g * 128:(g + 1) * 128], XF[:, 4 + g, cs], IDF)
                return ins
            sc.op("pe", trb)
            sc.op("act", lambda e: e.copy(out=BTM[:].rearrange("p g n -> p (g n)"), in_=ps[7][:, 0:256]))
            def dtmm(e, cs=cs):
                ins = None
                for kc in range(KC):
                    ins = e.matmul(ps[1][:, 0:8], HT[:, kc, cs], WDT[:, kc, :], start=(kc == 0), stop=(kc == KC - 1))
                return ins
            sc.op("pe", dtmm)
            sc.op("dve", lambda e: e.tensor_tensor(out=SM[:, 0:8], in0=ps[1][:, 0:8], in1=SSMB[:, 512:520], op=ALU.add))
            sc.op("act", lambda e: e.activation(out=SM[:, 0:8], in_=SM[:, 0:8], func=AF.Exp))
            sc.op("act", lambda e: e.activation(out=SM[:, 0:8], in_=SM[:, 0:8], func=AF.Ln, bias=1.0))
            sc.op("dve", lambda e: e.tensor_tensor(out=SM[:, 8:16], in0=SM[:, 0:8], in1=AB[:], op=ALU.mult))
            sc.op("pe", lambda e: e.matmul(ps[1][:, 16:24], TRI, SM[:, 8:16], start=True, stop=True))
            sc.op("dve", lambda e: e.tensor_copy(out=SM[:, 16:24], in_=ps[1][:, 16:24]))
            sc.op("act", lambda e: e.activation(out=SM[:, 24:32], in_=SM[:, 16:24], func=AF.Exp))
            sc.op("dve", lambda e: e.tensor_scalar(out=SM[:, 32:40], in0=SM[:, 16:24], scalar1=-1.0, scalar2=None, op0=ALU.mult))
            def xdt(e):
                ins = None
                for h in range(8):
                    ins = e.tensor_scalar(out=XDT[:, h * 64:(h + 1) * 64], in0=XTM[:, h * 64:(h + 1) * 64], scalar1=SM[:, h:h + 1],
                                          scalar2=None, op0=ALU.mult)
                return ins
            sc.op("dve", xdt)
            for g in range(2):
                def adtb(e, g=g):
                    ins = None
                    for hh in range(4):
                        h = g * 4 + hh
                        ins = e.tensor_scalar(out=ADTB[:, hh, :], in0=ONESF, scalar1=SM[:, 8 + h:9 + h], scalar2=None, op0=ALU.mult)
                    return ins
                sc.op("dve", adtb)

                def bc(e):
                    ins = None
                    for hh in range(4):
                        e.matmul(ps[2][:, hh * 128:(hh + 1) * 128], ADTB[:, hh, :], TRI, start=True, stop=False)
                        ins = e.matmul(ps[2][:, hh * 128:(hh + 1) * 128], IDF, NEGM, start=False, stop=True)
                    return ins
                sc.op("pe", bc)

                def lm(e, g=g):
                    ins = None
                    for hh in range(4):
                        h = g * 4 + hh
                        e.activation(out=LM[:, hh, :], in_=ps[2][:, hh * 128:(hh + 1) * 128], func=AF.Exp, bias=SM[:, 32 + h:33 + h])
                        ins = e.activation(out=SM[:, 40 + h:41 + h], in_=ps[2][:, hh * 128 + 127:hh * 128 + 128], func=AF.Exp)
                    return ins
                sc.op("act", lm)
                sc.op("pe", lambda e, g=g, cs=cs: e.matmul(ps[3][:, 0:128], BCT[:, g, cs], BCT[:, 2 + g, cs], start=True, stop=True))

                def wmul(e, g=g):
                    ins = None
                    for hh in range(4):
                        ins = e.tensor_tensor(out=WW[:, g * 4 + hh, :], in0=ps[3][:, 0:128], in1=LM[:, hh, :], op=ALU.mult)
                    return ins
                sc.op("dve", wmul)

                def xds(e, g=g):
                    ins = None
                    for hh in range(4):
                        h = g * 4 + hh
                        ins = e.tensor_scalar(out=XDS[:, h * 64:(h + 1) * 64], in0=XDT[:, h * 64:(h + 1) * 64], scalar1=LM[:, hh, 127:128],
                                              scalar2=None, op0=ALU.mult)
                    return ins
                sc.op("dve", xds)

            def ydiag(e):
                ins = None
                for h in range(8):
                    ins = e.matmul(ps[4][:, h * 64:(h + 1) * 64], WW[:, h, :], XDT[:, h * 64:(h + 1) * 64], start=True, stop=True)
                return ins
            sc.op("pe", ydiag)

            def yoff(e, cs=cs):
                ins = None
                for h in range(8):
                    ins = e.matmul(ps[5][:, h * 64:(h + 1) * 64], BCT[:, 2 + h // 4, cs], STB_C[:, h * 64:(h + 1) * 64], start=True, stop=True)
                return ins
            sc.op("pe", yoff)
            sc.op("act", lambda e: e.copy(out=YTM[:], in_=ps[4][:]))

            def yadd(e):
                ins = None
                for h in range(8):
                    ins = e.scalar_tensor_tensor(out=YTM[:, h * 64:(h + 1) * 64], in0=ps[5][:, h * 64:(h + 1) * 64], scalar=SM[:, 24 + h:25 + h],
                                                 in1=YTM[:, h * 64:(h + 1) * 64], op0=ALU.mult, op1=ALU.add)
                return ins
            sc.op("dve", yadd)
            sc.op("dve", lambda e: e.tensor_tensor(out=TMP[:], in0=XTM[:], in1=SSMB[:, 0:512], op=ALU.mult))
            sc.op("dve", lambda e: e.tensor_tensor(out=YTM[:], in0=YTM[:], in1=TMP[:], op=ALU.add))
            def smm(e):
                ins = None
                for g in range(2):
                    ins = e.matmul(ps[6][:, g * 256:(g + 1) * 256], BTM[:, g, :], XDS[:, g * 256:(g + 1) * 256], start=True, stop=True)
                return ins
            sc.op("pe", smm)

            def stu(e):
                ins = None
                for h in range(8):
                    ins = e.scalar_tensor_tensor(out=ST_C[:, h * 64:(h + 1) * 64], in0=ST_C[:, h * 64:(h + 1) * 64], scalar=SM[:, 40 + h:41 + h],
                                                 in1=ps[6][:, h * 64:(h + 1) * 64], op0=ALU.mult, op1=ALU.add)
                return ins
            sc.op("dve", stu)
            sc.op("act", lambda e: e.copy(out=STB_C[:], in_=ST_C[:]))
            def tr(e):
                ins = None
                for j in range(4):
                    ins = e.transpose(ps[7][:, j * 128:(j + 1) * 128], YTM[:, j * 128:(j + 1) * 128], IDF)
                return ins
            sc.op("pe", tr)
            sc.op("act", lambda e, cs=cs: e.copy(out=YC[:, :, cs], in_=ps[7][:].rearrange("p (j t) -> p j t", t=128)))
        for j in range(4):
            if j % 2 == 0:
                sl = win_slab(l, 14 + j // 2)
            proj_fm(sl[0], j % 2, ps[0], sl[1])
            sc.op("act", lambda e: e.activation(out=TMP[:], in_=ps[0][:], func=AF.Silu))
            sc.op("dve", lambda e, j=j: e.tensor_tensor(out=YC[:, j, :], in0=YC[:, j, :], in1=TMP[:], op=ALU.mult))
        TB2 = Tmp(WDBASE, TBSIZE)
        group_norm_to_yt(YC, [0, 1], 8, pc + 88, TB2)
        group_norm_to_yt(YC, [2, 3], 10, pc + 90, Tmp(WDBASE, TBSIZE))

    S_A = sb("S_A", [128, 4, 128], F32)
    LBL = sb("LBL", [128, DEPTH, 4], F32)
    LB = sb("LB", [128, DEPTH, 4], F32)
    OML = sb("OML", [128, DEPTH, 4], F32)
    LBT = sb("LBT", [128, 16], F32)
    assert off[0] <= 229344, off[0]

    def hgrn_init():
        ld = sc.dma("sp", lambda e: e.dma_start(out=LBL[:].rearrange("p a b -> p (a b)"), in_=lbl_in[:, :]), "d_misc")
        sc.op("dve", lambda e: e.tensor_copy(out=LBT[:, 0:4], in_=LBL[:, 0, :]), waits=[ld])
        for l in range(1, DEPTH):
            sc.op("dve", lambda e, l=l: e.tensor_tensor(out=LBT[:, 0:4], in0=LBT[:, 0:4], in1=LBL[:, l, :], op=ALU.max))
        for l in range(DEPTH):
            sc.op("dve", lambda e, l=l: e.tensor_tensor(out=LBL[:, l, :], in0=LBL[:, l, :], in1=LBT[:, 0:4], op=ALU.subtract))
        sc.op("act", lambda e: e.activation(out=LBL[:], in_=LBL[:], func=AF.Exp))
        sc.op("dve", lambda e: e.tensor_copy(out=LBT[:, 4:8], in_=LBL[:, 0, :]))
        for l in range(1, DEPTH):
            sc.op("dve", lambda e, l=l: e.tensor_tensor(out=LBT[:, 4:8], in0=LBT[:, 4:8], in1=LBL[:, l, :], op=ALU.add))
        sc.op("dve", lambda e: e.reciprocal(out=LBT[:, 8:12], in_=LBT[:, 4:8]))
        for l in range(DEPTH):
            sc.op("dve", lambda e, l=l: e.tensor_tensor(out=LBL[:, l, :], in0=LBL[:, l, :], in1=LBT[:, 8:12], op=ALU.mult))
        sc.op("dve", lambda e: e.memset(LB[:, 0, :], 0.0))
        for l in range(1, DEPTH):
            sc.op("dve", lambda e, l=l: e.tensor_tensor(out=LB[:, l, :], in0=LB[:, l - 1, :], in1=LBL[:, l, :], op=ALU.add))
        sc.op("dve", lambda e: e.tensor_scalar(out=OML[:], in0=LB[:], scalar1=-1.0, scalar2=1.0, op0=ALU.mult, op1=ALU.add))

    def mixer_hgrn(l, ti, TA, TB):
        pc = 128 * l
        O = TA.get([128, 4, T], F32)
        KTb = TA.get([128, 4, T], BF16)
        KTM = TA.get([128, 4, 8, 128], BF16)
        VTM = TA.get([128, 8, 512], BF16)
        QTb = YT
        t1 = TB.get([128, T], F32)
        t2 = TB.get([128, T], F32)
        t3 = TB.get([128, T], F32)
        t4 = TB.get([128, T], F32)
        t5 = TB.get([128, T], F32)
        EB = TB.get([128, 3, 4, 8], F32)
        TMPU = TB.get([128, 4, 128], F32)
        PM = TB.get([128, 4, 64], BF16)
        SMID = TB.get([128, 4, 128], BF16)
        for h in range(4):
            sl = win_slab(l, h // 2) if h % 2 == 0 else sl_q
            sl_q = sl
            proj_fm(sl[0], h % 2, ps[0], sl[1])
            sc.op("act", lambda e: e.activation(out=t1[:], in_=ps[0][:], func=AF.Silu))
            slf = win_slab(l, 2 + h // 2) if h % 2 == 0 else sl_f
            sl_f = slf
            proj_fm(slf[0], h % 2, ps[0], slf[1])
            sc.op("act", lambda e: e.activation(out=t2[:], in_=ps[0][:], func=AF.Sigmoid))
            sc.op("dve", lambda e, h=h: e.tensor_scalar(out=t2[:], in0=t2[:], scalar1=OML[:, l, h:h + 1], scalar2=LB[:, l, h:h + 1],
                                                        op0=ALU.mult, op1=ALU.add))
            sc.op("act", lambda e: e.activation(out=t3[:], in_=t2[:], func=AF.Ln))
            sc.op("dve", lambda e: e.tensor_scalar(out=t2[:], in0=t2[:], scalar1=-1.0, scalar2=1.0, op0=ALU.mult, op1=ALU.add))

            def scan(e):
                ins = None
                for c in range(8):
                    ins = e.tensor_tensor_scan(out=t4[:, c * 64:(c + 1) * 64], data0=ONESF[:, 0:64], data1=t3[:, c * 64:(c + 1) * 64],
                                               initial=0.0, op0=ALU.mult, op1=ALU.add)
                return ins
            sc.op("dve", scan)

            def dsub(e):
                ins = None
                for c in range(8):
                    ins = e.tensor_scalar(out=t3[:, c * 64:(c + 1) * 64], in0=t4[:, c * 64:(c + 1) * 64],
                                          scalar1=t4[:, c * 64 + 31:c * 64 + 32], scalar2=None, op0=ALU.subtract)
                return ins
            sc.op("dve", dsub)
            b3 = t4[:].rearrange("p (c t) -> p c t", t=64)
            sc.op("act", lambda e, h=h, b3=b3: e.activation(out=EB[:, 0, h, :], in_=b3[:, :, 31], func=AF.Exp))
            sc.op("act", lambda e, h=h, b3=b3: e.activation(out=EB[:, 1, h, :], in_=b3[:, :, 63], func=AF.Exp))
            sc.op("dve", lambda e, h=h, b3=b3: e.tensor_tensor(out=EB[:, 2, h, :], in0=b3[:, :, 63], in1=b3[:, :, 31], op=ALU.subtract))
            sc.op("act", lambda e, h=h: e.activation(out=EB[:, 2, h, :], in_=EB[:, 2, h, :], func=AF.Exp))
            sc.op("act", lambda e: e.activation(out=t5[:], in_=t3[:], func=AF.Exp))
            sc.op("dve", lambda e, h=h: e.tensor_tensor(out=QTb[:, h, :], in0=t1[:], in1=t5[:], op=ALU.mult))
            sc.op("act", lambda e: e.activation(out=t5[:], in_=t3[:], func=AF.Exp, scale=-1.0))
            sc.op("dve", lambda e: e.tensor_tensor(out=t1[:], in0=t2[:], in1=t5[:], op=ALU.mult))
            sc.op("act", lambda e, h=h: e.copy(out=KTb[:, h, :], in_=t1[:]))
            for rnd in range(2):
                def trk(e, rnd=rnd):
                    ins = None
                    for cc in range(4):
                        c = rnd * 4 + cc
                        ins = e.transpose(ps[7][0:64, cc * 128:(cc + 1) * 128], t1[:, c * 64:(c + 1) * 64], IDF)
                    return ins
                sc.op("pe", trk)
                sc.op("act", lambda e, h=h, rnd=rnd: e.copy(out=KTM[0:64, h, rnd * 4:rnd * 4 + 4, :].rearrange("p a b -> p (a b)"),
                                                             in_=ps[7][0:64, :]))
        v0 = win_slab(l, 4)
        v1 = win_slab(l, 5)
        for c in range(8):
            def mm(e, c=c):
                ins = None
                for half, (b, _) in enumerate((v0, v1)):
                    for kc in range(KC):
                        ins = e.matmul(ps[0][0:64, half * 256:(half + 1) * 256], HT[:, kc, c * 64:(c + 1) * 64], WGU[b][:, kc, :],
                                       start=(kc == 0), stop=(kc == KC - 1))
                return ins
            sc.op("pe", mm, waits=[v0[1], v1[1]] if c == 0 else [])
            sc.op("act", lambda e, c=c: e.copy(out=VTM[0:64, c, :], in_=ps[0][0:64, :]))
        for c in range(8):
            cs = slice(c * 64, (c + 1) * 64)

            def smid(e, c=c):
                ins = None
                for h in range(4):
                    ins = e.tensor_scalar(out=SMID[:, h, :], in0=S_A[:, h, :], scalar1=EB[:, 0, h, c:c + 1], scalar2=None, op0=ALU.mult)
                return ins
            sc.op("dve", smid)

            def pmm(e, cs=cs):
                ins = None
                for h in range(4):
                    ins = e.matmul(ps[3][0:64, h * 64:(h + 1) * 64], KTb[:, h, cs], QTb[:, h, cs], start=True, stop=True)
                return ins
            sc.op("pe", pmm)

            def pmask(e):
                ins = None
                for h in range(4):
                    ins = e.tensor_tensor(out=PM[0:64, h, :], in0=ps[3][0:64, h * 64:(h + 1) * 64], in1=TRI[0:64, 0:64], op=ALU.mult)
                return ins
            sc.op("dve", pmask)

            def omm(e, c=c, cs=cs):
                ins = None
                for h in range(4):
                    e.matmul(ps[4][:, h * 64:(h + 1) * 64], SMID[:, h, :], QTb[:, h, cs], start=True, stop=False)
                    ins = e.matmul(ps[4][:, h * 64:(h + 1) * 64], VTM[0:64, c, h * 128:(h + 1) * 128], PM[0:64, h, :], start=False, stop=True)
                return ins
            sc.op("pe", omm)

            def umm(e, c=c):
                ins = None
                for h in range(4):
                    ins = e.matmul(ps[5][:, h * 128:(h + 1) * 128], KTM[0:64, h, c, :], VTM[0:64, c, h * 128:(h + 1) * 128], start=True, stop=True)
                return ins
            sc.op("pe", umm)

            def utmp(e, c=c):
                ins = None
                for h in range(4):
                    ins = e.tensor_scalar(out=TMPU[:, h, :], in0=ps[5][:, h * 128:(h + 1) * 128], scalar1=EB[:, 2, h, c:c + 1], scalar2=None, op0=ALU.mult)
                return ins
            sc.op("dve", utmp)

            def supd(e, c=c):
                ins = None
                for h in range(4):
                    ins = e.scalar_tensor_tensor(out=S_A[:, h, :], in0=S_A[:, h, :], scalar=EB[:, 1, h, c:c + 1], in1=TMPU[:, h, :],
                                                 op0=ALU.mult, op1=ALU.add)
                return ins
            sc.op("dve", supd)
            sc.op("act", lambda e, cs=cs: e.copy(out=O[:, :, cs], in_=ps[4][:, 0:256].rearrange("p (h t) -> p h t", t=64)))
        SQh = t1[:].bitcast(BF16) if False else None
        for h in range(4):
            slg = win_slab(l, 6 + h // 2) if h % 2 == 0 else sl_g
            sl_g = slg
            sc.op("act", lambda e, h=h: e.activation(out=KTb[:, 0, :], in_=O[:, h, :], func=AF.Square))
            sc.op("pe", lambda e: e.matmul(ps[2][:], ONESB_t[:], KTb[:, 0, :], start=True, stop=True))
            sc.op("dve", lambda e: e.tensor_scalar(out=t2[:], in0=ps[2][:], scalar1=1.0 / 128, scalar2=EPS, op0=ALU.mult, op1=ALU.add))
            sc.op("act", lambda e: e.sqrt(out=t2[:], in_=t2[:]))
            sc.op("dve", lambda e: e.reciprocal(out=t2[:], in_=t2[:]))
            sc.op("dve", lambda e, h=h: e.scalar_tensor_tensor(out=t3[:], in0=O[:, h, :], scalar=PVM[:, pc + 104 + h:pc + 105 + h], in1=t2[:],
                                                               op0=ALU.mult, op1=ALU.mult))
            proj_fm(slg[0], h % 2, ps[0], slg[1])
            sc.op("act", lambda e: e.activation(out=t4[:], in_=ps[0][:], func=AF.Silu))
            sc.op("dve", lambda e, h=h: e.tensor_tensor(out=YT[:, h, :], in0=t3[:], in1=t4[:], op=ALU.mult))

    def layer_init(l):
        pc = 128 * l
        ssd_layer_init(l)
        sc.op("dve", lambda e: e.memset(S_A[:], 0.0))
        ld = sc.dma("sp", lambda e: e.dma_start(out=XT[:, 0:2, :].rearrange("p a (b c) -> p (a b) c", c=128),
                                                 in_=lbd_in[l]), "d_misc")
        sc.op("dve", lambda e: e.tensor_copy(out=LBD[:].rearrange("p a b c -> p (a b) c"),
                                             in_=XT[:, 0:2, :].rearrange("p a (b c) -> p (a b) c", c=128)), waits=[ld])
        ldw = sc.dma("sp", lambda e: e.dma_start(out=WBA[:], in_=anw_in[l]), "d_misc")
        sc.op("dve", lambda e: e.memset(HALO_D[:], 0.0), waits=[ldw])
        sc.op("dve", lambda e: e.memset(HPREV[:], 0.0))
        sc.op("act", lambda e: e.activation(out=PVM[:, pc + 40:pc + 44], in_=PVM[:, pc + 28:pc + 32], func=AF.Exp, scale=-1.0))
        sc.op("act", lambda e: e.activation(out=PVM[:, pc + 40:pc + 44], in_=PVM[:, pc + 40:pc + 44], func=AF.Ln, bias=1.0))
        sc.op("dve", lambda e: e.tensor_scalar(out=PVM[:, pc + 40:pc + 44], in0=PVM[:, pc + 40:pc + 44], scalar1=-8.0, scalar2=None, op0=ALU.mult))

    def out_proj(l):
        wo = wb[(l, "out")]
        for oc in range(16):
            b = win_i[0] % 4
            win_i[0] += 1
            tok = sc.dma("sp", lambda e, b=b, oc=oc: e.dma_start(out=WGU[b][:, :, 0:128], in_=wo[oc].rearrange("p (k c) -> p k c", c=128)), f"d_wgu{b}")

            def mm(e, b=b):
                ins = None
                for kc in range(KC):
                    ins = e.matmul(ps[3][:], WGU[b][:, kc, 0:128], YT[:, kc, :], start=(kc == 0), stop=(kc == KC - 1))
                return ins
            sc.op("pe", mm, waits=[tok])
            sc.op("dve", lambda e, oc=oc: e.tensor_tensor(out=XT[:, oc, :], in0=ps[3][:], in1=XT[:, oc, :], op=ALU.add))

    def mixer(l, ti):
        rmsnorm_to(lambda kc: HT[:, kc, :], 64 * l + 16)
        def mkTA():
            return Tmp(arena0 + 16384, 45056 - 16384)

        def mkTB():
            return Tmp(WDBASE, TBSIZE)
        sc.op("dve", lambda e: e.memset(YT[:, 0:12, :], 0.0))
        if 'A' in MIX:
            mixer_hgrn(l, ti, mkTA(), mkTB())
        if 'B' in MIX:
            mixer_attn(l, ti, mkTA(), mkTB())
        if 'C' in MIX:
            mixer_ssd(l, ti, mkTA(), mkTB())
        mixer_lru(l, ti, mkTA(), mkTB())
        out_proj(l)

    ct = load_consts()
    sc.op("dve", lambda e: e.tensor_copy(out=ONESB_t[:], in_=CON[:, 128:256]), waits=[ct])
    sc.op("dve", lambda e: e.tensor_copy(out=IDB_t[:], in_=CON[:, 0:128]))
    attn_init()
    hgrn_init()
    convert_all()
    for l in range(DEPTH):
        layer_init(l)
        for ti in range(NT):
            load_x_tile(l, ti)
            ffn(l, "ffn1", 64 * l + 0)
            if dbg != "ffn1only":
                mixer(l, ti)
                ffn(l, "ffn2", 64 * l + 32)
            if l == DEPTH - 1:
                final_store(ti)
            else:
                store_x_tile(ti)

    semnames = sorted({tok[0] for (_, _, _, tok, _) in sc.ops})
    sems = {n: nc.alloc_semaphore(n) for n in semnames}
    with nc.Block() as block:
        def emit(engname):
            def body(e):
                for (eng, fn, waits, tok, inc) in sc.ops:
                    if eng != engname:
                        continue
                    for (sn, val) in waits:
                        e.wait_ge(sems[sn], val)
                    ins = fn(e)
                    ins.then_inc(sems[tok[0]], inc)
            return body
        block.tensor(emit("pe"))
        block.scalar(emit("act"))
        block.vector(emit("dve"))
        block.gpsimd(emit("pool"))
        block.sync(emit("sp"))
    return nc


def make_consts():
    c = np.zeros((128, 512), np.float32)
    c[:, 0:128] = np.eye(128, dtype=np.float32)
    c[:, 128:256] = 1.0
    i = np.arange(128)
    c[:, 256:384] = (i[:, None] <= i[None, :]).astype(np.float32)
    c[:, 384:512] = np.where(i[:, None] <= i[None, :], 0.0, -30000.0)
    return c


def make_pvec(inp, DEPTH):
    pv = np.zeros((128, 64 * DEPTH + 16), np.float32)
    for l in range(DEPTH):
        pv[:, 64 * l + 0:64 * l + 16] = inp["ffn1_norm"][l].reshape(16, 128).T
        pv[:, 64 * l + 16:64 * l + 32] = inp["mix_norm"][l].reshape(16, 128).T
        pv[:, 64 * l + 32:64 * l + 48] = inp["ffn2_norm"][l].reshape(16, 128).T
    pv[:, 64 * DEPTH:64 * DEPTH + 16] = inp["final_norm"].reshape(16, 128).T
    return pv


def make_amask():
    m = np.zeros((128, 17, 128), np.float32)
    i = np.arange(128)[:, None]
    j = np.arange(128)[None, :]
    for dl in range(17):
        dist = 128 * dl + j - i
        for (win, dil) in ((128, 1), (512, 4), (2048, 16)):
            m[:, dl, :] += ((dist >= 0) & (dist % dil == 0) & (dist // dil <= 128)).astype(np.float32)
    return m.reshape(128, 17 * 128)


def make_pvm(inp, DEPTH):
    pm = np.zeros((128, 128 * DEPTH), np.float32)
    lbd = np.zeros((DEPTH, 128, 8, 128), np.float32)
    for l in range(DEPTH):
        pc = 128 * l
        cw = inp["lru_conv_w"][l]
        for j in range(4):
            pm[:, pc + j * 4:pc + j * 4 + 4] = cw[:, j * 128:(j + 1) * 128].T
        pm[:, pc + 16:pc + 20] = inp["lru_conv_b"][l].reshape(4, 128).T
        pm[:, pc + 20:pc + 24] = inp["lru_b_a"][l].reshape(4, 128).T
        pm[:, pc + 24:pc + 28] = inp["lru_b_x"][l].reshape(4, 128).T
        pm[:, pc + 28:pc + 32] = inp["lru_a_param"][l].reshape(4, 128).T
        pm[:, pc + 32:pc + 36] = inp["lru_norm"][l].reshape(4, 128).T
        pm[:, pc + 104:pc + 108] = inp["hgrn_norm"][l].reshape(4, 128).T
        scw = inp["ssm_conv_w"][l]
        for j in range(8):
            pm[:, pc + 48 + j * 4:pc + 52 + j * 4] = scw[:, j * 128:(j + 1) * 128].T
        pm[:, pc + 80:pc + 88] = inp["ssm_conv_b"][l].reshape(8, 128).T
        pm[:, pc + 88:pc + 92] = inp["ssm_norm"][l].reshape(4, 128).T
        for wi, nm in enumerate(("lru_w_a", "lru_w_x")):
            w = inp[nm][l]
            for j in range(4):
                for hb in range(2):
                    lbd[l, hb * 64:(hb + 1) * 64, wi * 4 + j, hb * 64:(hb + 1) * 64] = w[2 * j + hb]
    return pm, lbd


def run(inp, S, DEPTH, B, dbg=None):
    nc = build(S, DEPTH, dbg)
    consts = make_consts()
    pv = make_pvec(inp, DEPTH)
    pm, lbd = make_pvm(inp, DEPTH)
    amask = make_amask()
    lbl = np.ascontiguousarray(inp["hgrn_lb_logits"][:DEPTH].reshape(DEPTH, 4, 128).transpose(2, 0, 1).reshape(128, DEPTH * 4)).astype(np.float32)
    ssmb = np.zeros((DEPTH, 128, 528), np.float32)
    for l in range(DEPTH):
        ssmb[l, :, 0:512] = np.repeat(inp["ssm_d"][l], 64)[None, :]
        ssmb[l, :, 512:520] = inp["ssm_dt_bias"][l][None, :]
        ssmb[l, :, 520:528] = inp["ssm_a_log"][l][None, :]
    anw = np.ascontiguousarray(np.broadcast_to(inp["attn_norm"][:DEPTH, None, :], (DEPTH, 128, 512))).astype(np.float32)
    in_maps = []
    for b in range(B):
        m = {"x": np.ascontiguousarray(inp["x"][b]), "consts": consts, "pvec": pv, "pvm": pm, "lbd": lbd, "amask": amask, "anw": anw, "ssmb": ssmb, "lbl": lbl}
        for nm in ("ffn1_w_gate", "ffn1_w_up", "ffn1_w_down", "ffn2_w_gate", "ffn2_w_up", "ffn2_w_down", "w_in", "w_out"):
            m[nm] = np.asarray(inp[nm])
        in_maps.append(m)
    res = run_bass_kernel_spmd(nc, in_maps, core_ids=list(range(B)))
    global LAST
    LAST = res.results
    return np.stack([res.results[b]["out"] for b in range(B)], 0)


def kernel(**inputs):
    inp = {k: np.asarray(v) for k, v in inputs.items()}
    return run(inp, 16384, 4, 2).astype(np.float32)
```

```python
import numpy as np
import concourse.bass as bass
import concourse.mybir as mybir
from concourse.bass_utils import run_bass_kernel_spmd

F32, BF16 = mybir.dt.float32, mybir.dt.bfloat16
AF = mybir.ActivationFunctionType
ALU = mybir.AluOpType

D = 2048
DFF = 5632
NIN = 6152
T = 512
KC = 16
FCH = 44
EPS = 1e-6
ENGS = ("pe", "act", "dve", "pool", "sp")


class Sched:
    def __init__(self):
        self.ops = []
        self.cnt = {e: 0 for e in ENGS}
        self.dcnt = {}
        self.last = None

    def op(self, eng, fn, waits=(), chain=True):
        w = [x for x in waits if x is not None]
        if chain and self.last is not None:
            w.append(self.last)
        self.cnt[eng] += 1
        tok = ("c_" + eng, self.cnt[eng])
        self.ops.append((eng, fn, w, tok, 1))
        if chain:
            self.last = tok
        return tok

    def dma(self, q, fn, dsem, waits=(), chain=True):
        w = [x for x in waits if x is not None]
        if chain and self.last is not None:
            w.append(self.last)
        self.dcnt[dsem] = self.dcnt.get(dsem, 0) + 16
        tok = (dsem, self.dcnt[dsem])
        self.ops.append((q, fn, w, tok, 16))
        return tok


MIX = 'ABCD'
DBG = False
DBGMAP = {}
LAST = None


def build(S, DEPTH, dbg=None):
    NT = S // T
    nc = bass.Bass("TRN2", target_bir_lowering=False)

    def din(name, shape, dt=F32):
        return nc.dram_tensor(name, list(shape), dt, kind="ExternalInput").ap()

    x_in = din("x", [S, D])
    out = nc.dram_tensor("out", [S, D], F32, kind="ExternalOutput").ap()
    dbgout = nc.dram_tensor("dbg", [128, 8192], F32, kind="ExternalOutput").ap() if DBG else None
    dbgpos = [0]
    dbgmap = {}
    wsrc = {}
    for nm, shp in (("ffn1_w_gate", [DEPTH, D, DFF]), ("ffn1_w_up", [DEPTH, D, DFF]),
                    ("ffn1_w_down", [DEPTH, DFF, D]), ("ffn2_w_gate", [DEPTH, D, DFF]),
                    ("ffn2_w_up", [DEPTH, D, DFF]), ("ffn2_w_down", [DEPTH, DFF, D]),
                    ("w_in", [DEPTH, D, NIN]), ("w_out", [DEPTH, D, D])):
        wsrc[nm] = din(nm, shp)
    consts = din("consts", [128, 512])
    pvec = din("pvec", [128, 64 * DEPTH + 16])
    pvm_in = din("pvm", [128, 128 * DEPTH])
    lbd_in = din("lbd", [DEPTH, 128, 8, 128])
    amask_in = din("amask", [128, 17 * 128])
    anw_in = din("anw", [DEPTH, 128, 512])
    ssmb_in = din("ssmb", [DEPTH, 128, 528])
    lbl_in = din("lbl", [128, DEPTH * 4])

    xs = nc.dram_tensor("xs", [KC, 128, S], F32).ap()
    wb = {}
    for l in range(DEPTH):
        for f in ("ffn1", "ffn2"):
            wb[(l, f, "g")] = nc.dram_tensor(f"wb_{l}_{f}_g", [22, 128, KC * 256], BF16).ap()
            wb[(l, f, "u")] = nc.dram_tensor(f"wb_{l}_{f}_u", [22, 128, KC * 256], BF16).ap()
            wb[(l, f, "d")] = nc.dram_tensor(f"wb_{l}_{f}_d", [16, 128, FCH * 128], BF16).ap()
        wb[(l, "in")] = nc.dram_tensor(f"wb_{l}_in", [24, 128, KC * 256], BF16).ap()
        wb[(l, "dt")] = nc.dram_tensor(f"wb_{l}_dt", [128, KC * 8], BF16).ap()
        wb[(l, "out")] = nc.dram_tensor(f"wb_{l}_out", [16, 128, KC * 128], BF16).ap()

    off = [16640]

    def sb(name, shape, dt, at=None):
        nbytes = int(np.prod(shape[1:])) * (4 if dt == F32 else 2)
        if at is None:
            o = off[0]
            off[0] += (nbytes + 63) // 64 * 64
        else:
            o = at
        return nc.alloc_sbuf_tensor_at(name, list(shape), dt, offset=o)

    XT = sb("XT", [128, KC, T], F32)
    HT = sb("HT", [128, KC, T], BF16)
    arena0 = off[0]
    AT = sb("AT", [128, FCH, T], BF16)
    WGU_OFF = off[0]
    WGU = [sb(f"WGU{i}", [128, KC, 256], BF16) for i in range(4)]
    WD = [nc.alloc_sbuf_tensor_at(f"WD{i}", [128, FCH, 128], BF16, offset=WGU_OFF + i * 16384) for i in range(2)]
    WDBASE = off[0]
    TBSIZE = 14336
    off[0] += TBSIZE
    CON = sb("CON", [128, 512], F32)
    PV = sb("PV", [128, 64 * DEPTH + 16], F32)
    RSTD = sb("RSTD", [128, T], F32)
    SG = sb("SG", [128, T], F32)
    assert off[0] <= 229344, off[0]
    STG = nc.alloc_sbuf_tensor_at("STG", [128, 4, D], F32, offset=arena0)
    SQ = nc.alloc_sbuf_tensor_at("SQ", [128, KC, T], BF16, offset=arena0)
    NCV = 3
    CVI = [nc.alloc_sbuf_tensor_at(f"CVI{i}", [128, 2048], F32, offset=arena0 + i * 8192) for i in range(NCV)]
    CVO = [nc.alloc_sbuf_tensor_at(f"CVO{i}", [128, 2048], BF16, offset=arena0 + NCV * 8192 + i * 4096) for i in range(NCV)]

    IDF = CON[:, 0:128]
    ONESB = None

    ps = [nc.alloc_psum_tensor(f"ps{i}", [128, 512], F32) for i in range(8)]

    sc = Sched()

    def load_consts():
        t1 = sc.dma("sp", lambda e: e.dma_start(out=CON[:], in_=consts[:, :]), "d_misc")
        t2 = sc.dma("sp", lambda e: e.dma_start(out=PV[:], in_=pvec[:, :]), "d_misc")
        t2 = sc.dma("sp", lambda e: e.dma_start(out=PVM[:], in_=pvm_in[:, :]), "d_misc")
        return t2

    ONESB_t = sb("ONESB", [128, 128], BF16)
    IDB_t = sb("IDB", [128, 128], BF16)

    def convert_all():
        blocks = []

        def conv_block(src_ap, dst_ap, ncols, nsl, sw):
            blocks.append((src_ap, dst_ap, ncols, sw))

        def run_blocks():
            store_tok = [None] * NCV
            ld = {}

            def issue_load(k):
                src_ap, _, ncols, _ = blocks[k]
                i = k % NCV
                ld[k] = sc.dma("sp", lambda e: e.dma_start(out=CVI[i][:, 0:ncols], in_=src_ap), f"d_cvi{i}")
            issue_load(0)
            if len(blocks) > 1:
                issue_load(1)
            for k in range(len(blocks)):
                if k + 2 < len(blocks):
                    issue_load(k + 2)
                _, dst_ap, ncols, sw = blocks[k]
                i = k % NCV
                eng = ("dve", "pool", "act")[k % 3]

                def cast(e, i=i, ncols=ncols, eng=eng):
                    if eng == "act":
                        return e.copy(out=CVO[i][:, 0:ncols], in_=CVI[i][:, 0:ncols])
                    return e.tensor_copy(out=CVO[i][:, 0:ncols], in_=CVI[i][:, 0:ncols])
                sc.op(eng, cast, waits=[ld[k], store_tok[i]])
                store_tok[i] = sc.dma(
                    "sp", lambda e, i=i, ncols=ncols, sw=sw, dst_ap=dst_ap: e.dma_start(
                        out=dst_ap, in_=CVO[i][:, 0:ncols].rearrange("p (s c) -> p s c", c=sw)), f"d_cvo{i}")
            sc.op("dve", lambda e: e.tensor_copy(out=SG[:, 0:8], in_=SG[:, 8:16]), waits=store_tok)

        for l in range(DEPTH):
            for f in ("ffn1", "ffn2"):
                for nm, key in ((f + "_w_gate", "g"), (f + "_w_up", "u")):
                    src = wsrc[nm]
                    dst = wb[(l, f, key)].rearrange("s p (k c) -> s p k c", c=256)
                    for kc in range(KC):
                        for cb in range(0, DFF, 2048):
                            ncols = min(2048, DFF - cb)
                            s0 = cb // 256
                            nsl = ncols // 256
                            conv_block(src[l, kc * 128:(kc + 1) * 128, cb:cb + ncols],
                                       dst[s0:s0 + nsl, :, kc, :].rearrange("s p c -> p s c"), ncols, nsl, 256)
                src = wsrc[f + "_w_down"]
                dst = wb[(l, f, "d")].rearrange("s p (k c) -> s p k c", c=128)
                for kc in range(FCH):
                    conv_block(src[l, kc * 128:(kc + 1) * 128, :],
                               dst[:, :, kc, :].rearrange("s p c -> p s c"), 2048, 16, 128)
            src = wsrc["w_in"]
            dst = wb[(l, "in")].rearrange("s p (k c) -> s p k c", c=256)
            for kc in range(KC):
                for (c0, d0, ncols) in ((0, 0, 2048), (2048, 2048, 2048), (4096, 4096, 1024), (5128, 5120, 1024)):
                    s0 = d0 // 256
                    nsl = ncols // 256
                    conv_block(src[l, kc * 128:(kc + 1) * 128, c0:c0 + ncols],
                               dst[s0:s0 + nsl, :, kc, :].rearrange("s p c -> p s c"), ncols, nsl, 256)
                dstd = wb[(l, "dt")].rearrange("p (k c) -> p k c", c=8)
                conv_block(src[l, kc * 128:(kc + 1) * 128, 5120:5128], dstd[:, kc:kc + 1, :], 8, 1, 8)
            src = wsrc["w_out"]
            dst = wb[(l, "out")].rearrange("s p (k c) -> s p k c", c=128)
            for kc in range(KC):
                conv_block(src[l, kc * 128:(kc + 1) * 128, :],
                           dst[:, :, kc, :].rearrange("s p c -> p s c"), 2048, 16, 128)
        run_blocks()

    def load_x_tile(l, ti):
        t0 = ti * T
        if l == 0:
            ld = sc.dma("sp", lambda e: e.dma_start(
                out=STG[:], in_=x_in[t0:t0 + T, :].rearrange("(c p) d -> p c d", p=128)), "d_x")
            first = True
            for kc in range(KC):
                def tr(e, kc=kc):
                    ins = None
                    for c in range(4):
                        ins = e.transpose(ps[kc % 2][:, c * 128:(c + 1) * 128], STG[:, c, kc * 128:(kc + 1) * 128], IDF)
                    return ins
                sc.op("pe", tr, waits=[ld] if first else [])
                first = False
                sc.op("act" if kc % 2 else "dve",
                      (lambda e, kc=kc: e.copy(out=XT[:, kc, :], in_=ps[kc % 2][:])) if kc % 2 else
                      (lambda e, kc=kc: e.tensor_copy(out=XT[:, kc, :], in_=ps[kc % 2][:])))
        else:
            ld = sc.dma("sp", lambda e: e.dma_start(
                out=XT[:], in_=xs[:, :, t0:t0 + T].rearrange("k p t -> p k t")), "d_x")
            sc.op("dve", lambda e: e.tensor_copy(out=SG[:, 0:8], in_=SG[:, 8:16]), waits=[ld])

    def store_x_tile(ti):
        t0 = ti * T
        st = sc.dma("sp", lambda e: e.dma_start(
            out=xs[:, :, t0:t0 + T].rearrange("k p t -> p k t"), in_=XT[:]), "d_xst")
        sc.op("dve", lambda e: e.tensor_copy(out=SG[:, 0:8], in_=SG[:, 8:16]), waits=[st])

    def rmsnorm_to(dst_fn, wcol):
        sc.op("act", lambda e: e.activation(out=SQ[:], in_=XT[:], func=AF.Square))

        def mm(e):
            ins = None
            for kc in range(KC):
                ins = e.matmul(ps[2][:], ONESB_t[:], SQ[:, kc, :], start=(kc == 0), stop=(kc == KC - 1))
            return ins
        sc.op("pe", mm)

        sc.op("dve", lambda e: e.tensor_scalar(out=RSTD[:], in0=ps[2][:], scalar1=1.0 / D, scalar2=EPS, op0=ALU.mult, op1=ALU.add))
        sc.op("act", lambda e: e.sqrt(out=RSTD[:], in_=RSTD[:]))
        sc.op("dve", lambda e: e.reciprocal(out=RSTD[:], in_=RSTD[:]))

        def hh(e):
            ins = None
            for kc in range(KC):
                ins = e.scalar_tensor_tensor(out=dst_fn(kc), in0=XT[:, kc, :], scalar=PV[:, wcol + kc:wcol + kc + 1],
                                             in1=RSTD[:], op0=ALU.mult, op1=ALU.mult)
            return ins
        sc.op("dve", hh)

    wgu_free = [None] * 4
    wd_free = [None] * 2

    def ffn(l, f, wcol):
        rmsnorm_to(lambda kc: HT[:, kc, :], wcol)
        wg, wu, wd = wb[(l, f, "g")], wb[(l, f, "u")], wb[(l, f, "d")]
        base = sc.last
        SGs = (SG, RSTD)
        banks = ((ps[0], ps[1]), (ps[4], ps[5]))
        pe_t, act_t, dve_t, loads = {}, {}, {}, {}

        def issue_gu(s_):
            b = (s_ % 2) * 2
            w = [pe_t[2 * (s_ - 2) + 1]] if s_ >= 2 else [base]
            tg = sc.dma("sp", lambda e: e.dma_start(out=WGU[b][:].rearrange("p k c -> p (k c)"), in_=wg[s_]), f"d_wgu{b}",
                        waits=w, chain=False)
            tu = sc.dma("sp", lambda e: e.dma_start(out=WGU[b + 1][:].rearrange("p k c -> p (k c)"), in_=wu[s_]), f"d_wgu{b + 1}",
                        waits=w, chain=False)
            loads[s_] = (tg, tu)
        issue_gu(0)
        issue_gu(1)
        for fc in range(FCH):
            s_, j = fc // 2, fc % 2
            if j == 0 and s_ >= 1 and s_ + 1 < 22:
                issue_gu(s_ + 1)
            b = (s_ % 2) * 2
            pg, pu = banks[fc % 2]

            def mm(e, b=b, j=j, pg=pg, pu=pu):
                ins = None
                for kc in range(KC):
                    ins = e.matmul(pg[:], WGU[b][:, kc, j * 128:(j + 1) * 128], HT[:, kc, :], start=(kc == 0), stop=(kc == KC - 1))
                for kc in range(KC):
                    ins = e.matmul(pu[:], WGU[b + 1][:, kc, j * 128:(j + 1) * 128], HT[:, kc, :], start=(kc == 0), stop=(kc == KC - 1))
                return ins
            w = list(loads[s_]) if j == 0 else []
            w.append(dve_t[fc - 2] if fc >= 2 else base)
            pe_t[fc] = sc.op("pe", mm, waits=w, chain=False)
            sgb = SGs[fc % 2]
            act_t[fc] = sc.op("act", lambda e, sgb=sgb, pg=pg: e.activation(out=sgb[:], in_=pg[:], func=AF.Silu),
                              waits=[pe_t[fc]] + ([dve_t[fc - 2]] if fc >= 2 else []), chain=False)
            dve_t[fc] = sc.op("dve", lambda e, fc=fc, sgb=sgb, pu=pu: e.tensor_tensor(out=AT[:, fc, :], in0=sgb[:], in1=pu[:], op=ALU.mult),
                              waits=[act_t[fc]], chain=False)
        sc.last = dve_t[FCH - 1]
        base2 = sc.last
        dbanks = (ps[3], ps[6])
        pd_t, dd_t, dl = {}, {}, {}

        def issue_d(oc):
            b = oc % 2
            w = [pd_t[oc - 2]] if oc >= 2 else [base2]
            dl[oc] = sc.dma("sp", lambda e: e.dma_start(out=WD[b][:].rearrange("p k c -> p (k c)"), in_=wd[oc]), f"d_wd{b}",
                            waits=w, chain=False)
        issue_d(0)
        issue_d(1)
        for oc in range(16):
            if oc >= 1 and oc + 1 < 16:
                issue_d(oc + 1)
            b = oc % 2
            pb_ = dbanks[oc % 2]

            def mm(e, b=b, pb_=pb_):
                ins = None
                for k in range(FCH):
                    ins = e.matmul(pb_[:], WD[b][:, k, :], AT[:, k, :], start=(k == 0), stop=(k == FCH - 1))
                return ins
            pd_t[oc] = sc.op("pe", mm, waits=[dl[oc], dd_t[oc - 2] if oc >= 2 else base2], chain=False)
            dd_t[oc] = sc.op("dve", lambda e, oc=oc, pb_=pb_: e.scalar_tensor_tensor(out=XT[:, oc, :], in0=pb_[:], scalar=0.5,
                                                                                   in1=XT[:, oc, :], op0=ALU.mult, op1=ALU.add),
                             waits=[pd_t[oc]], chain=False)
        sc.last = dd_t[15]

    def final_store(ti):
        t0 = ti * T
        FN = nc.alloc_sbuf_tensor_at(f"FN{ti}", [128, T], F32, offset=arena0 + 32768)
        sc.op("act", lambda e: e.activation(out=HT[:], in_=XT[:], func=AF.Square))

        def mm(e):
            ins = None
            for kc in range(KC):
                ins = e.matmul(ps[2][:], ONESB_t[:], HT[:, kc, :], start=(kc == 0), stop=(kc == KC - 1))
            return ins
        sc.op("pe", mm)

        sc.op("dve", lambda e: e.tensor_scalar(out=RSTD[:], in0=ps[2][:], scalar1=1.0 / D, scalar2=EPS, op0=ALU.mult, op1=ALU.add))
        sc.op("act", lambda e: e.sqrt(out=RSTD[:], in_=RSTD[:]))
        sc.op("dve", lambda e: e.reciprocal(out=RSTD[:], in_=RSTD[:]))
        wcol = 64 * DEPTH
        for kc in range(KC):
            sc.op("dve", lambda e, kc=kc: e.scalar_tensor_tensor(out=FN[:], in0=XT[:, kc, :], scalar=PV[:, wcol + kc:wcol + kc + 1],
                                                                 in1=RSTD[:], op0=ALU.mult, op1=ALU.mult))

            def tr(e, kc=kc):
                ins = None
                for c in range(4):
                    ins = e.transpose(ps[4][:, c * 128:(c + 1) * 128], FN[:, c * 128:(c + 1) * 128], IDF)
                return ins
            sc.op("pe", tr)
            sc.op("act", lambda e, kc=kc: e.copy(out=STG[:, :, kc * 128:(kc + 1) * 128],
                                                 in_=ps[4][:].rearrange("p (c t) -> p c t", c=4)))
        st = sc.dma("sp", lambda e: e.dma_start(out=out[t0:t0 + T, :].rearrange("(c p) d -> p c d", p=128), in_=STG[:]), "d_out")
        sc.op("dve", lambda e: e.tensor_copy(out=SG[:, 0:8], in_=SG[:, 8:16]), waits=[st])


    PVM = sb("PVM", [128, 128 * DEPTH], F32)
    LBD = sb("LBD", [128, 2, 4, 128], BF16)
    HALO_D = sb("HALO_D", [128, 4, 4], F32)
    HPREV = sb("HPREV", [128, 4], F32)
    assert off[0] <= 229344, off[0]
    wdbase = 16640 + 0
    uid = [0]

    class Tmp:
        def __init__(self, base, size):
            self.base, self.size, self.o, self.n = base, size, 0, 0
        def get(self, shape, dt):
            nbytes = int(np.prod(shape[1:])) * (4 if dt == F32 else 2)
            nbytes = (nbytes + 63) // 64 * 64
            assert self.o + nbytes <= self.size, (self.o, nbytes, self.size)
            self.n += 1
            uid[0] += 1
            t = nc.alloc_sbuf_tensor_at(f"tmp{uid[0]}", list(shape), dt, offset=self.base + self.o)
            self.o += nbytes
            return t
    YT = nc.alloc_sbuf_tensor_at("YT", [128, KC, T], BF16, offset=arena0)
    wd_off = [None]

    win_i = [0]

    def win_slab(l, s):
        b = win_i[0] % 4
        win_i[0] += 1
        tok = sc.dma("sp", lambda e: e.dma_start(out=WGU[b][:].rearrange("p k c -> p (k c)"), in_=wb[(l, "in")][s]), f"d_wgu{b}")
        return b, tok

    def proj_fm(b, j, pst, tok=None):
        def mm(e):
            ins = None
            for kc in range(KC):
                ins = e.matmul(pst[:], WGU[b][:, kc, j * 128:(j + 1) * 128], HT[:, kc, :], start=(kc == 0), stop=(kc == KC - 1))
            return ins
        sc.op("pe", mm, waits=[tok])

    def mixer_lru(l, ti, TA, TB):
        pc = 128 * l
        YD = TA.get([128, 4, T], F32)
        XB = TB.get([128, T + 4], F32)
        XC = TB.get([128, T], F32)
        XCB = TB.get([128, T], BF16)
        R = TB.get([128, T], F32)
        I = TB.get([128, T], F32)
        A = TB.get([128, T], F32)
        G = TB.get([128, T], F32)
        slabs = {}
        for j in range(4):
            sx = 20 + j // 2
            sg = 22 + j // 2
            if j % 2 == 0:
                slabs["x"] = win_slab(l, sx)
                slabs["g"] = win_slab(l, sg)
            bx, tx = slabs["x"]
            bg, tg = slabs["g"]
            proj_fm(bx, j % 2, ps[0], tx)
            sc.op("act", lambda e: e.copy(out=XB[:, 3:T + 3], in_=ps[0][:]))
            sc.op("dve", lambda e, j=j: e.tensor_copy(out=XB[:, 0:3], in_=HALO_D[:, j, 0:3]))
            sc.op("dve", lambda e, j=j: e.tensor_copy(out=HALO_D[:, j, 0:3], in_=XB[:, T:T + 3]))

            def conv(e, j=j):
                e.tensor_scalar(out=XC[:], in0=XB[:, 0:T], scalar1=PVM[:, pc + j * 4:pc + j * 4 + 1],
                                scalar2=PVM[:, pc + 16 + j:pc + 17 + j], op0=ALU.mult, op1=ALU.add)
                ins = None
                for tp in range(1, 4):
                    ins = e.scalar_tensor_tensor(out=XC[:], in0=XB[:, tp:tp + T], scalar=PVM[:, pc + j * 4 + tp:pc + j * 4 + tp + 1],
                                                 in1=XC[:], op0=ALU.mult, op1=ALU.add)
                return ins
            sc.op("dve", conv)
            sc.op("act", lambda e: e.copy(out=XCB[:], in_=XC[:]))
            sc.op("pe", lambda e, j=j: e.matmul(ps[1][:], LBD[:, 0, j, :], XCB[:], start=True, stop=True))
            sc.op("act", lambda e, j=j: e.activation(out=R[:], in_=ps[1][:], func=AF.Sigmoid, bias=PVM[:, pc + 20 + j:pc + 21 + j]))
            sc.op("pe", lambda e, j=j: e.matmul(ps[1][:], LBD[:, 1, j, :], XCB[:], start=True, stop=True))
            sc.op("act", lambda e, j=j: e.activation(out=I[:], in_=ps[1][:], func=AF.Sigmoid, bias=PVM[:, pc + 24 + j:pc + 25 + j]))
            sc.op("act", lambda e, j=j: e.activation(out=A[:], in_=R[:], func=AF.Exp, scale=PVM[:, pc + 40 + j:pc + 41 + j]))

            def bt(e):
                e.tensor_tensor(out=R[:], in0=A[:], in1=A[:], op=ALU.mult)
                e.tensor_scalar(out=R[:], in0=R[:], scalar1=-1.0, scalar2=1.0, op0=ALU.mult, op1=ALU.add)
                return e.tensor_scalar(out=R[:], in0=R[:], scalar1=0.0, scalar2=None, op0=ALU.max)
            sc.op("dve", bt)
            sc.op("act", lambda e: e.sqrt(out=R[:], in_=R[:]))

            def bt2(e):
                e.tensor_tensor(out=I[:], in0=I[:], in1=XC[:], op=ALU.mult)
                return e.tensor_tensor(out=I[:], in0=I[:], in1=R[:], op=ALU.mult)
            sc.op("dve", bt2)
            sc.op("dve", lambda e, j=j: e.tensor_tensor_scan(out=XC[:], data0=A[:], data1=I[:], initial=HPREV[:, j:j + 1],
                                                              op0=ALU.mult, op1=ALU.add))
            sc.op("dve", lambda e, j=j: e.tensor_copy(out=HPREV[:, j:j + 1], in_=XC[:, T - 1:T]))
            proj_fm(bg, j % 2, ps[0], tg)
            sc.op("act", lambda e: e.activation(out=G[:], in_=ps[0][:], func=AF.Gelu))
            sc.op("dve", lambda e, j=j: e.tensor_tensor(out=YD[:, j, :], in0=XC[:], in1=G[:], op=ALU.mult))
        group_norm_to_yt(YD, [0, 1, 2, 3], 12, pc + 32, TA)

    def group_norm_to_yt(YD, chunks, ybase, wcol, TB):
        n = len(chunks)
        SQm = TB.get([128, n, T], BF16)
        RS = TB.get([128, T], F32)
        sc.op("act", lambda e: e.activation(out=SQm[:], in_=YD[:, chunks[0]:chunks[0] + n, :], func=AF.Square))

        def mm(e):
            ins = None
            for i in range(n):
                ins = e.matmul(ps[2][:], ONESB_t[:], SQm[:, i, :], start=(i == 0), stop=(i == n - 1))
            return ins
        sc.op("pe", mm)
        sc.op("dve", lambda e: e.tensor_scalar(out=RS[:], in0=ps[2][:], scalar1=1.0 / (128 * n), scalar2=EPS, op0=ALU.mult, op1=ALU.add))
        sc.op("act", lambda e: e.sqrt(out=RS[:], in_=RS[:]))
        sc.op("dve", lambda e: e.reciprocal(out=RS[:], in_=RS[:]))

        def hh(e):
            ins = None
            for i, c in enumerate(chunks):
                ins = e.scalar_tensor_tensor(out=YT[:, ybase + i, :], in0=YD[:, c, :], scalar=PVM[:, wcol + i:wcol + i + 1],
                                             in1=RS[:], op0=ALU.mult, op1=ALU.mult)
            return ins
        sc.op("dve", hh)


    MASK = sb("MASK", [128, 17, 128], BF16)
    KTH = sb("KTH", [128, 4, 2560], BF16)
    VH = sb("VH", [128, 20, 8, 72], BF16)
    WBA = sb("WBA", [128, 512], F32)
    assert off[0] <= 229344, off[0]

    def attn_init():
        ld = sc.dma("sp", lambda e: e.dma_start(out=STG[:, 0, :].rearrange("p (a b) -> p a b", b=128)[:, 0:17, :] if False else XT[:, 0:5, :].rearrange("p a b -> p (a b)")[:, 0:17 * 128], in_=amask_in[:, :]), "d_misc")
        sc.op("dve", lambda e: e.tensor_copy(out=MASK[:].rearrange("p a b -> p (a b)"), in_=XT[:, 0:5, :].rearrange("p a b -> p (a b)")[:, 0:17 * 128]), waits=[ld])
        sc.op("dve", lambda e: e.memset(VH[:], 1.0))

    def dump(name, ap, ncols):
        if not DBG:
            return
        DSTG = nc.alloc_sbuf_tensor_at(f"DSTG{dbgpos[0]}", [128, 2048], F32, offset=arena0 + 16384)
        c0 = dbgpos[0]
        dbgpos[0] += ncols
        DBGMAP[name] = (c0, ncols)
        sc.op("dve", lambda e: e.tensor_copy(out=DSTG[:, 0:ncols], in_=ap))
        st = sc.dma("sp", lambda e: e.dma_start(out=dbgout[:, c0:c0 + ncols], in_=DSTG[:, 0:ncols]), "d_dbg")
        sc.op("dve", lambda e: e.tensor_copy(out=SG[:, 0:8], in_=SG[:, 8:16]), waits=[st])

    def mixer_attn(l, ti, TA, TB):
        pc = 128 * l
        QZ = TA.get([128, 8, T], BF16)
        E = TB.get([128, 512], BF16)
        P = TB.get([128, 4, 128], BF16)
        OT = TB.get([128, 512], F32)
        JK = TB.get([128, 512], F32)
        YM = TB.get([128, 512], F32)
        SS = TB.get([128, 16], F32)
        bslot = (ti * 4) % 20
        NB = (1, 2, 6, 7)
        for j in range(4):
            if j % 2 == 0:
                sq_ = win_slab(l, 8 + j // 2)
                sk_ = win_slab(l, 10 + j // 2)
            proj_fm(sq_[0], j % 2, ps[0], sq_[1])
            if j == 0:
                sc.op("dve", lambda e: e.memset(QZ[:], 0.0))
            sc.op("act", lambda e, j=j: e.mul(out=QZ[0:64, 2 * j, :], in_=ps[0][0:64, :], mul=0.125))
            sc.op("act", lambda e, j=j: e.mul(out=QZ[64:128, 2 * j + 1, :], in_=ps[0][64:128, :], mul=0.125))
            proj_fm(sk_[0], j % 2, ps[0], sk_[1])
            sc.op("act", lambda e, j=j: e.copy(out=KTH[:, j, bslot * 128:bslot * 128 + T], in_=ps[0][:]))
        v0 = win_slab(l, 12)
        v1 = win_slab(l, 13)
        for c in range(4):
            def mm(e, c=c):
                ins = None
                for half, (b, _) in enumerate((v0, v1)):
                    for kc in range(KC):
                        ins = e.matmul(ps[0][:, half * 256:(half + 1) * 256], HT[:, kc, c * 128:(c + 1) * 128], WGU[b][:, kc, :],
                                       start=(kc == 0), stop=(kc == KC - 1))
                return ins
            sc.op("pe", mm, waits=[v0[1], v1[1]] if c == 0 else [])
            sc.op("act", lambda e, c=c: e.copy(out=VH[:, bslot + c, :, 0:64], in_=ps[0][:].rearrange("p (h d) -> p h d", d=64)))
        import os
        STAGE = int(os.environ.get('ATT_STAGE', '9'))
        for c in range(4 if STAGE >= 2 else 0):
            g = ti * 4 + c
            nd = min(16, g) + 1
            for quad in range(2):
                for dl in range(nd):
                    ks = (g - dl) % 20

                    def smm(e, c=c, quad=quad, ks=ks):
                        ins = None
                        for hh in range(4):
                            h = quad * 4 + hh
                            j, pb = h // 2, (h % 2) * 64
                            if os.environ.get('ATT_PB0'):
                                pb = 0
                            ins = e.matmul(ps[4][:, hh * 128:(hh + 1) * 128], KTH[:, j, ks * 128:(ks + 1) * 128],
                                           QZ[:, h, c * 128:(c + 1) * 128], start=True, stop=True)
                        return ins
                    sc.op("pe", smm)
                    if os.environ.get('ATT_MMONLY'):
                        continue
                    sc.op("act", lambda e: e.activation(out=E[:], in_=ps[4][:], func=AF.Exp))

                    def msk(e, dl=dl):
                        ins = None
                        for hh in range(4):
                            ins = e.tensor_tensor(out=P[:, hh, :], in0=E[:, hh * 128:(hh + 1) * 128], in1=MASK[:, dl, :], op=ALU.mult)
                        return ins
                    sc.op("dve", msk)

                    def nmm(e, quad=quad, ks=ks, dl=dl, nd=nd):
                        ins = None
                        for hh in range(4):
                            h = quad * 4 + hh
                            ins = e.matmul(ps[NB[hh]][:, 0:65], P[:, hh, :], VH[:, ks, h, 0:65],
                                           start=(dl == 0), stop=(dl == nd - 1))
                        return ins
                    if STAGE >= 3:
                        sc.op("pe", nmm)

                def rcp(e):
                    ins = None
                    for hh in range(4):
                        ins = e.reciprocal(out=SS[:, hh:hh + 1], in_=ps[NB[hh]][:, 64:65])
                    return ins
                if STAGE >= 4:
                    sc.op("dve", rcp)

                def fin(e, quad=quad):
                    ins = None
                    for hh in range(4):
                        h = quad * 4 + hh
                        ins = e.tensor_scalar(out=OT[:, h * 64:(h + 1) * 64], in0=ps[NB[hh]][:, 0:64], scalar1=SS[:, hh:hh + 1],
                                              scalar2=None, op0=ALU.mult)
                    return ins
                if STAGE >= 4:
                    sc.op("dve", fin)
            if STAGE < 5:
                continue
            if ti == 0 and c == 0 and l == 0:
                dump("QZ0", QZ[:, 0, 0:128], 128)
                dump("QZ1", QZ[:, 1, 0:128], 128)
                dump("K0", KTH[:, 0, 0:128], 128)
                dump("V0", VH[:, 0, 0, :], 72)
                dump("V1", VH[:, 0, 1, :], 72)
                dump("E", E[:], 512)
                dump("P", P[:].rearrange("p a b -> p (a b)"), 512)
                dump("OT", OT[:], 512)
                dump("MASK0", MASK[:, 0, :], 128)
                dump("HT", HT[:, :, 0:128], 2048)
            sc.op("act", lambda e: e.activation(out=JK[:], in_=OT[:], func=AF.Square))
            sc.op("dve", lambda e: e.reduce_sum(out=SS[:, 8:9], in_=JK[:], axis=mybir.AxisListType.X))
            sc.op("dve", lambda e: e.tensor_scalar(out=SS[:, 8:9], in0=SS[:, 8:9], scalar1=1.0 / 512, scalar2=EPS, op0=ALU.mult, op1=ALU.add))
            sc.op("act", lambda e: e.sqrt(out=SS[:, 8:9], in_=SS[:, 8:9]))
            sc.op("dve", lambda e: e.reciprocal(out=SS[:, 8:9], in_=SS[:, 8:9]))
            sc.op("dve", lambda e: e.scalar_tensor_tensor(out=YM[:], in0=OT[:], scalar=SS[:, 8:9], in1=WBA[:], op0=ALU.mult, op1=ALU.mult))

            def tr(e):
                ins = None
                for j in range(4):
                    ins = e.transpose(ps[5][:, j * 128:(j + 1) * 128], YM[:, j * 128:(j + 1) * 128], IDF)
                return ins
            sc.op("pe", tr)
            sc.op("act", lambda e, c=c: e.copy(out=YT[:, 4:8, c * 128:(c + 1) * 128], in_=ps[5][:].rearrange("p (j t) -> p j t", t=128)))

    SSMB = sb("SSMB", [128, 528], F32)
    AB = sb("AB", [128, 8], F32)
    WDT = sb("WDT", [128, KC, 8], BF16)
    ST_C = sb("ST_C", [128, 512], F32)
    STB_C = sb("STB_C", [128, 512], BF16)
    HALO_C = sb("HALO_C", [128, 8, 4], F32)
    assert off[0] <= 229344, off[0]
    ONESF = CON[:, 128:256]
    TRI = CON[:, 256:384]
    NEGM = CON[:, 384:512]

    def ssd_layer_init(l):
        ld = sc.dma("sp", lambda e: e.dma_start(out=SSMB[:], in_=ssmb_in[l]), "d_misc")
        ld2 = sc.dma("sp", lambda e: e.dma_start(out=WDT[:].rearrange("p k c -> p (k c)"), in_=wb[(l, "dt")]), "d_misc")
        sc.op("act", lambda e: e.activation(out=AB[:], in_=SSMB[:, 520:528], func=AF.Exp), waits=[ld, ld2])
        sc.op("dve", lambda e: e.tensor_scalar(out=AB[:], in0=AB[:], scalar1=-1.0, scalar2=None, op0=ALU.mult))
        sc.op("dve", lambda e: e.memset(ST_C[:], 0.0))
        sc.op("dve", lambda e: e.memset(STB_C[:], 0.0))
        sc.op("dve", lambda e: e.memset(HALO_C[:], 0.0))

    def mixer_ssd(l, ti, TA, TB):
        pc = 128 * l
        XB = TB.get([128, T + 4], F32)
        XC = TB.get([128, T], F32)
        XF = TA.get([128, 6, T], F32)
        BCT = TA.get([128, 4, T], BF16)
        YC = TA.get([128, 4, T], F32)
        XTM = TB.get([128, 512], F32)
        XDT = TB.get([128, 512], BF16)
        XDS = TA.get([128, 512], BF16)
        BTM = TB.get([128, 2, 128], BF16)
        SM = TB.get([128, 64], F32)
        LM = TB.get([128, 4, 128], F32)
        WW = TB.get([128, 8, 128], BF16)
        YTM = TB.get([128, 512], F32)
        TMP = TA.get([128, 512], F32)
        ADTB = TMP[:].rearrange("p (a b) -> p a b", b=128)
        for j in range(8):
            if j % 2 == 0:
                sl = win_slab(l, 16 + j // 2)
            proj_fm(sl[0], j % 2, ps[0], sl[1])
            sc.op("act", lambda e: e.copy(out=XB[:, 3:T + 3], in_=ps[0][:]))
            sc.op("dve", lambda e, j=j: e.tensor_copy(out=XB[:, 0:3], in_=HALO_C[:, j, 0:3]))
            sc.op("dve", lambda e, j=j: e.tensor_copy(out=HALO_C[:, j, 0:3], in_=XB[:, T:T + 3]))

            def conv(e, j=j):
                e.tensor_scalar(out=XC[:], in0=XB[:, 0:T], scalar1=PVM[:, pc + 48 + j * 4:pc + 49 + j * 4],
                                scalar2=PVM[:, pc + 80 + j:pc + 81 + j], op0=ALU.mult, op1=ALU.add)
                ins = None
                for tp in range(1, 4):
                    ins = e.scalar_tensor_tensor(out=XC[:], in0=XB[:, tp:tp + T], scalar=PVM[:, pc + 48 + j * 4 + tp:pc + 49 + j * 4 + tp],
                                                 in1=XC[:], op0=ALU.mult, op1=ALU.add)
                return ins
            sc.op("dve", conv)
            if j < 6:
                sc.op("act", lambda e, j=j: e.activation(out=XF[:, j, :], in_=XC[:], func=AF.Silu))
                if j >= 4:
                    sc.op("dve", lambda e, j=j: e.tensor_copy(out=BCT[:, j - 4, :], in_=XF[:, j, :]))
            else:
                sc.op("act", lambda e, j=j: e.activation(out=BCT[:, j - 4, :], in_=XC[:], func=AF.Silu))
        for c in range(4):
            cs = slice(c * 128, (c + 1) * 128)
            def trx(e, cs=cs):
                ins = None
                for j in range(4):
                    ins = e.transpose(ps[7][:, j * 128:(j + 1) * 128], XF[:, j, cs], IDF)
                return ins
            sc.op("pe", trx)
            sc.op("act", lambda e: e.copy(out=XTM[:], in_=ps[7][:]))

            def trb(e, cs=cs):
                ins = None
                for g in range(2):
                    ins = e.transpose(ps[7][:, g * 128:(g + 1) * 128], XF[:, 4 + g, cs], IDF)
                return ins
            sc.op("pe", trb)
            sc.op("act", lambda e: e.copy(out=BTM[:].rearrange("p g n -> p (g n)"), in_=ps[7][:, 0:256]))
            def dtmm(e, cs=cs):
                ins = None
                for kc in range(KC):
                    ins = e.matmul(ps[1][:, 0:8], HT[:, kc, cs], WDT[:, kc, :], start=(kc == 0), stop=(kc == KC - 1))
                return ins
            sc.op("pe", dtmm)
            sc.op("dve", lambda e: e.tensor_tensor(out=SM[:, 0:8], in0=ps[1][:, 0:8], in1=SSMB[:, 512:520], op=ALU.add))
            sc.op("act", lambda e: e.activation(out=SM[:, 0:8], in_=SM[:, 0:8], func=AF.Exp))
            sc.op("act", lambda e: e.activation(out=SM[:, 0:8], in_=SM[:, 0:8], func=AF.Ln, bias=1.0))
            sc.op("dve", lambda e: e.tensor_tensor(out=SM[:, 8:16], in0=SM[:, 0:8], in1=AB[:], op=ALU.mult))
            sc.op("pe", lambda e: e.matmul(ps[1][:, 16:24], TRI, SM[:, 8:16], start=True, stop=True))
            sc.op("dve", lambda e: e.tensor_copy(out=SM[:, 16:24], in_=ps[1][:, 16:24]))
            sc.op("act", lambda e: e.activation(out=SM[:, 24:32], in_=SM[:, 16:24], func=AF.Exp))
            sc.op("dve", lambda e: e.tensor_scalar(out=SM[:, 32:40], in0=SM[:, 16:24], scalar1=-1.0, scalar2=None, op0=ALU.mult))
            def xdt(e):
                ins = None
                for h in range(8):
                    ins = e.tensor_scalar(out=XDT[:, h * 64:(h + 1) * 64], in0=XTM[:, h * 64:(h + 1) * 64], scalar1=SM[:, h:h + 1],
                                          scalar2=None, op0=ALU.mult)
                return ins
            sc.op("dve", xdt)
            for g in range(2):
                def adtb(e, g=g):
                    ins = None
                    for hh in range(4):
                        h = g * 4 + hh
                        ins = e.tensor_scalar(out=ADTB[:, hh, :], in0=ONESF, scalar1=SM[:, 8 + h:9 + h], scalar2=None, op0=ALU.mult)
                    return ins
                sc.op("dve", adtb)

                def bc(e):
                    ins = None
                    for hh in range(4):
                        e.matmul(ps[2][:, hh * 128:(hh + 1) * 128], ADTB[:, hh, :], TRI, start=True, stop=False)
                        ins = e.matmul(ps[2][:, hh * 128:(hh + 1) * 128], IDF, NEGM, start=False, stop=True)
                    return ins
                sc.op("pe", bc)

                def lm(e, g=g):
                    ins = None
                    for hh in range(4):
                        h = g * 4 + hh
                        e.activation(out=LM[:, hh, :], in_=ps[2][:, hh * 128:(hh + 1) * 128], func=AF.Exp, bias=SM[:, 32 + h:33 + h])
                        ins = e.activation(out=SM[:, 40 + h:41 + h], in_=ps[2][:, hh * 128 + 127:hh * 128 + 128], func=AF.Exp)
                    return ins
                sc.op("act", lm)
                sc.op("pe", lambda e, g=g, cs=cs: e.matmul(ps[3][:, 0:128], BCT[:, g, cs], BCT[:, 2 + g, cs], start=True, stop=True))

                def wmul(e, g=g):
                    ins = None
                    for hh in range(4):
                        ins = e.tensor_tensor(out=WW[:, g * 4 + hh, :], in0=ps[3][:, 0:128], in1=LM[:, hh, :], op=ALU.mult)
                    return ins
                sc.op("dve", wmul)

                def xds(e, g=g):
                    ins = None
                    for hh in range(4):
                        h = g * 4 + hh
                        ins = e.tensor_scalar(out=XDS[:, h * 64:(h + 1) * 64], in0=XDT[:, h * 64:(h + 1) * 64], scalar1=LM[:, hh, 127:128],
                                              scalar2=None, op0=ALU.mult)
                    return ins
                sc.op("dve", xds)

            def ydiag(e):
                ins = None
                for h in range(8):
                    ins = e.matmul(ps[4][:, h * 64:(h + 1) * 64], WW[:, h, :], XDT[:, h * 64:(h + 1) * 64], start=True, stop=True)
                return ins
            sc.op("pe", ydiag)

            def yoff(e, cs=cs):
                ins = None
                for h in range(8):
                    ins = e.matmul(ps[5][:, h * 64:(h + 1) * 64], BCT[:, 2 + h // 4, cs], STB_C[:, h * 64:(h + 1) * 64], start=True, stop=True)
                return ins
            sc.op("pe", yoff)
            sc.op("act", lambda e: e.copy(out=YTM[:], in_=ps[4][:]))

            def yadd(e):
                ins = None
                for h in range(8):
                    ins = e.scalar_tensor_tensor(out=YTM[:, h * 64:(h + 1) * 64], in0=ps[5][:, h * 64:(h + 1) * 64], scalar=SM[:, 24 + h:25 + h],
                                                 in1=YTM[:, h * 64:(h + 1) * 64], op0=ALU.mult, op1=ALU.add)
                return ins
            sc.op("dve", yadd)
            sc.op("dve", lambda e: e.tensor_tensor(out=TMP[:], in0=XTM[:], in1=SSMB[:, 0:512], op=ALU.mult))
            sc.op("dve", lambda e: e.tensor_tensor(out=YTM[:], in0=YTM[:], in1=TMP[:], op=ALU.add))
            def smm(e):
                ins = None
                for g in range(2):
                    ins = e.matmul(ps[6][:, g * 256:(g + 1) * 256], BTM[:, g, :], XDS[:, g * 256:(g + 1) * 256], start=True, stop=True)
                return ins
            sc.op("pe", smm)

            def stu(e):
                ins = None
                for h in range(8):
                    ins = e.scalar_tensor_tensor(out=ST_C[:, h * 64:(h + 1) * 64], in0=ST_C[:, h * 64:(h + 1) * 64], scalar=SM[:, 40 + h:41 + h],
                                                 in1=ps[6][:, h * 64:(h + 1) * 64], op0=ALU.mult, op1=ALU.add)
                return ins
            sc.op("dve", stu)
            sc.op("act", lambda e: e.copy(out=STB_C[:], in_=ST_C[:]))
            def tr(e):
                ins = None
                for j in range(4):
                    ins = e.transpose(ps[7][:, j * 128:(j + 1) * 128], YTM[:, j * 128:(j + 1) * 128], IDF)
                return ins
            sc.op("pe", tr)
            sc.op("act", lambda e, cs=cs: e.copy(out=YC[:, :, cs], in_=ps[7][:].rearrange("p (j t) -> p j t", t=128)))
        for j in range(4):
            if j % 2 == 0:
                sl = win_slab(l, 14 + j // 2)
            proj_fm(sl[0], j % 2, ps[0], sl[1])
            sc.op("act", lambda e: e.activation(out=TMP[:], in_=ps[0][:], func=AF.Silu))
            sc.op("dve", lambda e, j=j: e.tensor_tensor(out=YC[:, j, :], in0=YC[:, j, :], in1=TMP[:], op=ALU.mult))
        TB2 = Tmp(WDBASE, TBSIZE)
        group_norm_to_yt(YC, [0, 1], 8, pc + 88, TB2)
        group_norm_to_yt(YC, [2, 3], 10, pc + 90, Tmp(WDBASE, TBSIZE))

    S_A = sb("S_A", [128, 4, 128], F32)
    LBL = sb("LBL", [128, DEPTH, 4], F32)
    LB = sb("LB", [128, DEPTH, 4], F32)
    OML = sb("OML", [128, DEPTH, 4], F32)
    LBT = sb("LBT", [128, 16], F32)
    assert off[0] <= 229344, off[0]

    def hgrn_init():
        ld = sc.dma("sp", lambda e: e.dma_start(out=LBL[:].rearrange("p a b -> p (a b)"), in_=lbl_in[:, :]), "d_misc")
        sc.op("dve", lambda e: e.tensor_copy(out=LBT[:, 0:4], in_=LBL[:, 0, :]), waits=[ld])
        for l in range(1, DEPTH):
            sc.op("dve", lambda e, l=l: e.tensor_tensor(out=LBT[:, 0:4], in0=LBT[:, 0:4], in1=LBL[:, l, :], op=ALU.max))
        for l in range(DEPTH):
            sc.op("dve", lambda e, l=l: e.tensor_tensor(out=LBL[:, l, :], in0=LBL[:, l, :], in1=LBT[:, 0:4], op=ALU.subtract))
        sc.op("act", lambda e: e.activation(out=LBL[:], in_=LBL[:], func=AF.Exp))
        sc.op("dve", lambda e: e.tensor_copy(out=LBT[:, 4:8], in_=LBL[:, 0, :]))
        for l in range(1, DEPTH):
            sc.op("dve", lambda e, l=l: e.tensor_tensor(out=LBT[:, 4:8], in0=LBT[:, 4:8], in1=LBL[:, l, :], op=ALU.add))
        sc.op("dve", lambda e: e.reciprocal(out=LBT[:, 8:12], in_=LBT[:, 4:8]))
        for l in range(DEPTH):
            sc.op("dve", lambda e, l=l: e.tensor_tensor(out=LBL[:, l, :], in0=LBL[:, l, :], in1=LBT[:, 8:12], op=ALU.mult))
        sc.op("dve", lambda e: e.memset(LB[:, 0, :], 0.0))
        for l in range(1, DEPTH):
            sc.op("dve", lambda e, l=l: e.tensor_tensor(out=LB[:, l, :], in0=LB[:, l - 1, :], in1=LBL[:, l, :], op=ALU.add))
        sc.op("dve", lambda e: e.tensor_scalar(out=OML[:], in0=LB[:], scalar1=-1.0, scalar2=1.0, op0=ALU.mult, op1=ALU.add))

    def mixer_hgrn(l, ti, TA, TB):
        pc = 128 * l
        O = TA.get([128, 4, T], F32)
        KTb = TA.get([128, 4, T], BF16)
        KTM = TA.get([128, 4, 8, 128], BF16)
        VTM = TA.get([128, 8, 512], BF16)
        QTb = YT
        t1 = TB.get([128, T], F32)
        t2 = TB.get([128, T], F32)
        t3 = TB.get([128, T], F32)
        t4 = TB.get([128, T], F32)
        t5 = TB.get([128, T], F32)
        EB = TB.get([128, 3, 4, 8], F32)
        TMPU = TB.get([128, 4, 128], F32)
        PM = TB.get([128, 4, 64], BF16)
        SMID = TB.get([128, 4, 128], BF16)
        for h in range(4):
            sl = win_slab(l, h // 2) if h % 2 == 0 else sl_q
            sl_q = sl
            proj_fm(sl[0], h % 2, ps[0], sl[1])
            sc.op("act", lambda e: e.activation(out=t1[:], in_=ps[0][:], func=AF.Silu))
            slf = win_slab(l, 2 + h // 2) if h % 2 == 0 else sl_f
            sl_f = slf
            proj_fm(slf[0], h % 2, ps[0], slf[1])
            sc.op("act", lambda e: e.activation(out=t2[:], in_=ps[0][:], func=AF.Sigmoid))
            sc.op("dve", lambda e, h=h: e.tensor_scalar(out=t2[:], in0=t2[:], scalar1=OML[:, l, h:h + 1], scalar2=LB[:, l, h:h + 1],
                                                        op0=ALU.mult, op1=ALU.add))
            sc.op("act", lambda e: e.activation(out=t3[:], in_=t2[:], func=AF.Ln))
            sc.op("dve", lambda e: e.tensor_scalar(out=t2[:], in0=t2[:], scalar1=-1.0, scalar2=1.0, op0=ALU.mult, op1=ALU.add))

            def scan(e):
                ins = None
                for c in range(8):
                    ins = e.tensor_tensor_scan(out=t4[:, c * 64:(c + 1) * 64], data0=ONESF[:, 0:64], data1=t3[:, c * 64:(c + 1) * 64],
                                               initial=0.0, op0=ALU.mult, op1=ALU.add)
                return ins
            sc.op("dve", scan)

            def dsub(e):
                ins = None
                for c in range(8):
                    ins = e.tensor_scalar(out=t3[:, c * 64:(c + 1) * 64], in0=t4[:, c * 64:(c + 1) * 64],
                                          scalar1=t4[:, c * 64 + 31:c * 64 + 32], scalar2=None, op0=ALU.subtract)
                return ins
            sc.op("dve", dsub)
            b3 = t4[:].rearrange("p (c t) -> p c t", t=64)
            sc.op("act", lambda e, h=h, b3=b3: e.activation(out=EB[:, 0, h, :], in_=b3[:, :, 31], func=AF.Exp))
            sc.op("act", lambda e, h=h, b3=b3: e.activation(out=EB[:, 1, h, :], in_=b3[:, :, 63], func=AF.Exp))
            sc.op("dve", lambda e, h=h, b3=b3: e.tensor_tensor(out=EB[:, 2, h, :], in0=b3[:, :, 63], in1=b3[:, :, 31], op=ALU.subtract))
            sc.op("act", lambda e, h=h: e.activation(out=EB[:, 2, h, :], in_=EB[:, 2, h, :], func=AF.Exp))
            sc.op("act", lambda e: e.activation(out=t5[:], in_=t3[:], func=AF.Exp))
            sc.op("dve", lambda e, h=h: e.tensor_tensor(out=QTb[:, h, :], in0=t1[:], in1=t5[:], op=ALU.mult))
            sc.op("act", lambda e: e.activation(out=t5[:], in_=t3[:], func=AF.Exp, scale=-1.0))
            sc.op("dve", lambda e: e.tensor_tensor(out=t1[:], in0=t2[:], in1=t5[:], op=ALU.mult))
            sc.op("act", lambda e, h=h: e.copy(out=KTb[:, h, :], in_=t1[:]))
            for rnd in range(2):
                def trk(e, rnd=rnd):
                    ins = None
                    for cc in range(4):
                        c = rnd * 4 + cc
                        ins = e.transpose(ps[7][0:64, cc * 128:(cc + 1) * 128], t1[:, c * 64:(c + 1) * 64], IDF)
                    return ins
                sc.op("pe", trk)
                sc.op("act", lambda e, h=h, rnd=rnd: e.copy(out=KTM[0:64, h, rnd * 4:rnd * 4 + 4, :].rearrange("p a b -> p (a b)"),
                                                             in_=ps[7][0:64, :]))
        v0 = win_slab(l, 4)
        v1 = win_slab(l, 5)
        for c in range(8):
            def mm(e, c=c):
                ins = None
                for half, (b, _) in enumerate((v0, v1)):
                    for kc in range(KC):
                        ins = e.matmul(ps[0][0:64, half * 256:(half + 1) * 256], HT[:, kc, c * 64:(c + 1) * 64], WGU[b][:, kc, :],
                                       start=(kc == 0), stop=(kc == KC - 1))
                return ins
            sc.op("pe", mm, waits=[v0[1], v1[1]] if c == 0 else [])
            sc.op("act", lambda e, c=c: e.copy(out=VTM[0:64, c, :], in_=ps[0][0:64, :]))
        for c in range(8):
            cs = slice(c * 64, (c + 1) * 64)

            def smid(e, c=c):
                ins = None
                for h in range(4):
                    ins = e.tensor_scalar(out=SMID[:, h, :], in0=S_A[:, h, :], scalar1=EB[:, 0, h, c:c + 1], scalar2=None, op0=ALU.mult)
                return ins
            sc.op("dve", smid)

            def pmm(e, cs=cs):
                ins = None
                for h in range(4):
                    ins = e.matmul(ps[3][0:64, h * 64:(h + 1) * 64], KTb[:, h, cs], QTb[:, h, cs], start=True, stop=True)
                return ins
            sc.op("pe", pmm)

            def pmask(e):
                ins = None
                for h in range(4):
                    ins = e.tensor_tensor(out=PM[0:64, h, :], in0=ps[3][0:64, h * 64:(h + 1) * 64], in1=TRI[0:64, 0:64], op=ALU.mult)
                return ins
            sc.op("dve", pmask)

            def omm(e, c=c, cs=cs):
                ins = None
                for h in range(4):
                    e.matmul(ps[4][:, h * 64:(h + 1) * 64], SMID[:, h, :], QTb[:, h, cs], start=True, stop=False)
                    ins = e.matmul(ps[4][:, h * 64:(h + 1) * 64], VTM[0:64, c, h * 128:(h + 1) * 128], PM[0:64, h, :], start=False, stop=True)
                return ins
            sc.op("pe", omm)

            def umm(e, c=c):
                ins = None
                for h in range(4):
                    ins = e.matmul(ps[5][:, h * 128:(h + 1) * 128], KTM[0:64, h, c, :], VTM[0:64, c, h * 128:(h + 1) * 128], start=True, stop=True)
                return ins
            sc.op("pe", umm)

            def utmp(e, c=c):
                ins = None
                for h in range(4):
                    ins = e.tensor_scalar(out=TMPU[:, h, :], in0=ps[5][:, h * 128:(h + 1) * 128], scalar1=EB[:, 2, h, c:c + 1], scalar2=None, op0=ALU.mult)
                return ins
            sc.op("dve", utmp)

            def supd(e, c=c):
                ins = None
                for h in range(4):
                    ins = e.scalar_tensor_tensor(out=S_A[:, h, :], in0=S_A[:, h, :], scalar=EB[:, 1, h, c:c + 1], in1=TMPU[:, h, :],
                                                 op0=ALU.mult, op1=ALU.add)
                return ins
            sc.op("dve", supd)
            sc.op("act", lambda e, cs=cs: e.copy(out=O[:, :, cs], in_=ps[4][:, 0:256].rearrange("p (h t) -> p h t", t=64)))
        SQh = t1[:].bitcast(BF16) if False else None
        for h in range(4):
            slg = win_slab(l, 6 + h // 2) if h % 2 == 0 else sl_g
            sl_g = slg
            sc.op("act", lambda e, h=h: e.activation(out=KTb[:, 0, :], in_=O[:, h, :], func=AF.Square))
            sc.op("pe", lambda e: e.matmul(ps[2][:], ONESB_t[:], KTb[:, 0, :], start=True, stop=True))
            sc.op("dve", lambda e: e.tensor_scalar(out=t2[:], in0=ps[2][:], scalar1=1.0 / 128, scalar2=EPS, op0=ALU.mult, op1=ALU.add))
            sc.op("act", lambda e: e.sqrt(out=t2[:], in_=t2[:]))
            sc.op("dve", lambda e: e.reciprocal(out=t2[:], in_=t2[:]))
            sc.op("dve", lambda e, h=h: e.scalar_tensor_tensor(out=t3[:], in0=O[:, h, :], scalar=PVM[:, pc + 104 + h:pc + 105 + h], in1=t2[:],
                                                               op0=ALU.mult, op1=ALU.mult))
            proj_fm(slg[0], h % 2, ps[0], slg[1])
            sc.op("act", lambda e: e.activation(out=t4[:], in_=ps[0][:], func=AF.Silu))
            sc.op("dve", lambda e, h=h: e.tensor_tensor(out=YT[:, h, :], in0=t3[:], in1=t4[:], op=ALU.mult))

    def layer_init(l):
        pc = 128 * l
        ssd_layer_init(l)
        sc.op("dve", lambda e: e.memset(S_A[:], 0.0))
        ld = sc.dma("sp", lambda e: e.dma_start(out=XT[:, 0:2, :].rearrange("p a (b c) -> p (a b) c", c=128),
                                                 in_=lbd_in[l]), "d_misc")
        sc.op("dve", lambda e: e.tensor_copy(out=LBD[:].rearrange("p a b c -> p (a b) c"),
                                             in_=XT[:, 0:2, :].rearrange("p a (b c) -> p (a b) c", c=128)), waits=[ld])
        ldw = sc.dma("sp", lambda e: e.dma_start(out=WBA[:], in_=anw_in[l]), "d_misc")
        sc.op("dve", lambda e: e.memset(HALO_D[:], 0.0), waits=[ldw])
        sc.op("dve", lambda e: e.memset(HPREV[:], 0.0))
        sc.op("act", lambda e: e.activation(out=PVM[:, pc + 40:pc + 44], in_=PVM[:, pc + 28:pc + 32], func=AF.Exp, scale=-1.0))
        sc.op("act", lambda e: e.activation(out=PVM[:, pc + 40:pc + 44], in_=PVM[:, pc + 40:pc + 44], func=AF.Ln, bias=1.0))
        sc.op("dve", lambda e: e.tensor_scalar(out=PVM[:, pc + 40:pc + 44], in0=PVM[:, pc + 40:pc + 44], scalar1=-8.0, scalar2=None, op0=ALU.mult))

    def out_proj(l):
        wo = wb[(l, "out")]
        for oc in range(16):
            b = win_i[0] % 4
            win_i[0] += 1
            tok = sc.dma("sp", lambda e, b=b, oc=oc: e.dma_start(out=WGU[b][:, :, 0:128], in_=wo[oc].rearrange("p (k c) -> p k c", c=128)), f"d_wgu{b}")

            def mm(e, b=b):
                ins = None
                for kc in range(KC):
                    ins = e.matmul(ps[3][:], WGU[b][:, kc, 0:128], YT[:, kc, :], start=(kc == 0), stop=(kc == KC - 1))
                return ins
            sc.op("pe", mm, waits=[tok])
            sc.op("dve", lambda e, oc=oc: e.tensor_tensor(out=XT[:, oc, :], in0=ps[3][:], in1=XT[:, oc, :], op=ALU.add))

    def mixer(l, ti):
        rmsnorm_to(lambda kc: HT[:, kc, :], 64 * l + 16)
        def mkTA():
            return Tmp(arena0 + 16384, 45056 - 16384)

        def mkTB():
            return Tmp(WDBASE, TBSIZE)
        sc.op("dve", lambda e: e.memset(YT[:, 0:12, :], 0.0))
        if 'A' in MIX:
            mixer_hgrn(l, ti, mkTA(), mkTB())
        if 'B' in MIX:
            mixer_attn(l, ti, mkTA(), mkTB())
        if 'C' in MIX:
            mixer_ssd(l, ti, mkTA(), mkTB())
        mixer_lru(l, ti, mkTA(), mkTB())
        out_proj(l)

    ct = load_consts()
    sc.op("dve", lambda e: e.tensor_copy(out=ONESB_t[:], in_=CON[:, 128:256]), waits=[ct])
    sc.op("dve", lambda e: e.tensor_copy(out=IDB_t[:], in_=CON[:, 0:128]))
    attn_init()
    hgrn_init()
    convert_all()
    for l in range(DEPTH):
        layer_init(l)
        for ti in range(NT):
            load_x_tile(l, ti)
            ffn(l, "ffn1", 64 * l + 0)
            if dbg != "ffn1only":
                mixer(l, ti)
                ffn(l, "ffn2", 64 * l + 32)
            if l == DEPTH - 1:
                final_store(ti)
            else:
                store_x_tile(ti)

    semnames = sorted({tok[0] for (_, _, _, tok, _) in sc.ops})
    sems = {n: nc.alloc_semaphore(n) for n in semnames}
    with nc.Block() as block:
        def emit(engname):
            def body(e):
                for (eng, fn, waits, tok, inc) in sc.ops:
                    if eng != engname:
                        continue
                    for (sn, val) in waits:
                        e.wait_ge(sems[sn], val)
                    ins = fn(e)
                    ins.then_inc(sems[tok[0]], inc)
            return body
        block.tensor(emit("pe"))
        block.scalar(emit("act"))
        block.vector(emit("dve"))
        block.gpsimd(emit("pool"))
        block.sync(emit("sp"))
    return nc


def make_consts():
    c = np.zeros((128, 512), np.float32)
    c[:, 0:128] = np.eye(128, dtype=np.float32)
    c[:, 128:256] = 1.0
    i = np.arange(128)
    c[:, 256:384] = (i[:, None] <= i[None, :]).astype(np.float32)
    c[:, 384:512] = np.where(i[:, None] <= i[None, :], 0.0, -30000.0)
    return c


def make_pvec(inp, DEPTH):
    pv = np.zeros((128, 64 * DEPTH + 16), np.float32)
    for l in range(DEPTH):
        pv[:, 64 * l + 0:64 * l + 16] = inp["ffn1_norm"][l].reshape(16, 128).T
        pv[:, 64 * l + 16:64 * l + 32] = inp["mix_norm"][l].reshape(16, 128).T
        pv[:, 64 * l + 32:64 * l + 48] = inp["ffn2_norm"][l].reshape(16, 128).T
    pv[:, 64 * DEPTH:64 * DEPTH + 16] = inp["final_norm"].reshape(16, 128).T
    return pv


def make_amask():
    m = np.zeros((128, 17, 128), np.float32)
    i = np.arange(128)[:, None]
    j = np.arange(128)[None, :]
    for dl in range(17):
        dist = 128 * dl + j - i
        for (win, dil) in ((128, 1), (512, 4), (2048, 16)):
            m[:, dl, :] += ((dist >= 0) & (dist % dil == 0) & (dist // dil <= 128)).astype(np.float32)
    return m.reshape(128, 17 * 128)


def make_pvm(inp, DEPTH):
    pm = np.zeros((128, 128 * DEPTH), np.float32)
    lbd = np.zeros((DEPTH, 128, 8, 128), np.float32)
    for l in range(DEPTH):
        pc = 128 * l
        cw = inp["lru_conv_w"][l]
        for j in range(4):
            pm[:, pc + j * 4:pc + j * 4 + 4] = cw[:, j * 128:(j + 1) * 128].T
        pm[:, pc + 16:pc + 20] = inp["lru_conv_b"][l].reshape(4, 128).T
        pm[:, pc + 20:pc + 24] = inp["lru_b_a"][l].reshape(4, 128).T
        pm[:, pc + 24:pc + 28] = inp["lru_b_x"][l].reshape(4, 128).T
        pm[:, pc + 28:pc + 32] = inp["lru_a_param"][l].reshape(4, 128).T
        pm[:, pc + 32:pc + 36] = inp["lru_norm"][l].reshape(4, 128).T
        pm[:, pc + 104:pc + 108] = inp["hgrn_norm"][l].reshape(4, 128).T
        scw = inp["ssm_conv_w"][l]
        for j in range(8):
            pm[:, pc + 48 + j * 4:pc + 52 + j * 4] = scw[:, j * 128:(j + 1) * 128].T
        pm[:, pc + 80:pc + 88] = inp["ssm_conv_b"][l].reshape(8, 128).T
        pm[:, pc + 88:pc + 92] = inp["ssm_norm"][l].reshape(4, 128).T
        for wi, nm in enumerate(("lru_w_a", "lru_w_x")):
            w = inp[nm][l]
            for j in range(4):
                for hb in range(2):
                    lbd[l, hb * 64:(hb + 1) * 64, wi * 4 + j, hb * 64:(hb + 1) * 64] = w[2 * j + hb]
    return pm, lbd


def run(inp, S, DEPTH, B, dbg=None):
    nc = build(S, DEPTH, dbg)
    consts = make_consts()
    pv = make_pvec(inp, DEPTH)
    pm, lbd = make_pvm(inp, DEPTH)
    amask = make_amask()
    lbl = np.ascontiguousarray(inp["hgrn_lb_logits"][:DEPTH].reshape(DEPTH, 4, 128).transpose(2, 0, 1).reshape(128, DEPTH * 4)).astype(np.float32)
    ssmb = np.zeros((DEPTH, 128, 528), np.float32)
    for l in range(DEPTH):
        ssmb[l, :, 0:512] = np.repeat(inp["ssm_d"][l], 64)[None, :]
        ssmb[l, :, 512:520] = inp["ssm_dt_bias"][l][None, :]
        ssmb[l, :, 520:528] = inp["ssm_a_log"][l][None, :]
    anw = np.ascontiguousarray(np.broadcast_to(inp["attn_norm"][:DEPTH, None, :], (DEPTH, 128, 512))).astype(np.float32)
    in_maps = []
    for b in range(B):
        m = {"x": np.ascontiguousarray(inp["x"][b]), "consts": consts, "pvec": pv, "pvm": pm, "lbd": lbd, "amask": amask, "anw": anw, "ssmb": ssmb, "lbl": lbl}
        for nm in ("ffn1_w_gate", "ffn1_w_up", "ffn1_w_down", "ffn2_w_gate", "ffn2_w_up", "ffn2_w_down", "w_in", "w_out"):
            m[nm] = np.asarray(inp[nm])
        in_maps.append(m)
    res = run_bass_kernel_spmd(nc, in_maps, core_ids=list(range(B)))
    global LAST
    LAST = res.results
    return np.stack([res.results[b]["out"] for b in range(B)], 0)


def kernel(**inputs):
    inp = {k: np.asarray(v) for k, v in inputs.items()}
    return run(inp, 16384, 4, 2).astype(np.float32)
```

```python
import numpy as np
import concourse.bass as bass
import concourse.mybir as mybir
from concourse.bass_utils import run_bass_kernel_spmd

F32, BF16 = mybir.dt.float32, mybir.dt.bfloat16
AF = mybir.ActivationFunctionType
ALU = mybir.AluOpType

D = 2048
DFF = 5632
NIN = 6152
T = 512
KC = 16
FCH = 44
EPS = 1e-6
ENGS = ("pe", "act", "dve", "pool", "sp")


class Sched:
    def __init__(self):
        self.ops = []
        self.cnt = {e: 0 for e in ENGS}
        self.dcnt = {}
        self.last = None

    def op(self, eng, fn, waits=(), chain=True):
        w = [x for x in waits if x is not None]
        if chain and self.last is not None:
            w.append(self.last)
        self.cnt[eng] += 1
        tok = ("c_" + eng, self.cnt[eng])
        self.ops.append((eng, fn, w, tok, 1))
        if chain:
            self.last = tok
        return tok

    def dma(self, q, fn, dsem, waits=(), chain=True):
        w = [x for x in waits if x is not None]
        if chain and self.last is not None:
            w.append(self.last)
        self.dcnt[dsem] = self.dcnt.get(dsem, 0) + 16
        tok = (dsem, self.dcnt[dsem])
        self.ops.append((q, fn, w, tok, 16))
        return tok


MIX = 'ABCD'
DBG = False
DBGMAP = {}
LAST = None


def build(S, DEPTH, dbg=None):
    NT = S // T
    nc = bass.Bass("TRN2", target_bir_lowering=False)

    def din(name, shape, dt=F32):
        return nc.dram_tensor(name, list(shape), dt, kind="ExternalInput").ap()

    x_in = din("x", [S, D])
    out = nc.dram_tensor("out", [S, D], F32, kind="ExternalOutput").ap()
    dbgout = nc.dram_tensor("dbg", [128, 8192], F32, kind="ExternalOutput").ap() if DBG else None
    dbgpos = [0]
    dbgmap = {}
    wsrc = {}
    for nm, shp in (("ffn1_w_gate", [DEPTH, D, DFF]), ("ffn1_w_up", [DEPTH, D, DFF]),
                    ("ffn1_w_down", [DEPTH, DFF, D]), ("ffn2_w_gate", [DEPTH, D, DFF]),
                    ("ffn2_w_up", [DEPTH, D, DFF]), ("ffn2_w_down", [DEPTH, DFF, D]),
                    ("w_in", [DEPTH, D, NIN]), ("w_out", [DEPTH, D, D])):
        wsrc[nm] = din(nm, shp)
    consts = din("consts", [128, 512])
    pvec = din("pvec", [128, 64 * DEPTH + 16])
    pvm_in = din("pvm", [128, 128 * DEPTH])
    lbd_in = din("lbd", [DEPTH, 128, 8, 128])
    amask_in = din("amask", [128, 17 * 128])
    anw_in = din("anw", [DEPTH, 128, 512])
    ssmb_in = din("ssmb", [DEPTH, 128, 528])
    lbl_in = din("lbl", [128, DEPTH * 4])

    xs = nc.dram_tensor("xs", [KC, 128, S], F32).ap()
    wb = {}
    for l in range(DEPTH):
        for f in ("ffn1", "ffn2"):
            wb[(l, f, "g")] = nc.dram_tensor(f"wb_{l}_{f}_g", [22, 128, KC * 256], BF16).ap()
            wb[(l, f, "u")] = nc.dram_tensor(f"wb_{l}_{f}_u", [22, 128, KC * 256], BF16).ap()
            wb[(l, f, "d")] = nc.dram_tensor(f"wb_{l}_{f}_d", [16, 128, FCH * 128], BF16).ap()
        wb[(l, "in")] = nc.dram_tensor(f"wb_{l}_in", [24, 128, KC * 256], BF16).ap()
        wb[(l, "dt")] = nc.dram_tensor(f"wb_{l}_dt", [128, KC * 8], BF16).ap()
        wb[(l, "out")] = nc.dram_tensor(f"wb_{l}_out", [16, 128, KC * 128], BF16).ap()

    off = [16640]

    def sb(name, shape, dt, at=None):
        nbytes = int(np.prod(shape[1:])) * (4 if dt == F32 else 2)
        if at is None:
            o = off[0]
            off[0] += (nbytes + 63) // 64 * 64
        else:
            o = at
        return nc.alloc_sbuf_tensor_at(name, list(shape), dt, offset=o)

    XT = sb("XT", [128, KC, T], F32)
    HT = sb("HT", [128, KC, T], BF16)
    arena0 = off[0]
    AT = sb("AT", [128, FCH, T], BF16)
    WGU_OFF = off[0]
    WGU = [sb(f"WGU{i}", [128, KC, 256], BF16) for i in range(4)]
    WD = [nc.alloc_sbuf_tensor_at(f"WD{i}", [128, FCH, 128], BF16, offset=WGU_OFF + i * 16384) for i in range(2)]
    WDBASE = off[0]
    TBSIZE = 14336
    off[0] += TBSIZE
    CON = sb("CON", [128, 512], F32)
    PV = sb("PV", [128, 64 * DEPTH + 16], F32)
    RSTD = sb("RSTD", [128, T], F32)
    SG = sb("SG", [128, T], F32)
    assert off[0] <= 229344, off[0]
    STG = nc.alloc_sbuf_tensor_at("STG", [128, 4, D], F32, offset=arena0)
    SQ = nc.alloc_sbuf_tensor_at("SQ", [128, KC, T], BF16, offset=arena0)
    NCV = 3
    CVI = [nc.alloc_sbuf_tensor_at(f"CVI{i}", [128, 2048], F32, offset=arena0 + i * 8192) for i in range(NCV)]
    CVO = [nc.alloc_sbuf_tensor_at(f"CVO{i}", [128, 2048], BF16, offset=arena0 + NCV * 8192 + i * 4096) for i in range(NCV)]

    IDF = CON[:, 0:128]
    ONESB = None

    ps = [nc.alloc_psum_tensor(f"ps{i}", [128, 512], F32) for i in range(8)]

    sc = Sched()

    def load_consts():
        t1 = sc.dma("sp", lambda e: e.dma_start(out=CON[:], in_=consts[:, :]), "d_misc")
        t2 = sc.dma("sp", lambda e: e.dma_start(out=PV[:], in_=pvec[:, :]), "d_misc")
        t2 = sc.dma("sp", lambda e: e.dma_start(out=PVM[:], in_=pvm_in[:, :]), "d_misc")
        return t2

    ONESB_t = sb("ONESB", [128, 128], BF16)
    IDB_t = sb("IDB", [128, 128], BF16)

    def convert_all():
        blocks = []

        def conv_block(src_ap, dst_ap, ncols, nsl, sw):
            blocks.append((src_ap, dst_ap, ncols, sw))

        def run_blocks():
            store_tok = [None] * NCV
            ld = {}

            def issue_load(k):
                src_ap, _, ncols, _ = blocks[k]
                i = k % NCV
                ld[k] = sc.dma("sp", lambda e: e.dma_start(out=CVI[i][:, 0:ncols], in_=src_ap), f"d_cvi{i}")
            issue_load(0)
            if len(blocks) > 1:
                issue_load(1)
            for k in range(len(blocks)):
                if k + 2 < len(blocks):
                    issue_load(k + 2)
                _, dst_ap, ncols, sw = blocks[k]
                i = k % NCV
                eng = ("dve", "pool", "act")[k % 3]

                def cast(e, i=i, ncols=ncols, eng=eng):
                    if eng == "act":
                        return e.copy(out=CVO[i][:, 0:ncols], in_=CVI[i][:, 0:ncols])
                    return e.tensor_copy(out=CVO[i][:, 0:ncols], in_=CVI[i][:, 0:ncols])
                sc.op(eng, cast, waits=[ld[k], store_tok[i]])
                store_tok[i] = sc.dma(
                    "sp", lambda e, i=i, ncols=ncols, sw=sw, dst_ap=dst_ap: e.dma_start(
                        out=dst_ap, in_=CVO[i][:, 0:ncols].rearrange("p (s c) -> p s c", c=sw)), f"d_cvo{i}")
            sc.op("dve", lambda e: e.tensor_copy(out=SG[:, 0:8], in_=SG[:, 8:16]), waits=store_tok)

        for l in range(DEPTH):
            for f in ("ffn1", "ffn2"):
                for nm, key in ((f + "_w_gate", "g"), (f + "_w_up", "u")):
                    src = wsrc[nm]
                    dst = wb[(l, f, key)].rearrange("s p (k c) -> s p k c", c=256)
                    for kc in range(KC):
                        for cb in range(0, DFF, 2048):
                            ncols = min(2048, DFF - cb)
                            s0 = cb // 256
                            nsl = ncols // 256
                            conv_block(src[l, kc * 128:(kc + 1) * 128, cb:cb + ncols],
                                       dst[s0:s0 + nsl, :, kc, :].rearrange("s p c -> p s c"), ncols, nsl, 256)
                src = wsrc[f + "_w_down"]
                dst = wb[(l, f, "d")].rearrange("s p (k c) -> s p k c", c=128)
                for kc in range(FCH):
                    conv_block(src[l, kc * 128:(kc + 1) * 128, :],
                               dst[:, :, kc, :].rearrange("s p c -> p s c"), 2048, 16, 128)
            src = wsrc["w_in"]
            dst = wb[(l, "in")].rearrange("s p (k c) -> s p k c", c=256)
            for kc in range(KC):
                for (c0, d0, ncols) in ((0, 0, 2048), (2048, 2048, 2048), (4096, 4096, 1024), (5128, 5120, 1024)):
                    s0 = d0 // 256
                    nsl = ncols // 256
                    conv_block(src[l, kc * 128:(kc + 1) * 128, c0:c0 + ncols],
                               dst[s0:s0 + nsl, :, kc, :].rearrange("s p c -> p s c"), ncols, nsl, 256)
                dstd = wb[(l, "dt")].rearrange("p (k c) -> p k c", c=8)
                conv_block(src[l, kc * 128:(kc + 1) * 128, 5120:5128], dstd[:, kc:kc + 1, :], 8, 1, 8)
            src = wsrc["w_out"]
            dst = wb[(l, "out")].rearrange("s p (k c) -> s p k c", c=128)
            for kc in range(KC):
                conv_block(src[l, kc * 128:(kc + 1) * 128, :],
                           dst[:, :, kc, :].rearrange("s p c -> p s c"), 2048, 16, 128)
        run_blocks()

    def load_x_tile(l, ti):
        t0 = ti * T
        if l == 0:
            ld = sc.dma("sp", lambda e: e.dma_start(
                out=STG[:], in_=x_in[t0:t0 + T, :].rearrange("(c p) d -> p c d", p=128)), "d_x")
            first = True
            for kc in range(KC):
                def tr(e, kc=kc):
                    ins = None
                    for c in range(4):
                        ins = e.transpose(ps[kc % 2][:, c * 128:(c + 1) * 128], STG[:, c, kc * 128:(kc + 1) * 128], IDF)
                    return ins
                sc.op("pe", tr, waits=[ld] if first else [])
                first = False
                sc.op("act" if kc % 2 else "dve",
                      (lambda e, kc=kc: e.copy(out=XT[:, kc, :], in_=ps[kc % 2][:])) if kc % 2 else
                      (lambda e, kc=kc: e.tensor_copy(out=XT[:, kc, :], in_=ps[kc % 2][:])))
        else:
            ld = sc.dma("sp", lambda e: e.dma_start(
                out=XT[:], in_=xs[:, :, t0:t0 + T].rearrange("k p t -> p k t")), "d_x")
            sc.op("dve", lambda e: e.tensor_copy(out=SG[:, 0:8], in_=SG[:, 8:16]), waits=[ld])

    def store_x_tile(ti):
        t0 = ti * T
        st = sc.dma("sp", lambda e: e.dma_start(
            out=xs[:, :, t0:t0 + T].rearrange("k p t -> p k t"), in_=XT[:]), "d_xst")
        sc.op("dve", lambda e: e.tensor_copy(out=SG[:, 0:8], in_=SG[:, 8:16]), waits=[st])

    def rmsnorm_to(dst_fn, wcol):
        sc.op("act", lambda e: e.activation(out=SQ[:], in_=XT[:], func=AF.Square))

        def mm(e):
            ins = None
            for kc in range(KC):
                ins = e.matmul(ps[2][:], ONESB_t[:], SQ[:, kc, :], start=(kc == 0), stop=(kc == KC - 1))
            return ins
        sc.op("pe", mm)

        sc.op("dve", lambda e: e.tensor_scalar(out=RSTD[:], in0=ps[2][:], scalar1=1.0 / D, scalar2=EPS, op0=ALU.mult, op1=ALU.add))
        sc.op("act", lambda e: e.sqrt(out=RSTD[:], in_=RSTD[:]))
        sc.op("dve", lambda e: e.reciprocal(out=RSTD[:], in_=RSTD[:]))

        def hh(e):
            ins = None
            for kc in range(KC):
                ins = e.scalar_tensor_tensor(out=dst_fn(kc), in0=XT[:, kc, :], scalar=PV[:, wcol + kc:wcol + kc + 1],
                                             in1=RSTD[:], op0=ALU.mult, op1=ALU.mult)
            return ins
        sc.op("dve", hh)

    wgu_free = [None] * 4
    wd_free = [None] * 2

    def ffn(l, f, wcol):
        rmsnorm_to(lambda kc: HT[:, kc, :], wcol)
        wg, wu, wd = wb[(l, f, "g")], wb[(l, f, "u")], wb[(l, f, "d")]
        base = sc.last
        SGs = (SG, RSTD)
        banks = ((ps[0], ps[1]), (ps[4], ps[5]))
        pe_t, act_t, dve_t, loads = {}, {}, {}, {}

        def issue_gu(s_):
            b = (s_ % 2) * 2
            w = [pe_t[2 * (s_ - 2) + 1]] if s_ >= 2 else [base]
            tg = sc.dma("sp", lambda e: e.dma_start(out=WGU[b][:].rearrange("p k c -> p (k c)"), in_=wg[s_]), f"d_wgu{b}",
                        waits=w, chain=False)
            tu = sc.dma("sp", lambda e: e.dma_start(out=WGU[b + 1][:].rearrange("p k c -> p (k c)"), in_=wu[s_]), f"d_wgu{b + 1}",
                        waits=w, chain=False)
            loads[s_] = (tg, tu)
        issue_gu(0)
        issue_gu(1)
        for fc in range(FCH):
            s_, j = fc // 2, fc % 2
            if j == 0 and s_ >= 1 and s_ + 1 < 22:
                issue_gu(s_ + 1)
            b = (s_ % 2) * 2
            pg, pu = banks[fc % 2]

            def mm(e, b=b, j=j, pg=pg, pu=pu):
                ins = None
                for kc in range(KC):
                    ins = e.matmul(pg[:], WGU[b][:, kc, j * 128:(j + 1) * 128], HT[:, kc, :], start=(kc == 0), stop=(kc == KC - 1))
                for kc in range(KC):
                    ins = e.matmul(pu[:], WGU[b + 1][:, kc, j * 128:(j + 1) * 128], HT[:, kc, :], start=(kc == 0), stop=(kc == KC - 1))
                return ins
            w = list(loads[s_]) if j == 0 else []
            w.append(dve_t[fc - 2] if fc >= 2 else base)
            pe_t[fc] = sc.op("pe", mm, waits=w, chain=False)
            sgb = SGs[fc % 2]
            act_t[fc] = sc.op("act", lambda e, sgb=sgb, pg=pg: e.activation(out=sgb[:], in_=pg[:], func=AF.Silu),
                              waits=[pe_t[fc]] + ([dve_t[fc - 2]] if fc >= 2 else []), chain=False)
            dve_t[fc] = sc.op("dve", lambda e, fc=fc, sgb=sgb, pu=pu: e.tensor_tensor(out=AT[:, fc, :], in0=sgb[:], in1=pu[:], op=ALU.mult),
                              waits=[act_t[fc]], chain=False)
        sc.last = dve_t[FCH - 1]
        base2 = sc.last
        dbanks = (ps[3], ps[6])
        pd_t, dd_t, dl = {}, {}, {}

        def issue_d(oc):
            b = oc % 2
            w = [pd_t[oc - 2]] if oc >= 2 else [base2]
            dl[oc] = sc.dma("sp", lambda e: e.dma_start(out=WD[b][:].rearrange("p k c -> p (k c)"), in_=wd[oc]), f"d_wd{b}",
                            waits=w, chain=False)
        issue_d(0)
        issue_d(1)
        for oc in range(16):
            if oc >= 1 and oc + 1 < 16:
                issue_d(oc + 1)
            b = oc % 2
            pb_ = dbanks[oc % 2]

            def mm(e, b=b, pb_=pb_):
                ins = None
                for k in range(FCH):
                    ins = e.matmul(pb_[:], WD[b][:, k, :], AT[:, k, :], start=(k == 0), stop=(k == FCH - 1))
                return ins
            pd_t[oc] = sc.op("pe", mm, waits=[dl[oc], dd_t[oc - 2] if oc >= 2 else base2], chain=False)
            dd_t[oc] = sc.op("dve", lambda e, oc=oc, pb_=pb_: e.scalar_tensor_tensor(out=XT[:, oc, :], in0=pb_[:], scalar=0.5,
                                                                                   in1=XT[:, oc, :], op0=ALU.mult, op1=ALU.add),
                             waits=[pd_t[oc]], chain=False)
        sc.last = dd_t[15]

    def final_store(ti):
        t0 = ti * T
        FN = nc.alloc_sbuf_tensor_at(f"FN{ti}", [128, T], F32, offset=arena0 + 32768)
        sc.op("act", lambda e: e.activation(out=HT[:], in_=XT[:], func=AF.Square))

        def mm(e):
            ins = None
            for kc in range(KC):
                ins = e.matmul(ps[2][:], ONESB_t[:], HT[:, kc, :], start=(kc == 0), stop=(kc == KC - 1))
            return ins
        sc.op("pe", mm)

        sc.op("dve", lambda e: e.tensor_scalar(out=RSTD[:], in0=ps[2][:], scalar1=1.0 / D, scalar2=EPS, op0=ALU.mult, op1=ALU.add))
        sc.op("act", lambda e: e.sqrt(out=RSTD[:], in_=RSTD[:]))
        sc.op("dve", lambda e: e.reciprocal(out=RSTD[:], in_=RSTD[:]))
        wcol = 64 * DEPTH
        for kc in range(KC):
            sc.op("dve", lambda e, kc=kc: e.scalar_tensor_tensor(out=FN[:], in0=XT[:, kc, :], scalar=PV[:, wcol + kc:wcol + kc + 1],
                                                                 in1=RSTD[:], op0=ALU.mult, op1=ALU.mult))

            def tr(e, kc=kc):
                ins = None
                for c in range(4):
                    ins = e.transpose(ps[4][:, c * 128:(c + 1) * 128], FN[:, c * 128:(c + 1) * 128], IDF)
                return ins
            sc.op("pe", tr)
            sc.op("act", lambda e, kc=kc: e.copy(out=STG[:, :, kc * 128:(kc + 1) * 128],
                                                 in_=ps[4][:].rearrange("p (c t) -> p c t", c=4)))
        st = sc.dma("sp", lambda e: e.dma_start(out=out[t0:t0 + T, :].rearrange("(c p) d -> p c d", p=128), in_=STG[:]), "d_out")
        sc.op("dve", lambda e: e.tensor_copy(out=SG[:, 0:8], in_=SG[:, 8:16]), waits=[st])


    PVM = sb("PVM", [128, 128 * DEPTH], F32)
    LBD = sb("LBD", [128, 2, 4, 128], BF16)
    HALO_D = sb("HALO_D", [128, 4, 4], F32)
    HPREV = sb("HPREV", [128, 4], F32)
    assert off[0] <= 229344, off[0]
    wdbase = 16640 + 0
    uid = [0]

    class Tmp:
        def __init__(self, base, size):
            self.base, self.size, self.o, self.n = base, size, 0, 0
        def get(self, shape, dt):
            nbytes = int(np.prod(shape[1:])) * (4 if dt == F32 else 2)
            nbytes = (nbytes + 63) // 64 * 64
            assert self.o + nbytes <= self.size, (self.o, nbytes, self.size)
            self.n += 1
            uid[0] += 1
            t = nc.alloc_sbuf_tensor_at(f"tmp{uid[0]}", list(shape), dt, offset=self.base + self.o)
            self.o += nbytes
            return t
    YT = nc.alloc_sbuf_tensor_at("YT", [128, KC, T], BF16, offset=arena0)
    wd_off = [None]

    win_i = [0]

    def win_slab(l, s):
        b = win_i[0] % 4
        win_i[0] += 1
        tok = sc.dma("sp", lambda e: e.dma_start(out=WGU[b][:].rearrange("p k c -> p (k c)"), in_=wb[(l, "in")][s]), f"d_wgu{b}")
        return b, tok

    def proj_fm(b, j, pst, tok=None):
        def mm(e):
            ins = None
            for kc in range(KC):
                ins = e.matmul(pst[:], WGU[b][:, kc, j * 128:(j + 1) * 128], HT[:, kc, :], start=(kc == 0), stop=(kc == KC - 1))
            return ins
        sc.op("pe", mm, waits=[tok])

    def mixer_lru(l, ti, TA, TB):
        pc = 128 * l
        YD = TA.get([128, 4, T], F32)
        XB = TB.get([128, T + 4], F32)
        XC = TB.get([128, T], F32)
        XCB = TB.get([128, T], BF16)
        R = TB.get([128, T], F32)
        I = TB.get([128, T], F32)
        A = TB.get([128, T], F32)
        G = TB.get([128, T], F32)
        slabs = {}
        for j in range(4):
            sx = 20 + j // 2
            sg = 22 + j // 2
            if j % 2 == 0:
                slabs["x"] = win_slab(l, sx)
                slabs["g"] = win_slab(l, sg)
            bx, tx = slabs["x"]
            bg, tg = slabs["g"]
            proj_fm(bx, j % 2, ps[0], tx)
            sc.op("act", lambda e: e.copy(out=XB[:, 3:T + 3], in_=ps[0][:]))
            sc.op("dve", lambda e, j=j: e.tensor_copy(out=XB[:, 0:3], in_=HALO_D[:, j, 0:3]))
            sc.op("dve", lambda e, j=j: e.tensor_copy(out=HALO_D[:, j, 0:3], in_=XB[:, T:T + 3]))

            def conv(e, j=j):
                e.tensor_scalar(out=XC[:], in0=XB[:, 0:T], scalar1=PVM[:, pc + j * 4:pc + j * 4 + 1],
                                scalar2=PVM[:, pc + 16 + j:pc + 17 + j], op0=ALU.mult, op1=ALU.add)
                ins = None
                for tp in range(1, 4):
                    ins = e.scalar_tensor_tensor(out=XC[:], in0=XB[:, tp:tp + T], scalar=PVM[:, pc + j * 4 + tp:pc + j * 4 + tp + 1],
                                                 in1=XC[:], op0=ALU.mult, op1=ALU.add)
                return ins
            sc.op("dve", conv)
            sc.op("act", lambda e: e.copy(out=XCB[:], in_=XC[:]))
            sc.op("pe", lambda e, j=j: e.matmul(ps[1][:], LBD[:, 0, j, :], XCB[:], start=True, stop=True))
            sc.op("act", lambda e, j=j: e.activation(out=R[:], in_=ps[1][:], func=AF.Sigmoid, bias=PVM[:, pc + 20 + j:pc + 21 + j]))
            sc.op("pe", lambda e, j=j: e.matmul(ps[1][:], LBD[:, 1, j, :], XCB[:], start=True, stop=True))
            sc.op("act", lambda e, j=j: e.activation(out=I[:], in_=ps[1][:], func=AF.Sigmoid, bias=PVM[:, pc + 24 + j:pc + 25 + j]))
            sc.op("act", lambda e, j=j: e.activation(out=A[:], in_=R[:], func=AF.Exp, scale=PVM[:, pc + 40 + j:pc + 41 + j]))

            def bt(e):
                e.tensor_tensor(out=R[:], in0=A[:], in1=A[:], op=ALU.mult)
                e.tensor_scalar(out=R[:], in0=R[:], scalar1=-1.0, scalar2=1.0, op0=ALU.mult, op1=ALU.add)
                return e.tensor_scalar(out=R[:], in0=R[:], scalar1=0.0, scalar2=None, op0=ALU.max)
            sc.op("dve", bt)
            sc.op("act", lambda e: e.sqrt(out=R[:], in_=R[:]))

            def bt2(e):
                e.tensor_tensor(out=I[:], in0=I[:], in1=XC[:], op=ALU.mult)
                return e.tensor_tensor(out=I[:], in0=I[:], in1=R[:], op=ALU.mult)
            sc.op("dve", bt2)
            sc.op("dve", lambda e, j=j: e.tensor_tensor_scan(out=XC[:], data0=A[:], data1=I[:], initial=HPREV[:, j:j + 1],
                                                              op0=ALU.mult, op1=ALU.add))
            sc.op("dve", lambda e, j=j: e.tensor_copy(out=HPREV[:, j:j + 1], in_=XC[:, T - 1:T]))
            proj_fm(bg, j % 2, ps[0], tg)
            sc.op("act", lambda e: e.activation(out=G[:], in_=ps[0][:], func=AF.Gelu))
            sc.op("dve", lambda e, j=j: e.tensor_tensor(out=YD[:, j, :], in0=XC[:], in1=G[:], op=ALU.mult))
        group_norm_to_yt(YD, [0, 1, 2, 3], 12, pc + 32, TA)

    def group_norm_to_yt(YD, chunks, ybase, wcol, TB):
        n = len(chunks)
        SQm = TB.get([128, n, T], BF16)
        RS = TB.get([128, T], F32)
        sc.op("act", lambda e: e.activation(out=SQm[:], in_=YD[:, chunks[0]:chunks[0] + n, :], func=AF.Square))

        def mm(e):
            ins = None
            for i in range(n):
                ins = e.matmul(ps[2][:], ONESB_t[:], SQm[:, i, :], start=(i == 0), stop=(i == n - 1))
            return ins
        sc.op("pe", mm)
        sc.op("dve", lambda e: e.tensor_scalar(out=RS[:], in0=ps[2][:], scalar1=1.0 / (128 * n), scalar2=EPS, op0=ALU.mult, op1=ALU.add))
        sc.op("act", lambda e: e.sqrt(out=RS[:], in_=RS[:]))
        sc.op("dve", lambda e: e.reciprocal(out=RS[:], in_=RS[:]))

        def hh(e):
            ins = None
            for i, c in enumerate(chunks):
                ins = e.scalar_tensor_tensor(out=YT[:, ybase + i, :], in0=YD[:, c, :], scalar=PVM[:, wcol + i:wcol + i + 1],
                                             in1=RS[:], op0=ALU.mult, op1=ALU.mult)
            return ins
        sc.op("dve", hh)


    MASK = sb("MASK", [128, 17, 128], BF16)
    KTH = sb("KTH", [128, 4, 2560], BF16)
    VH = sb("VH", [128, 20, 8, 72], BF16)
    WBA = sb("WBA", [128, 512], F32)
    assert off[0] <= 229344, off[0]

    def attn_init():
        ld = sc.dma("sp", lambda e: e.dma_start(out=STG[:, 0, :].rearrange("p (a b) -> p a b", b=128)[:, 0:17, :] if False else XT[:, 0:5, :].rearrange("p a b -> p (a b)")[:, 0:17 * 128], in_=amask_in[:, :]), "d_misc")
        sc.op("dve", lambda e: e.tensor_copy(out=MASK[:].rearrange("p a b -> p (a b)"), in_=XT[:, 0:5, :].rearrange("p a b -> p (a b)")[:, 0:17 * 128]), waits=[ld])
        sc.op("dve", lambda e: e.memset(VH[:], 1.0))

    def dump(name, ap, ncols):
        if not DBG:
            return
        DSTG = nc.alloc_sbuf_tensor_at(f"DSTG{dbgpos[0]}", [128, 2048], F32, offset=arena0 + 16384)
        c0 = dbgpos[0]
        dbgpos[0] += ncols
        DBGMAP[name] = (c0, ncols)
        sc.op("dve", lambda e: e.tensor_copy(out=DSTG[:, 0:ncols], in_=ap))
        st = sc.dma("sp", lambda e: e.dma_start(out=dbgout[:, c0:c0 + ncols], in_=DSTG[:, 0:ncols]), "d_dbg")
        sc.op("dve", lambda e: e.tensor_copy(out=SG[:, 0:8], in_=SG[:, 8:16]), waits=[st])

    def mixer_attn(l, ti, TA, TB):
        pc = 128 * l
        QZ = TA.get([128, 8, T], BF16)
        E = TB.get([128, 512], BF16)
        P = TB.get([128, 4, 128], BF16)
        E2 = TB.get([128, 512], BF16)
        P2 = TB.get([128, 4, 128], BF16)
        OT = TB.get([128, 512], F32)
        JK = TB.get([128, 512], F32)
        YM = TB.get([128, 512], F32)
        SS = TB.get([128, 16], F32)
        bslot = (ti * 4) % 20
        NB = (1, 2, 6, 7)
        for j in range(4):
            if j % 2 == 0:
                sq_ = win_slab(l, 8 + j // 2)
                sk_ = win_slab(l, 10 + j // 2)
            proj_fm(sq_[0], j % 2, ps[0], sq_[1])
            if j == 0:
                sc.op("dve", lambda e: e.memset(QZ[:], 0.0))
            sc.op("act", lambda e, j=j: e.mul(out=QZ[0:64, 2 * j, :], in_=ps[0][0:64, :], mul=0.125))
            sc.op("act", lambda e, j=j: e.mul(out=QZ[64:128, 2 * j + 1, :], in_=ps[0][64:128, :], mul=0.125))
            proj_fm(sk_[0], j % 2, ps[0], sk_[1])
            sc.op("act", lambda e, j=j: e.copy(out=KTH[:, j, bslot * 128:bslot * 128 + T], in_=ps[0][:]))
        v0 = win_slab(l, 12)
        v1 = win_slab(l, 13)
        for c in range(4):
            def mm(e, c=c):
                ins = None
                for half, (b, _) in enumerate((v0, v1)):
                    for kc in range(KC):
                        ins = e.matmul(ps[0][:, half * 256:(half + 1) * 256], HT[:, kc, c * 128:(c + 1) * 128], WGU[b][:, kc, :],
                                       start=(kc == 0), stop=(kc == KC - 1))
                return ins
            sc.op("pe", mm, waits=[v0[1], v1[1]] if c == 0 else [])
            sc.op("act", lambda e, c=c: e.copy(out=VH[:, bslot + c, :, 0:64], in_=ps[0][:].rearrange("p (h d) -> p h d", d=64)))
        import os
        STAGE = int(os.environ.get('ATT_STAGE', '9'))
        for c in range(4):
            g = ti * 4 + c
            nd = min(16, g) + 1
            for quad in range(2):
                base = sc.last
                sbank = (ps[4], ps[5])
                Eb = (E, E2)
                Pb = (P, P2)
                smm_t, act_t, dve_t, nmm_t = {}, {}, {}, {}

                def emit_smm(u, c=c, quad=quad, g=g):
                    ks = (g - u) % 20
                    bank = sbank[u % 2]

                    def smm(e):
                        ins = None
                        for hh in range(4):
                            h = quad * 4 + hh
                            ins = e.matmul(bank[:, hh * 128:(hh + 1) * 128], KTH[:, h // 2, ks * 128:(ks + 1) * 128],
                                           QZ[:, h, c * 128:(c + 1) * 128], start=True, stop=True)
                        return ins
                    smm_t[u] = sc.op("pe", smm, waits=[act_t[u - 2] if u >= 2 else base], chain=False)
                emit_smm(0)
                for dl in range(nd):
                    if dl + 1 < nd:
                        emit_smm(dl + 1)
                    ks = (g - dl) % 20
                    bank, Eu, Pu = sbank[dl % 2], Eb[dl % 2], Pb[dl % 2]
                    act_t[dl] = sc.op("act", lambda e, bank=bank, Eu=Eu: e.activation(out=Eu[:], in_=bank[:], func=AF.Exp),
                                      waits=[smm_t[dl]] + ([dve_t[dl - 2]] if dl >= 2 else []), chain=False)

                    def msk(e, dl=dl, Eu=Eu, Pu=Pu):
                        ins = None
                        for hh in range(4):
                            ins = e.tensor_tensor(out=Pu[:, hh, :], in0=Eu[:, hh * 128:(hh + 1) * 128], in1=MASK[:, dl, :], op=ALU.mult)
                        return ins
                    dve_t[dl] = sc.op("dve", msk, waits=[act_t[dl]] + ([nmm_t[dl - 2]] if dl >= 2 else []), chain=False)

                    def nmm(e, quad=quad, ks=ks, dl=dl, nd=nd, Pu=Pu):
                        ins = None
                        for hh in range(4):
                            h = quad * 4 + hh
                            ins = e.matmul(ps[NB[hh]][:, 0:65], Pu[:, hh, :], VH[:, ks, h, 0:65],
                                           start=(dl == 0), stop=(dl == nd - 1))
                        return ins
                    nmm_t[dl] = sc.op("pe", nmm, waits=[dve_t[dl]], chain=False)
                sc.last = nmm_t[nd - 1]

                def rcp(e):
                    ins = None
                    for hh in range(4):
                        ins = e.reciprocal(out=SS[:, hh:hh + 1], in_=ps[NB[hh]][:, 64:65])
                    return ins
                if STAGE >= 4:
                    sc.op("dve", rcp)

                def fin(e, quad=quad):
                    ins = None
                    for hh in range(4):
                        h = quad * 4 + hh
                        ins = e.tensor_scalar(out=OT[:, h * 64:(h + 1) * 64], in0=ps[NB[hh]][:, 0:64], scalar1=SS[:, hh:hh + 1],
                                              scalar2=None, op0=ALU.mult)
                    return ins
                if STAGE >= 4:
                    sc.op("dve", fin)
            if STAGE < 5:
                continue
            if ti == 0 and c == 0 and l == 0:
                dump("QZ0", QZ[:, 0, 0:128], 128)
                dump("QZ1", QZ[:, 1, 0:128], 128)
                dump("K0", KTH[:, 0, 0:128], 128)
                dump("V0", VH[:, 0, 0, :], 72)
                dump("V1", VH[:, 0, 1, :], 72)
                dump("E", E[:], 512)
                dump("P", P[:].rearrange("p a b -> p (a b)"), 512)
                dump("OT", OT[:], 512)
                dump("MASK0", MASK[:, 0, :], 128)
                dump("HT", HT[:, :, 0:128], 2048)
            sc.op("act", lambda e: e.activation(out=JK[:], in_=OT[:], func=AF.Square))
            sc.op("dve", lambda e: e.reduce_sum(out=SS[:, 8:9], in_=JK[:], axis=mybir.AxisListType.X))
            sc.op("dve", lambda e: e.tensor_scalar(out=SS[:, 8:9], in0=SS[:, 8:9], scalar1=1.0 / 512, scalar2=EPS, op0=ALU.mult, op1=ALU.add))
            sc.op("act", lambda e: e.sqrt(out=SS[:, 8:9], in_=SS[:, 8:9]))
            sc.op("dve", lambda e: e.reciprocal(out=SS[:, 8:9], in_=SS[:, 8:9]))
            sc.op("dve", lambda e: e.scalar_tensor_tensor(out=YM[:], in0=OT[:], scalar=SS[:, 8:9], in1=WBA[:], op0=ALU.mult, op1=ALU.mult))

            def tr(e):
                ins = None
                for j in range(4):
                    ins = e.transpose(ps[5][:, j * 128:(j + 1) * 128], YM[:, j * 128:(j + 1) * 128], IDF)
                return ins
            sc.op("pe", tr)
            sc.op("act", lambda e, c=c: e.copy(out=YT[:, 4:8, c * 128:(c + 1) * 128], in_=ps[5][:].rearrange("p (j t) -> p j t", t=128)))

    SSMB = sb("SSMB", [128, 528], F32)
    AB = sb("AB", [128, 8], F32)
    WDT = sb("WDT", [128, KC, 8], BF16)
    ST_C = sb("ST_C", [128, 512], F32)
    STB_C = sb("STB_C", [128, 512], BF16)
    HALO_C = sb("HALO_C", [128, 8, 4], F32)
    assert off[0] <= 229344, off[0]
    ONESF = CON[:, 128:256]
    TRI = CON[:, 256:384]
    NEGM = CON[:, 384:512]

    def ssd_layer_init(l):
        ld = sc.dma("sp", lambda e: e.dma_start(out=SSMB[:], in_=ssmb_in[l]), "d_misc")
        ld2 = sc.dma("sp", lambda e: e.dma_start(out=WDT[:].rearrange("p k c -> p (k c)"), in_=wb[(l, "dt")]), "d_misc")
        sc.op("act", lambda e: e.activation(out=AB[:], in_=SSMB[:, 520:528], func=AF.Exp), waits=[ld, ld2])
        sc.op("dve", lambda e: e.tensor_scalar(out=AB[:], in0=AB[:], scalar1=-1.0, scalar2=None, op0=ALU.mult))
        sc.op("dve", lambda e: e.memset(ST_C[:], 0.0))
        sc.op("dve", lambda e: e.memset(STB_C[:], 0.0))
        sc.op("dve", lambda e: e.memset(HALO_C[:], 0.0))

    def mixer_ssd(l, ti, TA, TB):
        pc = 128 * l
        XB = TB.get([128, T + 4], F32)
        XC = TB.get([128, T], F32)
        XF = TA.get([128, 6, T], F32)
        BCT = TA.get([128, 4, T], BF16)
        YC = TA.get([128, 4, T], F32)
        XTM = TB.get([128, 512], F32)
        XDT = TB.get([128, 512], BF16)
        XDS = TA.get([128, 512], BF16)
        BTM = TB.get([128, 2, 128], BF16)
        SM = TB.get([128, 64], F32)
        LM = TB.get([128, 4, 128], F32)
        WW = TB.get([128, 8, 128], BF16)
        YTM = TB.get([128, 512], F32)
        TMP = TA.get([128, 512], F32)
        ADTB = TMP[:].rearrange("p (a b) -> p a b", b=128)
        for j in range(8):
            if j % 2 == 0:
                sl = win_slab(l, 16 + j // 2)
            proj_fm(sl[0], j % 2, ps[0], sl[1])
            sc.op("act", lambda e: e.copy(out=XB[:, 3:T + 3], in_=ps[0][:]))
            sc.op("dve", lambda e, j=j: e.tensor_copy(out=XB[:, 0:3], in_=HALO_C[:, j, 0:3]))
            sc.op("dve", lambda e, j=j: e.tensor_copy(out=HALO_C[:, j, 0:3], in_=XB[:, T:T + 3]))

            def conv(e, j=j):
                e.tensor_scalar(out=XC[:], in0=XB[:, 0:T], scalar1=PVM[:, pc + 48 + j * 4:pc + 49 + j * 4],
                                scalar2=PVM[:, pc + 80 + j:pc + 81 + j], op0=ALU.mult, op1=ALU.add)
                ins = None
                for tp in range(1, 4):
                    ins = e.scalar_tensor_tensor(out=XC[:], in0=XB[:, tp:tp + T], scalar=PVM[:, pc + 48 + j * 4 + tp:pc + 49 + j * 4 + tp],
                                                 in1=XC[:], op0=ALU.mult, op1=ALU.add)
                return ins
            sc.op("dve", conv)
            if j < 6:
                sc.op("act", lambda e, j=j: e.activation(out=XF[:, j, :], in_=XC[:], func=AF.Silu))
                if j >= 4:
                    sc.op("dve", lambda e, j=j: e.tensor_copy(out=BCT[:, j - 4, :], in_=XF[:, j, :]))
            else:
                sc.op("act", lambda e, j=j: e.activation(out=BCT[:, j - 4, :], in_=XC[:], func=AF.Silu))
        for c in range(4):
            cs = slice(c * 128, (c + 1) * 128)
            def trx(e, cs=cs):
                ins = None
                for j in range(4):
                    ins = e.transpose(ps[7][:, j * 128:(j + 1) * 128], XF[:, j, cs], IDF)
                return ins
            sc.op("pe", trx)
            sc.op("act", lambda e: e.copy(out=XTM[:], in_=ps[7][:]))

            def trb(e, cs=cs):
                ins = None
                for g in range(2):
                    ins = e.transpose(ps[7][:, g * 128:(g + 1) * 128], XF[:, 4 + g, cs], IDF)
                return ins
            sc.op("pe", trb)
            sc.op("act", lambda e: e.copy(out=BTM[:].rearrange("p g n -> p (g n)"), in_=ps[7][:, 0:256]))
            def dtmm(e, cs=cs):
                ins = None
                for kc in range(KC):
                    ins = e.matmul(ps[1][:, 0:8], HT[:, kc, cs], WDT[:, kc, :], start=(kc == 0), stop=(kc == KC - 1))
                return ins
            sc.op("pe", dtmm)
            sc.op("dve", lambda e: e.tensor_tensor(out=SM[:, 0:8], in0=ps[1][:, 0:8], in1=SSMB[:, 512:520], op=ALU.add))
            sc.op("act", lambda e: e.activation(out=SM[:, 0:8], in_=SM[:, 0:8], func=AF.Exp))
            sc.op("act", lambda e: e.activation(out=SM[:, 0:8], in_=SM[:, 0:8], func=AF.Ln, bias=1.0))
            sc.op("dve", lambda e: e.tensor_tensor(out=SM[:, 8:16], in0=SM[:, 0:8], in1=AB[:], op=ALU.mult))
            sc.op("pe", lambda e: e.matmul(ps[1][:, 16:24], TRI, SM[:, 8:16], start=True, stop=True))
            sc.op("dve", lambda e: e.tensor_copy(out=SM[:, 16:24], in_=ps[1][:, 16:24]))
            sc.op("act", lambda e: e.activation(out=SM[:, 24:32], in_=SM[:, 16:24], func=AF.Exp))
            sc.op("dve", lambda e: e.tensor_scalar(out=SM[:, 32:40], in0=SM[:, 16:24], scalar1=-1.0, scalar2=None, op0=ALU.mult))
            def xdt(e):
                ins = None
                for h in range(8):
                    ins = e.tensor_scalar(out=XDT[:, h * 64:(h + 1) * 64], in0=XTM[:, h * 64:(h + 1) * 64], scalar1=SM[:, h:h + 1],
                                          scalar2=None, op0=ALU.mult)
                return ins
            sc.op("dve", xdt)
            for g in range(2):
                def adtb(e, g=g):
                    ins = None
                    for hh in range(4):
                        h = g * 4 + hh
                        ins = e.tensor_scalar(out=ADTB[:, hh, :], in0=ONESF, scalar1=SM[:, 8 + h:9 + h], scalar2=None, op0=ALU.mult)
                    return ins
                sc.op("dve", adtb)

                def bc(e):
                    ins = None
                    for hh in range(4):
                        e.matmul(ps[2][:, hh * 128:(hh + 1) * 128], ADTB[:, hh, :], TRI, start=True, stop=False)
                        ins = e.matmul(ps[2][:, hh * 128:(hh + 1) * 128], IDF, NEGM, start=False, stop=True)
                    return ins
                sc.op("pe", bc)

                def lm(e, g=g):
                    ins = None
                    for hh in range(4):
                        h = g * 4 + hh
                        e.activation(out=LM[:, hh, :], in_=ps[2][:, hh * 128:(hh + 1) * 128], func=AF.Exp, bias=SM[:, 32 + h:33 + h])
                        ins = e.activation(out=SM[:, 40 + h:41 + h], in_=ps[2][:, hh * 128 + 127:hh * 128 + 128], func=AF.Exp)
                    return ins
                sc.op("act", lm)
                sc.op("pe", lambda e, g=g, cs=cs: e.matmul(ps[3][:, 0:128], BCT[:, g, cs], BCT[:, 2 + g, cs], start=True, stop=True))

                def wmul(e, g=g):
                    ins = None
                    for hh in range(4):
                        ins = e.tensor_tensor(out=WW[:, g * 4 + hh, :], in0=ps[3][:, 0:128], in1=LM[:, hh, :], op=ALU.mult)
                    return ins
                sc.op("dve", wmul)

                def xds(e, g=g):
                    ins = None
                    for hh in range(4):
                        h = g * 4 + hh
                        ins = e.tensor_scalar(out=XDS[:, h * 64:(h + 1) * 64], in0=XDT[:, h * 64:(h + 1) * 64], scalar1=LM[:, hh, 127:128],
                                              scalar2=None, op0=ALU.mult)
                    return ins
                sc.op("dve", xds)

            def ydiag(e):
                ins = None
                for h in range(8):
                    ins = e.matmul(ps[4][:, h * 64:(h + 1) * 64], WW[:, h, :], XDT[:, h * 64:(h + 1) * 64], start=True, stop=True)
                return ins
            sc.op("pe", ydiag)

            def yoff(e, cs=cs):
                ins = None
                for h in range(8):
                    ins = e.matmul(ps[5][:, h * 64:(h + 1) * 64], BCT[:, 2 + h // 4, cs], STB_C[:, h * 64:(h + 1) * 64], start=True, stop=True)
                return ins
            sc.op("pe", yoff)
            sc.op("act", lambda e: e.copy(out=YTM[:], in_=ps[4][:]))

            def yadd(e):
                ins = None
                for h in range(8):
                    ins = e.scalar_tensor_tensor(out=YTM[:, h * 64:(h + 1) * 64], in0=ps[5][:, h * 64:(h + 1) * 64], scalar=SM[:, 24 + h:25 + h],
                                                 in1=YTM[:, h * 64:(h + 1) * 64], op0=ALU.mult, op1=ALU.add)
                return ins
            sc.op("dve", yadd)
            sc.op("dve", lambda e: e.tensor_tensor(out=TMP[:], in0=XTM[:], in1=SSMB[:, 0:512], op=ALU.mult))
            sc.op("dve", lambda e: e.tensor_tensor(out=YTM[:], in0=YTM[:], in1=TMP[:], op=ALU.add))
            def smm(e):
                ins = None
                for g in range(2):
                    ins = e.matmul(ps[6][:, g * 256:(g + 1) * 256], BTM[:, g, :], XDS[:, g * 256:(g + 1) * 256], start=True, stop=True)
                return ins
            sc.op("pe", smm)

            def stu(e):
                ins = None
                for h in range(8):
                    ins = e.scalar_tensor_tensor(out=ST_C[:, h * 64:(h + 1) * 64], in0=ST_C[:, h * 64:(h + 1) * 64], scalar=SM[:, 40 + h:41 + h],
                                                 in1=ps[6][:, h * 64:(h + 1) * 64], op0=ALU.mult, op1=ALU.add)
                return ins
            sc.op("dve", stu)
            sc.op("act", lambda e: e.copy(out=STB_C[:], in_=ST_C[:]))
            def tr(e):
                ins = None
                for j in range(4):
                    ins = e.transpose(ps[7][:, j * 128:(j + 1) * 128], YTM[:, j * 128:(j + 1) * 128], IDF)
                return ins
            sc.op("pe", tr)
            sc.op("act", lambda e, cs=cs: e.copy(out=YC[:, :, cs], in_=ps[7][:].rearrange("p (j t) -> p j t", t=128)))
        for j in range(4):
            if j % 2 == 0:
                sl = win_slab(l, 14 + j // 2)
            proj_fm(sl[0], j % 2, ps[0], sl[1])
            sc.op("act", lambda e: e.activation(out=TMP[:], in_=ps[0][:], func=AF.Silu))
            sc.op("dve", lambda e, j=j: e.tensor_tensor(out=YC[:, j, :], in0=YC[:, j, :], in1=TMP[:], op=ALU.mult))
        TB2 = Tmp(WDBASE, TBSIZE)
        group_norm_to_yt(YC, [0, 1], 8, pc + 88, TB2)
        group_norm_to_yt(YC, [2, 3], 10, pc + 90, Tmp(WDBASE, TBSIZE))

    S_A = sb("S_A", [128, 4, 128], F32)
    LBL = sb("LBL", [128, DEPTH, 4], F32)
    LB = sb("LB", [128, DEPTH, 4], F32)
    OML = sb("OML", [128, DEPTH, 4], F32)
    LBT = sb("LBT", [128, 16], F32)
    assert off[0] <= 229344, off[0]

    def hgrn_init():
        ld = sc.dma("sp", lambda e: e.dma_start(out=LBL[:].rearrange("p a b -> p (a b)"), in_=lbl_in[:, :]), "d_misc")
        sc.op("dve", lambda e: e.tensor_copy(out=LBT[:, 0:4], in_=LBL[:, 0, :]), waits=[ld])
        for l in range(1, DEPTH):
            sc.op("dve", lambda e, l=l: e.tensor_tensor(out=LBT[:, 0:4], in0=LBT[:, 0:4], in1=LBL[:, l, :], op=ALU.max))
        for l in range(DEPTH):
            sc.op("dve", lambda e, l=l: e.tensor_tensor(out=LBL[:, l, :], in0=LBL[:, l, :], in1=LBT[:, 0:4], op=ALU.subtract))
        sc.op("act", lambda e: e.activation(out=LBL[:], in_=LBL[:], func=AF.Exp))
        sc.op("dve", lambda e: e.tensor_copy(out=LBT[:, 4:8], in_=LBL[:, 0, :]))
        for l in range(1, DEPTH):
            sc.op("dve", lambda e, l=l: e.tensor_tensor(out=LBT[:, 4:8], in0=LBT[:, 4:8], in1=LBL[:, l, :], op=ALU.add))
        sc.op("dve", lambda e: e.reciprocal(out=LBT[:, 8:12], in_=LBT[:, 4:8]))
        for l in range(DEPTH):
            sc.op("dve", lambda e, l=l: e.tensor_tensor(out=LBL[:, l, :], in0=LBL[:, l, :], in1=LBT[:, 8:12], op=ALU.mult))
        sc.op("dve", lambda e: e.memset(LB[:, 0, :], 0.0))
        for l in range(1, DEPTH):
            sc.op("dve", lambda e, l=l: e.tensor_tensor(out=LB[:, l, :], in0=LB[:, l - 1, :], in1=LBL[:, l, :], op=ALU.add))
        sc.op("dve", lambda e: e.tensor_scalar(out=OML[:], in0=LB[:], scalar1=-1.0, scalar2=1.0, op0=ALU.mult, op1=ALU.add))

    def mixer_hgrn(l, ti, TA, TB):
        pc = 128 * l
        O = TA.get([128, 4, T], F32)
        KTb = TA.get([128, 4, T], BF16)
        KTM = TA.get([128, 4, 8, 128], BF16)
        VTM = TA.get([128, 8, 512], BF16)
        QTb = YT
        t1 = TB.get([128, T], F32)
        t2 = TB.get([128, T], F32)
        t3 = TB.get([128, T], F32)
        t4 = TB.get([128, T], F32)
        t5 = TB.get([128, T], F32)
        EB = TB.get([128, 3, 4, 8], F32)
        TMPU = TB.get([128, 4, 128], F32)
        PM = TB.get([128, 4, 64], BF16)
        SMID = TB.get([128, 4, 128], BF16)
        for h in range(4):
            sl = win_slab(l, h // 2) if h % 2 == 0 else sl_q
            sl_q = sl
            proj_fm(sl[0], h % 2, ps[0], sl[1])
            sc.op("act", lambda e: e.activation(out=t1[:], in_=ps[0][:], func=AF.Silu))
            slf = win_slab(l, 2 + h // 2) if h % 2 == 0 else sl_f
            sl_f = slf
            proj_fm(slf[0], h % 2, ps[0], slf[1])
            sc.op("act", lambda e: e.activation(out=t2[:], in_=ps[0][:], func=AF.Sigmoid))
            sc.op("dve", lambda e, h=h: e.tensor_scalar(out=t2[:], in0=t2[:], scalar1=OML[:, l, h:h + 1], scalar2=LB[:, l, h:h + 1],
                                                        op0=ALU.mult, op1=ALU.add))
            sc.op("act", lambda e: e.activation(out=t3[:], in_=t2[:], func=AF.Ln))
            sc.op("dve", lambda e: e.tensor_scalar(out=t2[:], in0=t2[:], scalar1=-1.0, scalar2=1.0, op0=ALU.mult, op1=ALU.add))

            def scan(e):
                ins = None
                for c in range(8):
                    ins = e.tensor_tensor_scan(out=t4[:, c * 64:(c + 1) * 64], data0=ONESF[:, 0:64], data1=t3[:, c * 64:(c + 1) * 64],
                                               initial=0.0, op0=ALU.mult, op1=ALU.add)
                return ins
            sc.op("dve", scan)

            def dsub(e):
                ins = None
                for c in range(8):
                    ins = e.tensor_scalar(out=t3[:, c * 64:(c + 1) * 64], in0=t4[:, c * 64:(c + 1) * 64],
                                          scalar1=t4[:, c * 64 + 31:c * 64 + 32], scalar2=None, op0=ALU.subtract)
                return ins
            sc.op("dve", dsub)
            b3 = t4[:].rearrange("p (c t) -> p c t", t=64)
            sc.op("act", lambda e, h=h, b3=b3: e.activation(out=EB[:, 0, h, :], in_=b3[:, :, 31], func=AF.Exp))
            sc.op("act", lambda e, h=h, b3=b3: e.activation(out=EB[:, 1, h, :], in_=b3[:, :, 63], func=AF.Exp))
            sc.op("dve", lambda e, h=h, b3=b3: e.tensor_tensor(out=EB[:, 2, h, :], in0=b3[:, :, 63], in1=b3[:, :, 31], op=ALU.subtract))
            sc.op("act", lambda e, h=h: e.activation(out=EB[:, 2, h, :], in_=EB[:, 2, h, :], func=AF.Exp))
            sc.op("act", lambda e: e.activation(out=t5[:], in_=t3[:], func=AF.Exp))
            sc.op("dve", lambda e, h=h: e.tensor_tensor(out=QTb[:, h, :], in0=t1[:], in1=t5[:], op=ALU.mult))
            sc.op("act", lambda e: e.activation(out=t5[:], in_=t3[:], func=AF.Exp, scale=-1.0))
            sc.op("dve", lambda e: e.tensor_tensor(out=t1[:], in0=t2[:], in1=t5[:], op=ALU.mult))
            sc.op("act", lambda e, h=h: e.copy(out=KTb[:, h, :], in_=t1[:]))
            for rnd in range(2):
                def trk(e, rnd=rnd):
                    ins = None
                    for cc in range(4):
                        c = rnd * 4 + cc
                        ins = e.transpose(ps[7][0:64, cc * 128:(cc + 1) * 128], t1[:, c * 64:(c + 1) * 64], IDF)
                    return ins
                sc.op("pe", trk)
                sc.op("act", lambda e, h=h, rnd=rnd: e.copy(out=KTM[0:64, h, rnd * 4:rnd * 4 + 4, :].rearrange("p a b -> p (a b)"),
                                                             in_=ps[7][0:64, :]))
        v0 = win_slab(l, 4)
        v1 = win_slab(l, 5)
        for c in range(8):
            def mm(e, c=c):
                ins = None
                for half, (b, _) in enumerate((v0, v1)):
                    for kc in range(KC):
                        ins = e.matmul(ps[0][0:64, half * 256:(half + 1) * 256], HT[:, kc, c * 64:(c + 1) * 64], WGU[b][:, kc, :],
                                       start=(kc == 0), stop=(kc == KC - 1))
                return ins
            sc.op("pe", mm, waits=[v0[1], v1[1]] if c == 0 else [])
            sc.op("act", lambda e, c=c: e.copy(out=VTM[0:64, c, :], in_=ps[0][0:64, :]))
        for c in range(8):
            cs = slice(c * 64, (c + 1) * 64)

            def smid(e, c=c):
                ins = None
                for h in range(4):
                    ins = e.tensor_scalar(out=SMID[:, h, :], in0=S_A[:, h, :], scalar1=EB[:, 0, h, c:c + 1], scalar2=None, op0=ALU.mult)
                return ins
            sc.op("dve", smid)

            def pmm(e, cs=cs):
                ins = None
                for h in range(4):
                    ins = e.matmul(ps[3][0:64, h * 64:(h + 1) * 64], KTb[:, h, cs], QTb[:, h, cs], start=True, stop=True)
                return ins
            sc.op("pe", pmm)

            def pmask(e):
                ins = None
                for h in range(4):
                    ins = e.tensor_tensor(out=PM[0:64, h, :], in0=ps[3][0:64, h * 64:(h + 1) * 64], in1=TRI[0:64, 0:64], op=ALU.mult)
                return ins
            sc.op("dve", pmask)

            def omm(e, c=c, cs=cs):
                ins = None
                for h in range(4):
                    e.matmul(ps[4][:, h * 64:(h + 1) * 64], SMID[:, h, :], QTb[:, h, cs], start=True, stop=False)
                    ins = e.matmul(ps[4][:, h * 64:(h + 1) * 64], VTM[0:64, c, h * 128:(h + 1) * 128], PM[0:64, h, :], start=False, stop=True)
                return ins
            sc.op("pe", omm)

            def umm(e, c=c):
                ins = None
                for h in range(4):
                    ins = e.matmul(ps[5][:, h * 128:(h + 1) * 128], KTM[0:64, h, c, :], VTM[0:64, c, h * 128:(h + 1) * 128], start=True, stop=True)
                return ins
            sc.op("pe", umm)

            def utmp(e, c=c):
                ins = None
                for h in range(4):
                    ins = e.tensor_scalar(out=TMPU[:, h, :], in0=ps[5][:, h * 128:(h + 1) * 128], scalar1=EB[:, 2, h, c:c + 1], scalar2=None, op0=ALU.mult)
                return ins
            sc.op("dve", utmp)

            def supd(e, c=c):
                ins = None
                for h in range(4):
                    ins = e.scalar_tensor_tensor(out=S_A[:, h, :], in0=S_A[:, h, :], scalar=EB[:, 1, h, c:c + 1], in1=TMPU[:, h, :],
                                                 op0=ALU.mult, op1=ALU.add)
                return ins
            sc.op("dve", supd)
            sc.op("act", lambda e, cs=cs: e.copy(out=O[:, :, cs], in_=ps[4][:, 0:256].rearrange("p (h t) -> p h t", t=64)))
        SQh = t1[:].bitcast(BF16) if False else None
        for h in range(4):
            slg = win_slab(l, 6 + h // 2) if h % 2 == 0 else sl_g
            sl_g = slg
            sc.op("act", lambda e, h=h: e.activation(out=KTb[:, 0, :], in_=O[:, h, :], func=AF.Square))
            sc.op("pe", lambda e: e.matmul(ps[2][:], ONESB_t[:], KTb[:, 0, :], start=True, stop=True))
            sc.op("dve", lambda e: e.tensor_scalar(out=t2[:], in0=ps[2][:], scalar1=1.0 / 128, scalar2=EPS, op0=ALU.mult, op1=ALU.add))
            sc.op("act", lambda e: e.sqrt(out=t2[:], in_=t2[:]))
            sc.op("dve", lambda e: e.reciprocal(out=t2[:], in_=t2[:]))
            sc.op("dve", lambda e, h=h: e.scalar_tensor_tensor(out=t3[:], in0=O[:, h, :], scalar=PVM[:, pc + 104 + h:pc + 105 + h], in1=t2[:],
                                                               op0=ALU.mult, op1=ALU.mult))
            proj_fm(slg[0], h % 2, ps[0], slg[1])
            sc.op("act", lambda e: e.activation(out=t4[:], in_=ps[0][:], func=AF.Silu))
            sc.op("dve", lambda e, h=h: e.tensor_tensor(out=YT[:, h, :], in0=t3[:], in1=t4[:], op=ALU.mult))

    def layer_init(l):
        pc = 128 * l
        ssd_layer_init(l)
        sc.op("dve", lambda e: e.memset(S_A[:], 0.0))
        ld = sc.dma("sp", lambda e: e.dma_start(out=XT[:, 0:2, :].rearrange("p a (b c) -> p (a b) c", c=128),
                                                 in_=lbd_in[l]), "d_misc")
        sc.op("dve", lambda e: e.tensor_copy(out=LBD[:].rearrange("p a b c -> p (a b) c"),
                                             in_=XT[:, 0:2, :].rearrange("p a (b c) -> p (a b) c", c=128)), waits=[ld])
        ldw = sc.dma("sp", lambda e: e.dma_start(out=WBA[:], in_=anw_in[l]), "d_misc")
        sc.op("dve", lambda e: e.memset(HALO_D[:], 0.0), waits=[ldw])
        sc.op("dve", lambda e: e.memset(HPREV[:], 0.0))
        sc.op("act", lambda e: e.activation(out=PVM[:, pc + 40:pc + 44], in_=PVM[:, pc + 28:pc + 32], func=AF.Exp, scale=-1.0))
        sc.op("act", lambda e: e.activation(out=PVM[:, pc + 40:pc + 44], in_=PVM[:, pc + 40:pc + 44], func=AF.Ln, bias=1.0))
        sc.op("dve", lambda e: e.tensor_scalar(out=PVM[:, pc + 40:pc + 44], in0=PVM[:, pc + 40:pc + 44], scalar1=-8.0, scalar2=None, op0=ALU.mult))

    def out_proj(l):
        wo = wb[(l, "out")]
        for oc in range(16):
            b = win_i[0] % 4
            win_i[0] += 1
            tok = sc.dma("sp", lambda e, b=b, oc=oc: e.dma_start(out=WGU[b][:, :, 0:128], in_=wo[oc].rearrange("p (k c) -> p k c", c=128)), f"d_wgu{b}")

            def mm(e, b=b):
                ins = None
                for kc in range(KC):
                    ins = e.matmul(ps[3][:], WGU[b][:, kc, 0:128], YT[:, kc, :], start=(kc == 0), stop=(kc == KC - 1))
                return ins
            sc.op("pe", mm, waits=[tok])
            sc.op("dve", lambda e, oc=oc: e.tensor_tensor(out=XT[:, oc, :], in0=ps[3][:], in1=XT[:, oc, :], op=ALU.add))

    def mixer(l, ti):
        rmsnorm_to(lambda kc: HT[:, kc, :], 64 * l + 16)
        def mkTA():
            return Tmp(arena0 + 16384, 45056 - 16384)

        def mkTB():
            return Tmp(WDBASE, TBSIZE)
        sc.op("dve", lambda e: e.memset(YT[:, 0:12, :], 0.0))
        if 'A' in MIX:
            mixer_hgrn(l, ti, mkTA(), mkTB())
        if 'B' in MIX:
            mixer_attn(l, ti, mkTA(), mkTB())
        if 'C' in MIX:
            mixer_ssd(l, ti, mkTA(), mkTB())
        mixer_lru(l, ti, mkTA(), mkTB())
        out_proj(l)

    ct = load_consts()
    sc.op("dve", lambda e: e.tensor_copy(out=ONESB_t[:], in_=CON[:, 128:256]), waits=[ct])
    sc.op("dve", lambda e: e.tensor_copy(out=IDB_t[:], in_=CON[:, 0:128]))
    attn_init()
    hgrn_init()
    convert_all()
    for l in range(DEPTH):
        layer_init(l)
        for ti in range(NT):
            load_x_tile(l, ti)
            ffn(l, "ffn1", 64 * l + 0)
            if dbg != "ffn1only":
                mixer(l, ti)
                ffn(l, "ffn2", 64 * l + 32)
            if l == DEPTH - 1:
                final_store(ti)
            else:
                store_x_tile(ti)

    semnames = sorted({tok[0] for (_, _, _, tok, _) in sc.ops})
    sems = {n: nc.alloc_semaphore(n) for n in semnames}
    with nc.Block() as block:
        def emit(engname):
            def body(e):
                for (eng, fn, waits, tok, inc) in sc.ops:
                    if eng != engname:
                        continue
                    for (sn, val) in waits:
                        e.wait_ge(sems[sn], val)
                    ins = fn(e)
                    ins.then_inc(sems[tok[0]], inc)
            return body
        block.tensor(emit("pe"))
        block.scalar(emit("act"))
        block.vector(emit("dve"))
        block.gpsimd(emit("pool"))
        block.sync(emit("sp"))
    return nc


def make_consts():
    c = np.zeros((128, 512), np.float32)
    c[:, 0:128] = np.eye(128, dtype=np.float32)
    c[:, 128:256] = 1.0
    i = np.arange(128)
    c[:, 256:384] = (i[:, None] <= i[None, :]).astype(np.float32)
    c[:, 384:512] = np.where(i[:, None] <= i[None, :], 0.0, -30000.0)
    return c


def make_pvec(inp, DEPTH):
    pv = np.zeros((128, 64 * DEPTH + 16), np.float32)
    for l in range(DEPTH):
        pv[:, 64 * l + 0:64 * l + 16] = inp["ffn1_norm"][l].reshape(16, 128).T
        pv[:, 64 * l + 16:64 * l + 32] = inp["mix_norm"][l].reshape(16, 128).T
        pv[:, 64 * l + 32:64 * l + 48] = inp["ffn2_norm"][l].reshape(16, 128).T
    pv[:, 64 * DEPTH:64 * DEPTH + 16] = inp["final_norm"].reshape(16, 128).T
    return pv


def make_amask():
    m = np.zeros((128, 17, 128), np.float32)
    i = np.arange(128)[:, None]
    j = np.arange(128)[None, :]
    for dl in range(17):
        dist = 128 * dl + j - i
        for (win, dil) in ((128, 1), (512, 4), (2048, 16)):
            m[:, dl, :] += ((dist >= 0) & (dist % dil == 0) & (dist // dil <= 128)).astype(np.float32)
    return m.reshape(128, 17 * 128)


def make_pvm(inp, DEPTH):
    pm = np.zeros((128, 128 * DEPTH), np.float32)
    lbd = np.zeros((DEPTH, 128, 8, 128), np.float32)
    for l in range(DEPTH):
        pc = 128 * l
        cw = inp["lru_conv_w"][l]
        for j in range(4):
            pm[:, pc + j * 4:pc + j * 4 + 4] = cw[:, j * 128:(j + 1) * 128].T
        pm[:, pc + 16:pc + 20] = inp["lru_conv_b"][l].reshape(4, 128).T
        pm[:, pc + 20:pc + 24] = inp["lru_b_a"][l].reshape(4, 128).T
        pm[:, pc + 24:pc + 28] = inp["lru_b_x"][l].reshape(4, 128).T
        pm[:, pc + 28:pc + 32] = inp["lru_a_param"][l].reshape(4, 128).T
        pm[:, pc + 32:pc + 36] = inp["lru_norm"][l].reshape(4, 128).T
        pm[:, pc + 104:pc + 108] = inp["hgrn_norm"][l].reshape(4, 128).T
        scw = inp["ssm_conv_w"][l]
        for j in range(8):
            pm[:, pc + 48 + j * 4:pc + 52 + j * 4] = scw[:, j * 128:(j + 1) * 128].T
        pm[:, pc + 80:pc + 88] = inp["ssm_conv_b"][l].reshape(8, 128).T
        pm[:, pc + 88:pc + 92] = inp["ssm_norm"][l].reshape(4, 128).T
        for wi, nm in enumerate(("lru_w_a", "lru_w_x")):
            w = inp[nm][l]
            for j in range(4):
                for hb in range(2):
                    lbd[l, hb * 64:(hb + 1) * 64, wi * 4 + j, hb * 64:(hb + 1) * 64] = w[2 * j + hb]
    return pm, lbd


def run(inp, S, DEPTH, B, dbg=None):
    nc = build(S, DEPTH, dbg)
    consts = make_consts()
    pv = make_pvec(inp, DEPTH)
    pm, lbd = make_pvm(inp, DEPTH)
    amask = make_amask()
    lbl = np.ascontiguousarray(inp["hgrn_lb_logits"][:DEPTH].reshape(DEPTH, 4, 128).transpose(2, 0, 1).reshape(128, DEPTH * 4)).astype(np.float32)
    ssmb = np.zeros((DEPTH, 128, 528), np.float32)
    for l in range(DEPTH):
        ssmb[l, :, 0:512] = np.repeat(inp["ssm_d"][l], 64)[None, :]
        ssmb[l, :, 512:520] = inp["ssm_dt_bias"][l][None, :]
        ssmb[l, :, 520:528] = inp["ssm_a_log"][l][None, :]
    anw = np.ascontiguousarray(np.broadcast_to(inp["attn_norm"][:DEPTH, None, :], (DEPTH, 128, 512))).astype(np.float32)
    in_maps = []
    for b in range(B):
        m = {"x": np.ascontiguousarray(inp["x"][b]), "consts": consts, "pvec": pv, "pvm": pm, "lbd": lbd, "amask": amask, "anw": anw, "ssmb": ssmb, "lbl": lbl}
        for nm in ("ffn1_w_gate", "ffn1_w_up", "ffn1_w_down", "ffn2_w_gate", "ffn2_w_up", "ffn2_w_down", "w_in", "w_out"):
            m[nm] = np.asarray(inp[nm])
        in_maps.append(m)
    res = run_bass_kernel_spmd(nc, in_maps, core_ids=list(range(B)))
    global LAST
    LAST = res.results
    return np.stack([res.results[b]["out"] for b in range(B)], 0)


def kernel(**inputs):
    inp = {k: np.asarray(v) for k, v in inputs.items()}
    return run(inp, 16384, 4, 2).astype(np.float32)
```

```python
import numpy as np
import concourse.bass as bass
import concourse.mybir as mybir
from concourse.bass_utils import run_bass_kernel_spmd

F32, BF16 = mybir.dt.float32, mybir.dt.bfloat16
AF = mybir.ActivationFunctionType
ALU = mybir.AluOpType

D = 2048
DFF = 5632
NIN = 6152
T = 512
KC = 16
FCH = 44
EPS = 1e-6
ENGS = ("pe", "act", "dve", "pool", "sp")


class Sched:
    def __init__(self):
        self.ops = []
        self.cnt = {e: 0 for e in ENGS}
        self.dcnt = {}
        self.last = None

    def op(self, eng, fn, waits=(), chain=True):
        w = [x for x in waits if x is not None]
        if chain and self.last is not None:
            w.append(self.last)
        self.cnt[eng] += 1
        tok = ("c_" + eng, self.cnt[eng])
        self.ops.append((eng, fn, w, tok, 1))
        if chain:
            self.last = tok
        return tok

    def dma(self, q, fn, dsem, waits=(), chain=True):
        w = [x for x in waits if x is not None]
        if chain and self.last is not None:
            w.append(self.last)
        self.dcnt[dsem] = self.dcnt.get(dsem, 0) + 16
        tok = (dsem, self.dcnt[dsem])
        self.ops.append((q, fn, w, tok, 16))
        return tok


MIX = 'ABCD'
DBG = False
DBGMAP = {}
LAST = None


def build(S, DEPTH, dbg=None):
    NT = S // T
    nc = bass.Bass("TRN2", target_bir_lowering=False)

    def din(name, shape, dt=F32):
        return nc.dram_tensor(name, list(shape), dt, kind="ExternalInput").ap()

    x_in = din("x", [S, D])
    out = nc.dram_tensor("out", [S, D], F32, kind="ExternalOutput").ap()
    dbgout = nc.dram_tensor("dbg", [128, 8192], F32, kind="ExternalOutput").ap() if DBG else None
    dbgpos = [0]
    dbgmap = {}
    wsrc = {}
    for nm, shp in (("ffn1_w_gate", [DEPTH, D, DFF]), ("ffn1_w_up", [DEPTH, D, DFF]),
                    ("ffn1_w_down", [DEPTH, DFF, D]), ("ffn2_w_gate", [DEPTH, D, DFF]),
                    ("ffn2_w_up", [DEPTH, D, DFF]), ("ffn2_w_down", [DEPTH, DFF, D]),
                    ("w_in", [DEPTH, D, NIN]), ("w_out", [DEPTH, D, D])):
        wsrc[nm] = din(nm, shp)
    consts = din("consts", [128, 512])
    pvec = din("pvec", [128, 64 * DEPTH + 16])
    pvm_in = din("pvm", [128, 128 * DEPTH])
    lbd_in = din("lbd", [DEPTH, 128, 8, 128])
    amask_in = din("amask", [128, 17 * 128])
    anw_in = din("anw", [DEPTH, 128, 512])
    ssmb_in = din("ssmb", [DEPTH, 128, 528])
    lbl_in = din("lbl", [128, DEPTH * 4])

    xs = nc.dram_tensor("xs", [KC, 128, S], F32).ap()
    wb = {}
    for l in range(DEPTH):
        for f in ("ffn1", "ffn2"):
            wb[(l, f, "g")] = nc.dram_tensor(f"wb_{l}_{f}_g", [22, 128, KC * 256], BF16).ap()
            wb[(l, f, "u")] = nc.dram_tensor(f"wb_{l}_{f}_u", [22, 128, KC * 256], BF16).ap()
            wb[(l, f, "d")] = nc.dram_tensor(f"wb_{l}_{f}_d", [16, 128, FCH * 128], BF16).ap()
        wb[(l, "in")] = nc.dram_tensor(f"wb_{l}_in", [24, 128, KC * 256], BF16).ap()
        wb[(l, "dt")] = nc.dram_tensor(f"wb_{l}_dt", [128, KC * 8], BF16).ap()
        wb[(l, "out")] = nc.dram_tensor(f"wb_{l}_out", [16, 128, KC * 128], BF16).ap()

    off = [16640]

    def sb(name, shape, dt, at=None):
        nbytes = int(np.prod(shape[1:])) * (4 if dt == F32 else 2)
        if at is None:
            o = off[0]
            off[0] += (nbytes + 63) // 64 * 64
        else:
            o = at
        return nc.alloc_sbuf_tensor_at(name, list(shape), dt, offset=o)

    XT = sb("XT", [128, KC, T], F32)
    HT = sb("HT", [128, KC, T], BF16)
    arena0 = off[0]
    AT = sb("AT", [128, FCH, T], BF16)
    WGU_OFF = off[0]
    WGU = [sb(f"WGU{i}", [128, KC, 256], BF16) for i in range(4)]
    WD = [nc.alloc_sbuf_tensor_at(f"WD{i}", [128, FCH, 128], BF16, offset=WGU_OFF + i * 16384) for i in range(2)]
    WDBASE = off[0]
    TBSIZE = 14336
    off[0] += TBSIZE
    CON = sb("CON", [128, 512], F32)
    PV = sb("PV", [128, 64 * DEPTH + 16], F32)
    RSTD = sb("RSTD", [128, T], F32)
    SG = sb("SG", [128, T], F32)
    assert off[0] <= 229344, off[0]
    STG = nc.alloc_sbuf_tensor_at("STG", [128, 4, D], F32, offset=arena0)
    SQ = nc.alloc_sbuf_tensor_at("SQ", [128, KC, T], BF16, offset=arena0)
    NCV = 3
    CVI = [nc.alloc_sbuf_tensor_at(f"CVI{i}", [128, 2048], F32, offset=arena0 + i * 8192) for i in range(NCV)]
    CVO = [nc.alloc_sbuf_tensor_at(f"CVO{i}", [128, 2048], BF16, offset=arena0 + NCV * 8192 + i * 4096) for i in range(NCV)]

    IDF = CON[:, 0:128]
    ONESB = None

    ps = [nc.alloc_psum_tensor(f"ps{i}", [128, 512], F32) for i in range(8)]

    sc = Sched()

    def load_consts():
        t1 = sc.dma("sp", lambda e: e.dma_start(out=CON[:], in_=consts[:, :]), "d_misc")
        t2 = sc.dma("sp", lambda e: e.dma_start(out=PV[:], in_=pvec[:, :]), "d_misc")
        t2 = sc.dma("sp", lambda e: e.dma_start(out=PVM[:], in_=pvm_in[:, :]), "d_misc")
        return t2

    ONESB_t = sb("ONESB", [128, 128], BF16)
    IDB_t = sb("IDB", [128, 128], BF16)

    def convert_all():
        blocks = []

        def conv_block(src_ap, dst_ap, ncols, nsl, sw):
            blocks.append((src_ap, dst_ap, ncols, sw))

        def run_blocks():
            store_tok = [None] * NCV
            ld = {}

            def issue_load(k):
                src_ap, _, ncols, _ = blocks[k]
                i = k % NCV
                ld[k] = sc.dma("sp", lambda e: e.dma_start(out=CVI[i][:, 0:ncols], in_=src_ap), f"d_cvi{i}")
            issue_load(0)
            if len(blocks) > 1:
                issue_load(1)
            for k in range(len(blocks)):
                if k + 2 < len(blocks):
                    issue_load(k + 2)
                _, dst_ap, ncols, sw = blocks[k]
                i = k % NCV
                eng = ("dve", "pool", "act")[k % 3]

                def cast(e, i=i, ncols=ncols, eng=eng):
                    if eng == "act":
                        return e.copy(out=CVO[i][:, 0:ncols], in_=CVI[i][:, 0:ncols])
                    return e.tensor_copy(out=CVO[i][:, 0:ncols], in_=CVI[i][:, 0:ncols])
                sc.op(eng, cast, waits=[ld[k], store_tok[i]])
                store_tok[i] = sc.dma(
                    "sp", lambda e, i=i, ncols=ncols, sw=sw, dst_ap=dst_ap: e.dma_start(
                        out=dst_ap, in_=CVO[i][:, 0:ncols].rearrange("p (s c) -> p s c", c=sw)), f"d_cvo{i}")
            sc.op("dve", lambda e: e.tensor_copy(out=SG[:, 0:8], in_=SG[:, 8:16]), waits=store_tok)

        for l in range(DEPTH):
            for f in ("ffn1", "ffn2"):
                for nm, key in ((f + "_w_gate", "g"), (f + "_w_up", "u")):
                    src = wsrc[nm]
                    dst = wb[(l, f, key)].rearrange("s p (k c) -> s p k c", c=256)
                    for kc in range(KC):
                        for cb in range(0, DFF, 2048):
                            ncols = min(2048, DFF - cb)
                            s0 = cb // 256
                            nsl = ncols // 256
                            conv_block(src[l, kc * 128:(kc + 1) * 128, cb:cb + ncols],
                                       dst[s0:s0 + nsl, :, kc, :].rearrange("s p c -> p s c"), ncols, nsl, 256)
                src = wsrc[f + "_w_down"]
                dst = wb[(l, f, "d")].rearrange("s p (k c) -> s p k c", c=128)
                for kc in range(FCH):
                    conv_block(src[l, kc * 128:(kc + 1) * 128, :],
                               dst[:, :, kc, :].rearrange("s p c -> p s c"), 2048, 16, 128)
            src = wsrc["w_in"]
            dst = wb[(l, "in")].rearrange("s p (k c) -> s p k c", c=256)
            for kc in range(KC):
                for (c0, d0, ncols) in ((0, 0, 2048), (2048, 2048, 2048), (4096, 4096, 1024), (5128, 5120, 1024)):
                    s0 = d0 // 256
                    nsl = ncols // 256
                    conv_block(src[l, kc * 128:(kc + 1) * 128, c0:c0 + ncols],
                               dst[s0:s0 + nsl, :, kc, :].rearrange("s p c -> p s c"), ncols, nsl, 256)
                dstd = wb[(l, "dt")].rearrange("p (k c) -> p k c", c=8)
                conv_block(src[l, kc * 128:(kc + 1) * 128, 5120:5128], dstd[:, kc:kc + 1, :], 8, 1, 8)
            src = wsrc["w_out"]
            dst = wb[(l, "out")].rearrange("s p (k c) -> s p k c", c=128)
            for kc in range(KC):
                conv_block(src[l, kc * 128:(kc + 1) * 128, :],
                           dst[:, :, kc, :].rearrange("s p c -> p s c"), 2048, 16, 128)
        run_blocks()

    def load_x_tile(l, ti):
        t0 = ti * T
        if l == 0:
            ld = sc.dma("sp", lambda e: e.dma_start(
                out=STG[:], in_=x_in[t0:t0 + T, :].rearrange("(c p) d -> p c d", p=128)), "d_x")
            first = True
            for kc in range(KC):
                def tr(e, kc=kc):
                    ins = None
                    for c in range(4):
                        ins = e.transpose(ps[kc % 2][:, c * 128:(c + 1) * 128], STG[:, c, kc * 128:(kc + 1) * 128], IDF)
                    return ins
                sc.op("pe", tr, waits=[ld] if first else [])
                first = False
                sc.op("act" if kc % 2 else "dve",
                      (lambda e, kc=kc: e.copy(out=XT[:, kc, :], in_=ps[kc % 2][:])) if kc % 2 else
                      (lambda e, kc=kc: e.tensor_copy(out=XT[:, kc, :], in_=ps[kc % 2][:])))
        else:
            ld = sc.dma("sp", lambda e: e.dma_start(
                out=XT[:], in_=xs[:, :, t0:t0 + T].rearrange("k p t -> p k t")), "d_x")
            sc.op("dve", lambda e: e.tensor_copy(out=SG[:, 0:8], in_=SG[:, 8:16]), waits=[ld])

    def store_x_tile(ti):
        t0 = ti * T
        st = sc.dma("sp", lambda e: e.dma_start(
            out=xs[:, :, t0:t0 + T].rearrange("k p t -> p k t"), in_=XT[:]), "d_xst")
        sc.op("dve", lambda e: e.tensor_copy(out=SG[:, 0:8], in_=SG[:, 8:16]), waits=[st])

    def rmsnorm_to(dst_fn, wcol):
        sc.op("act", lambda e: e.activation(out=SQ[:], in_=XT[:], func=AF.Square))

        def mm(e):
            ins = None
            for kc in range(KC):
                ins = e.matmul(ps[2][:], ONESB_t[:], SQ[:, kc, :], start=(kc == 0), stop=(kc == KC - 1))
            return ins
        sc.op("pe", mm)

        sc.op("dve", lambda e: e.tensor_scalar(out=RSTD[:], in0=ps[2][:], scalar1=1.0 / D, scalar2=EPS, op0=ALU.mult, op1=ALU.add))
        sc.op("act", lambda e: e.sqrt(out=RSTD[:], in_=RSTD[:]))
        sc.op("dve", lambda e: e.reciprocal(out=RSTD[:], in_=RSTD[:]))

        def hh(e):
            ins = None
            for kc in range(KC):
                ins = e.scalar_tensor_tensor(out=dst_fn(kc), in0=XT[:, kc, :], scalar=PV[:, wcol + kc:wcol + kc + 1],
                                             in1=RSTD[:], op0=ALU.mult, op1=ALU.mult)
            return ins
        sc.op("dve", hh)

    wgu_free = [None] * 4
    wd_free = [None] * 2

    def ffn(l, f, wcol):
        rmsnorm_to(lambda kc: HT[:, kc, :], wcol)
        wg, wu, wd = wb[(l, f, "g")], wb[(l, f, "u")], wb[(l, f, "d")]
        base = sc.last
        SGs = (SG, RSTD)
        banks = ((ps[0], ps[1]), (ps[4], ps[5]))
        pe_t, act_t, dve_t, loads = {}, {}, {}, {}

        def issue_gu(s_):
            b = (s_ % 2) * 2
            w = [pe_t[2 * (s_ - 2) + 1]] if s_ >= 2 else [base]
            tg = sc.dma("sp", lambda e: e.dma_start(out=WGU[b][:].rearrange("p k c -> p (k c)"), in_=wg[s_]), f"d_wgu{b}",
                        waits=w, chain=False)
            tu = sc.dma("sp", lambda e: e.dma_start(out=WGU[b + 1][:].rearrange("p k c -> p (k c)"), in_=wu[s_]), f"d_wgu{b + 1}",
                        waits=w, chain=False)
            loads[s_] = (tg, tu)
        issue_gu(0)
        issue_gu(1)
        for fc in range(FCH):
            s_, j = fc // 2, fc % 2
            if j == 0 and s_ >= 1 and s_ + 1 < 22:
                issue_gu(s_ + 1)
            b = (s_ % 2) * 2
            pg, pu = banks[fc % 2]

            def mm(e, b=b, j=j, pg=pg, pu=pu):
                ins = None
                for kc in range(KC):
                    ins = e.matmul(pg[:], WGU[b][:, kc, j * 128:(j + 1) * 128], HT[:, kc, :], start=(kc == 0), stop=(kc == KC - 1))
                for kc in range(KC):
                    ins = e.matmul(pu[:], WGU[b + 1][:, kc, j * 128:(j + 1) * 128], HT[:, kc, :], start=(kc == 0), stop=(kc == KC - 1))
                return ins
            w = list(loads[s_]) if j == 0 else []
            w.append(dve_t[fc - 2] if fc >= 2 else base)
            pe_t[fc] = sc.op("pe", mm, waits=w, chain=False)
            sgb = SGs[fc % 2]
            act_t[fc] = sc.op("act", lambda e, sgb=sgb, pg=pg: e.activation(out=sgb[:], in_=pg[:], func=AF.Silu),
                              waits=[pe_t[fc]] + ([dve_t[fc - 2]] if fc >= 2 else []), chain=False)
            dve_t[fc] = sc.op("dve", lambda e, fc=fc, sgb=sgb, pu=pu: e.tensor_tensor(out=AT[:, fc, :], in0=sgb[:], in1=pu[:], op=ALU.mult),
                              waits=[act_t[fc]], chain=False)
        sc.last = dve_t[FCH - 1]
        base2 = sc.last
        dbanks = (ps[3], ps[6])
        pd_t, dd_t, dl = {}, {}, {}

        def issue_d(oc):
            b = oc % 2
            w = [pd_t[oc - 2]] if oc >= 2 else [base2]
            dl[oc] = sc.dma("sp", lambda e: e.dma_start(out=WD[b][:].rearrange("p k c -> p (k c)"), in_=wd[oc]), f"d_wd{b}",
                            waits=w, chain=False)
        issue_d(0)
        issue_d(1)
        for oc in range(16):
            if oc >= 1 and oc + 1 < 16:
                issue_d(oc + 1)
            b = oc % 2
            pb_ = dbanks[oc % 2]

            def mm(e, b=b, pb_=pb_):
                ins = None
                for k in range(FCH):
                    ins = e.matmul(pb_[:], WD[b][:, k, :], AT[:, k, :], start=(k == 0), stop=(k == FCH - 1))
                return ins
            pd_t[oc] = sc.op("pe", mm, waits=[dl[oc], dd_t[oc - 2] if oc >= 2 else base2], chain=False)
            dd_t[oc] = sc.op("dve", lambda e, oc=oc, pb_=pb_: e.scalar_tensor_tensor(out=XT[:, oc, :], in0=pb_[:], scalar=0.5,
                                                                                   in1=XT[:, oc, :], op0=ALU.mult, op1=ALU.add),
                             waits=[pd_t[oc]], chain=False)
        sc.last = dd_t[15]

    def final_store(ti):
        t0 = ti * T
        FN = nc.alloc_sbuf_tensor_at(f"FN{ti}", [128, T], F32, offset=arena0 + 32768)
        sc.op("act", lambda e: e.activation(out=HT[:], in_=XT[:], func=AF.Square))

        def mm(e):
            ins = None
            for kc in range(KC):
                ins = e.matmul(ps[2][:], ONESB_t[:], HT[:, kc, :], start=(kc == 0), stop=(kc == KC - 1))
            return ins
        sc.op("pe", mm)

        sc.op("dve", lambda e: e.tensor_scalar(out=RSTD[:], in0=ps[2][:], scalar1=1.0 / D, scalar2=EPS, op0=ALU.mult, op1=ALU.add))
        sc.op("act", lambda e: e.sqrt(out=RSTD[:], in_=RSTD[:]))
        sc.op("dve", lambda e: e.reciprocal(out=RSTD[:], in_=RSTD[:]))
        wcol = 64 * DEPTH
        for kc in range(KC):
            sc.op("dve", lambda e, kc=kc: e.scalar_tensor_tensor(out=FN[:], in0=XT[:, kc, :], scalar=PV[:, wcol + kc:wcol + kc + 1],
                                                                 in1=RSTD[:], op0=ALU.mult, op1=ALU.mult))

            def tr(e, kc=kc):
                ins = None
                for c in range(4):
                    ins = e.transpose(ps[4][:, c * 128:(c + 1) * 128], FN[:, c * 128:(c + 1) * 128], IDF)
                return ins
            sc.op("pe", tr)
            sc.op("act", lambda e, kc=kc: e.copy(out=STG[:, :, kc * 128:(kc + 1) * 128],
                                                 in_=ps[4][:].rearrange("p (c t) -> p c t", c=4)))
        st = sc.dma("sp", lambda e: e.dma_start(out=out[t0:t0 + T, :].rearrange("(c p) d -> p c d", p=128), in_=STG[:]), "d_out")
        sc.op("dve", lambda e: e.tensor_copy(out=SG[:, 0:8], in_=SG[:, 8:16]), waits=[st])


    PVM = sb("PVM", [128, 128 * DEPTH], F32)
    LBD = sb("LBD", [128, 2, 4, 128], BF16)
    HALO_D = sb("HALO_D", [128, 4, 4], F32)
    HPREV = sb("HPREV", [128, 4], F32)
    assert off[0] <= 229344, off[0]
    wdbase = 16640 + 0
    uid = [0]

    class Tmp:
        def __init__(self, base, size):
            self.base, self.size, self.o, self.n = base, size, 0, 0
        def get(self, shape, dt):
            nbytes = int(np.prod(shape[1:])) * (4 if dt == F32 else 2)
            nbytes = (nbytes + 63) // 64 * 64
            assert self.o + nbytes <= self.size, (self.o, nbytes, self.size)
            self.n += 1
            uid[0] += 1
            t = nc.alloc_sbuf_tensor_at(f"tmp{uid[0]}", list(shape), dt, offset=self.base + self.o)
            self.o += nbytes
            return t
    YT = nc.alloc_sbuf_tensor_at("YT", [128, KC, T], BF16, offset=arena0)
    wd_off = [None]

    win_i = [0]

    def win_slab(l, s):
        b = win_i[0] % 4
        win_i[0] += 1
        tok = sc.dma("sp", lambda e: e.dma_start(out=WGU[b][:].rearrange("p k c -> p (k c)"), in_=wb[(l, "in")][s]), f"d_wgu{b}")
        return b, tok

    def proj_fm(b, j, pst, tok=None):
        def mm(e):
            ins = None
            for kc in range(KC):
                ins = e.matmul(pst[:], WGU[b][:, kc, j * 128:(j + 1) * 128], HT[:, kc, :], start=(kc == 0), stop=(kc == KC - 1))
            return ins
        sc.op("pe", mm, waits=[tok])

    def mixer_lru(l, ti, TA, TB):
        pc = 128 * l
        YD = TA.get([128, 4, T], F32)
        XB = TB.get([128, T + 4], F32)
        XC = TB.get([128, T], F32)
        XCB = TB.get([128, T], BF16)
        R = TB.get([128, T], F32)
        I = TB.get([128, T], F32)
        A = TB.get([128, T], F32)
        G = TB.get([128, T], F32)
        slabs = {}
        for j in range(4):
            sx = 20 + j // 2
            sg = 22 + j // 2
            if j % 2 == 0:
                slabs["x"] = win_slab(l, sx)
                slabs["g"] = win_slab(l, sg)
            bx, tx = slabs["x"]
            bg, tg = slabs["g"]
            proj_fm(bx, j % 2, ps[0], tx)
            sc.op("act", lambda e: e.copy(out=XB[:, 3:T + 3], in_=ps[0][:]))
            sc.op("dve", lambda e, j=j: e.tensor_copy(out=XB[:, 0:3], in_=HALO_D[:, j, 0:3]))
            sc.op("dve", lambda e, j=j: e.tensor_copy(out=HALO_D[:, j, 0:3], in_=XB[:, T:T + 3]))

            def conv(e, j=j):
                e.tensor_scalar(out=XC[:], in0=XB[:, 0:T], scalar1=PVM[:, pc + j * 4:pc + j * 4 + 1],
                                scalar2=PVM[:, pc + 16 + j:pc + 17 + j], op0=ALU.mult, op1=ALU.add)
                ins = None
                for tp in range(1, 4):
                    ins = e.scalar_tensor_tensor(out=XC[:], in0=XB[:, tp:tp + T], scalar=PVM[:, pc + j * 4 + tp:pc + j * 4 + tp + 1],
                                                 in1=XC[:], op0=ALU.mult, op1=ALU.add)
                return ins
            sc.op("dve", conv)
            sc.op("act", lambda e: e.copy(out=XCB[:], in_=XC[:]))
            sc.op("pe", lambda e, j=j: e.matmul(ps[1][:], LBD[:, 0, j, :], XCB[:], start=True, stop=True))
            sc.op("act", lambda e, j=j: e.activation(out=R[:], in_=ps[1][:], func=AF.Sigmoid, bias=PVM[:, pc + 20 + j:pc + 21 + j]))
            sc.op("pe", lambda e, j=j: e.matmul(ps[1][:], LBD[:, 1, j, :], XCB[:], start=True, stop=True))
            sc.op("act", lambda e, j=j: e.activation(out=I[:], in_=ps[1][:], func=AF.Sigmoid, bias=PVM[:, pc + 24 + j:pc + 25 + j]))
            sc.op("act", lambda e, j=j: e.activation(out=A[:], in_=R[:], func=AF.Exp, scale=PVM[:, pc + 40 + j:pc + 41 + j]))

            def bt(e):
                e.tensor_tensor(out=R[:], in0=A[:], in1=A[:], op=ALU.mult)
                e.tensor_scalar(out=R[:], in0=R[:], scalar1=-1.0, scalar2=1.0, op0=ALU.mult, op1=ALU.add)
                return e.tensor_scalar(out=R[:], in0=R[:], scalar1=0.0, scalar2=None, op0=ALU.max)
            sc.op("dve", bt)
            sc.op("act", lambda e: e.sqrt(out=R[:], in_=R[:]))

            def bt2(e):
                e.tensor_tensor(out=I[:], in0=I[:], in1=XC[:], op=ALU.mult)
                return e.tensor_tensor(out=I[:], in0=I[:], in1=R[:], op=ALU.mult)
            sc.op("dve", bt2)
            sc.op("dve", lambda e, j=j: e.tensor_tensor_scan(out=XC[:], data0=A[:], data1=I[:], initial=HPREV[:, j:j + 1],
                                                              op0=ALU.mult, op1=ALU.add))
            sc.op("dve", lambda e, j=j: e.tensor_copy(out=HPREV[:, j:j + 1], in_=XC[:, T - 1:T]))
            proj_fm(bg, j % 2, ps[0], tg)
            sc.op("act", lambda e: e.activation(out=G[:], in_=ps[0][:], func=AF.Gelu))
            sc.op("dve", lambda e, j=j: e.tensor_tensor(out=YD[:, j, :], in0=XC[:], in1=G[:], op=ALU.mult))
        group_norm_to_yt(YD, [0, 1, 2, 3], 12, pc + 32, TA)

    def group_norm_to_yt(YD, chunks, ybase, wcol, TB):
        n = len(chunks)
        SQm = TB.get([128, n, T], BF16)
        RS = TB.get([128, T], F32)
        sc.op("act", lambda e: e.activation(out=SQm[:], in_=YD[:, chunks[0]:chunks[0] + n, :], func=AF.Square))

        def mm(e):
            ins = None
            for i in range(n):
                ins = e.matmul(ps[2][:], ONESB_t[:], SQm[:, i, :], start=(i == 0), stop=(i == n - 1))
            return ins
        sc.op("pe", mm)
        sc.op("dve", lambda e: e.tensor_scalar(out=RS[:], in0=ps[2][:], scalar1=1.0 / (128 * n), scalar2=EPS, op0=ALU.mult, op1=ALU.add))
        sc.op("act", lambda e: e.sqrt(out=RS[:], in_=RS[:]))
        sc.op("dve", lambda e: e.reciprocal(out=RS[:], in_=RS[:]))

        def hh(e):
            ins = None
            for i, c in enumerate(chunks):
                ins = e.scalar_tensor_tensor(out=YT[:, ybase + i, :], in0=YD[:, c, :], scalar=PVM[:, wcol + i:wcol + i + 1],
                                             in1=RS[:], op0=ALU.mult, op1=ALU.mult)
            return ins
        sc.op("dve", hh)


    MASK = sb("MASK", [128, 17, 128], BF16)
    KTH = sb("KTH", [128, 4, 2560], BF16)
    VH = sb("VH", [128, 20, 8, 72], BF16)
    WBA = sb("WBA", [128, 512], F32)
    assert off[0] <= 229344, off[0]

    def attn_init():
        ld = sc.dma("sp", lambda e: e.dma_start(out=STG[:, 0, :].rearrange("p (a b) -> p a b", b=128)[:, 0:17, :] if False else XT[:, 0:5, :].rearrange("p a b -> p (a b)")[:, 0:17 * 128], in_=amask_in[:, :]), "d_misc")
        sc.op("dve", lambda e: e.tensor_copy(out=MASK[:].rearrange("p a b -> p (a b)"), in_=XT[:, 0:5, :].rearrange("p a b -> p (a b)")[:, 0:17 * 128]), waits=[ld])
        sc.op("dve", lambda e: e.memset(VH[:], 1.0))

    def dump(name, ap, ncols):
        if not DBG:
            return
        DSTG = nc.alloc_sbuf_tensor_at(f"DSTG{dbgpos[0]}", [128, 2048], F32, offset=arena0 + 16384)
        c0 = dbgpos[0]
        dbgpos[0] += ncols
        DBGMAP[name] = (c0, ncols)
        sc.op("dve", lambda e: e.tensor_copy(out=DSTG[:, 0:ncols], in_=ap))
        st = sc.dma("sp", lambda e: e.dma_start(out=dbgout[:, c0:c0 + ncols], in_=DSTG[:, 0:ncols]), "d_dbg")
        sc.op("dve", lambda e: e.tensor_copy(out=SG[:, 0:8], in_=SG[:, 8:16]), waits=[st])

    def mixer_attn(l, ti, TA, TB):
        pc = 128 * l
        QZ = TA.get([128, 8, T], BF16)
        E = TB.get([128, 512], BF16)
        P = TB.get([128, 4, 128], BF16)
        E2 = TB.get([128, 512], BF16)
        P2 = TB.get([128, 4, 128], BF16)
        OT = TB.get([128, 512], F32)
        JK = TB.get([128, 512], F32)
        YM = TB.get([128, 512], F32)
        SS = TB.get([128, 16], F32)
        bslot = (ti * 4) % 20
        NB = (1, 2, 6, 7)
        for j in range(4):
            if j % 2 == 0:
                sq_ = win_slab(l, 8 + j // 2)
                sk_ = win_slab(l, 10 + j // 2)
            proj_fm(sq_[0], j % 2, ps[0], sq_[1])
            if j == 0:
                sc.op("dve", lambda e: e.memset(QZ[:], 0.0))
            sc.op("act", lambda e, j=j: e.mul(out=QZ[0:64, 2 * j, :], in_=ps[0][0:64, :], mul=0.125))
            sc.op("act", lambda e, j=j: e.mul(out=QZ[64:128, 2 * j + 1, :], in_=ps[0][64:128, :], mul=0.125))
            proj_fm(sk_[0], j % 2, ps[0], sk_[1])
            sc.op("act", lambda e, j=j: e.copy(out=KTH[:, j, bslot * 128:bslot * 128 + T], in_=ps[0][:]))
        v0 = win_slab(l, 12)
        v1 = win_slab(l, 13)
        for c in range(4):
            def mm(e, c=c):
                ins = None
                for half, (b, _) in enumerate((v0, v1)):
                    for kc in range(KC):
                        ins = e.matmul(ps[0][:, half * 256:(half + 1) * 256], HT[:, kc, c * 128:(c + 1) * 128], WGU[b][:, kc, :],
                                       start=(kc == 0), stop=(kc == KC - 1))
                return ins
            sc.op("pe", mm, waits=[v0[1], v1[1]] if c == 0 else [])
            sc.op("act", lambda e, c=c: e.copy(out=VH[:, bslot + c, :, 0:64], in_=ps[0][:].rearrange("p (h d) -> p h d", d=64)))
        import os
        STAGE = int(os.environ.get('ATT_STAGE', '9'))
        for c in range(4):
            g = ti * 4 + c
            nd = min(16, g) + 1
            for quad in range(2):
                base = sc.last
                sbank = (ps[4], ps[5])
                Eb = (E, E2)
                Pb = (P, P2)
                smm_t, act_t, dve_t, nmm_t = {}, {}, {}, {}

                def emit_smm(u, c=c, quad=quad, g=g):
                    ks = (g - u) % 20
                    bank = sbank[u % 2]

                    def smm(e):
                        ins = None
                        for hh in range(4):
                            h = quad * 4 + hh
                            ins = e.matmul(bank[:, hh * 128:(hh + 1) * 128], KTH[:, h // 2, ks * 128:(ks + 1) * 128],
                                           QZ[:, h, c * 128:(c + 1) * 128], start=True, stop=True)
                        return ins
                    smm_t[u] = sc.op("pe", smm, waits=[act_t[u - 2] if u >= 2 else base], chain=False)
                emit_smm(0)
                for dl in range(nd):
                    if dl + 1 < nd:
                        emit_smm(dl + 1)
                    ks = (g - dl) % 20
                    bank, Eu, Pu = sbank[dl % 2], Eb[dl % 2], Pb[dl % 2]
                    act_t[dl] = sc.op("act", lambda e, bank=bank, Eu=Eu: e.activation(out=Eu[:], in_=bank[:], func=AF.Exp),
                                      waits=[smm_t[dl]] + ([dve_t[dl - 2]] if dl >= 2 else []), chain=False)

                    def msk(e, dl=dl, Eu=Eu, Pu=Pu):
                        ins = None
                        for hh in range(4):
                            ins = e.tensor_tensor(out=Pu[:, hh, :], in0=Eu[:, hh * 128:(hh + 1) * 128], in1=MASK[:, dl, :], op=ALU.mult)
                        return ins
                    dve_t[dl] = sc.op("dve", msk, waits=[act_t[dl]] + ([nmm_t[dl - 2]] if dl >= 2 else []), chain=False)

                    def nmm(e, quad=quad, ks=ks, dl=dl, nd=nd, Pu=Pu):
                        ins = None
                        for hh in range(4):
                            h = quad * 4 + hh
                            ins = e.matmul(ps[NB[hh]][:, 0:65], Pu[:, hh, :], VH[:, ks, h, 0:65],
                                           start=(dl == 0), stop=(dl == nd - 1))
                        return ins
                    nmm_t[dl] = sc.op("pe", nmm, waits=[dve_t[dl]], chain=False)
                sc.last = nmm_t[nd - 1]

                def rcp(e):
                    ins = None
                    for hh in range(4):
                        ins = e.reciprocal(out=SS[:, hh:hh + 1], in_=ps[NB[hh]][:, 64:65])
                    return ins
                if STAGE >= 4:
                    sc.op("dve", rcp)

                def fin(e, quad=quad):
                    ins = None
                    for hh in range(4):
                        h = quad * 4 + hh
                        ins = e.tensor_scalar(out=OT[:, h * 64:(h + 1) * 64], in0=ps[NB[hh]][:, 0:64], scalar1=SS[:, hh:hh + 1],
                                              scalar2=None, op0=ALU.mult)
                    return ins
                if STAGE >= 4:
                    sc.op("dve", fin)
            if STAGE < 5:
                continue
            if ti == 0 and c == 0 and l == 0:
                dump("QZ0", QZ[:, 0, 0:128], 128)
                dump("QZ1", QZ[:, 1, 0:128], 128)
                dump("K0", KTH[:, 0, 0:128], 128)
                dump("V0", VH[:, 0, 0, :], 72)
                dump("V1", VH[:, 0, 1, :], 72)
                dump("E", E[:], 512)
                dump("P", P[:].rearrange("p a b -> p (a b)"), 512)
                dump("OT", OT[:], 512)
                dump("MASK0", MASK[:, 0, :], 128)
                dump("HT", HT[:, :, 0:128], 2048)
            sc.op("act", lambda e: e.activation(out=JK[:], in_=OT[:], func=AF.Square))
            sc.op("dve", lambda e: e.reduce_sum(out=SS[:, 8:9], in_=JK[:], axis=mybir.AxisListType.X))
            sc.op("dve", lambda e: e.tensor_scalar(out=SS[:, 8:9], in0=SS[:, 8:9], scalar1=1.0 / 512, scalar2=EPS, op0=ALU.mult, op1=ALU.add))
            sc.op("act", lambda e: e.sqrt(out=SS[:, 8:9], in_=SS[:, 8:9]))
            sc.op("dve", lambda e: e.reciprocal(out=SS[:, 8:9], in_=SS[:, 8:9]))
            sc.op("dve", lambda e: e.scalar_tensor_tensor(out=YM[:], in0=OT[:], scalar=SS[:, 8:9], in1=WBA[:], op0=ALU.mult, op1=ALU.mult))

            def tr(e):
                ins = None
                for j in range(4):
                    ins = e.transpose(ps[5][:, j * 128:(j + 1) * 128], YM[:, j * 128:(j + 1) * 128], IDF)
                return ins
            sc.op("pe", tr)
            sc.op("act", lambda e, c=c: e.copy(out=YT[:, 4:8, c * 128:(c + 1) * 128], in_=ps[5][:].rearrange("p (j t) -> p j t", t=128)))

    SSMB = sb("SSMB", [128, 528], F32)
    AB = sb("AB", [128, 8], F32)
    WDT = sb("WDT", [128, KC, 8], BF16)
    ST_C = sb("ST_C", [128, 512], F32)
    STB_C = sb("STB_C", [128, 512], BF16)
    HALO_C = sb("HALO_C", [128, 8, 4], F32)
    assert off[0] <= 229344, off[0]
    ONESF = CON[:, 128:256]
    TRI = CON[:, 256:384]
    NEGM = CON[:, 384:512]

    def ssd_layer_init(l):
        ld = sc.dma("sp", lambda e: e.dma_start(out=SSMB[:], in_=ssmb_in[l]), "d_misc")
        ld2 = sc.dma("sp", lambda e: e.dma_start(out=WDT[:].rearrange("p k c -> p (k c)"), in_=wb[(l, "dt")]), "d_misc")
        sc.op("act", lambda e: e.activation(out=AB[:], in_=SSMB[:, 520:528], func=AF.Exp), waits=[ld, ld2])
        sc.op("dve", lambda e: e.tensor_scalar(out=AB[:], in0=AB[:], scalar1=-1.0, scalar2=None, op0=ALU.mult))
        sc.op("dve", lambda e: e.memset(ST_C[:], 0.0))
        sc.op("dve", lambda e: e.memset(STB_C[:], 0.0))
        sc.op("dve", lambda e: e.memset(HALO_C[:], 0.0))

    def mixer_ssd(l, ti, TA, TB):
        pc = 128 * l
        XB = TB.get([128, T + 4], F32)
        XC = TB.get([128, T], F32)
        XF = TA.get([128, 6, T], F32)
        BCT = TA.get([128, 4, T], BF16)
        YC = TA.get([128, 4, T], F32)
        XTM = TB.get([128, 512], F32)
        XDT = TB.get([128, 512], BF16)
        XDS = TA.get([128, 512], BF16)
        BTM = TB.get([128, 2, 128], BF16)
        SM = TB.get([128, 64], F32)
        LM = TB.get([128, 4, 128], F32)
        WW = TB.get([128, 8, 128], BF16)
        YTM = TB.get([128, 512], F32)
        TMP = TA.get([128, 512], F32)
        ADTB = TMP[:].rearrange("p (a b) -> p a b", b=128)
        for j in range(8):
            if j % 2 == 0:
                sl = win_slab(l, 16 + j // 2)
            proj_fm(sl[0], j % 2, ps[0], sl[1])
            sc.op("act", lambda e: e.copy(out=XB[:, 3:T + 3], in_=ps[0][:]))
            sc.op("dve", lambda e, j=j: e.tensor_copy(out=XB[:, 0:3], in_=HALO_C[:, j, 0:3]))
            sc.op("dve", lambda e, j=j: e.tensor_copy(out=HALO_C[:, j, 0:3], in_=XB[:, T:T + 3]))

            def conv(e, j=j):
                e.tensor_scalar(out=XC[:], in0=XB[:, 0:T], scalar1=PVM[:, pc + 48 + j * 4:pc + 49 + j * 4],
                                scalar2=PVM[:, pc + 80 + j:pc + 81 + j], op0=ALU.mult, op1=ALU.add)
                ins = None
                for tp in range(1, 4):
                    ins = e.scalar_tensor_tensor(out=XC[:], in0=XB[:, tp:tp + T], scalar=PVM[:, pc + 48 + j * 4 + tp:pc + 49 + j * 4 + tp],
                                                 in1=XC[:], op0=ALU.mult, op1=ALU.add)
                return ins
            sc.op("dve", conv)
            if j < 6:
                sc.op("act", lambda e, j=j: e.activation(out=XF[:, j, :], in_=XC[:], func=AF.Silu))
                if j >= 4:
                    sc.op("dve", lambda e, j=j: e.tensor_copy(out=BCT[:, j - 4, :], in_=XF[:, j, :]))
            else:
                sc.op("act", lambda e, j=j: e.activation(out=BCT[:, j - 4, :], in_=XC[:], func=AF.Silu))
        for c in range(4):
            cs = slice(c * 128, (c + 1) * 128)
            def trx(e, cs=cs):
                ins = None
                for j in range(4):
                    ins = e.transpose(ps[7][:, j * 128:(j + 1) * 128], XF[:, j, cs], IDF)
                return ins
            sc.op("pe", trx)
            sc.op("act", lambda e: e.copy(out=XTM[:], in_=ps[7][:]))

            def trb(e, cs=cs):
                ins = None
                for g in range(2):
                    ins = e.transpose(ps[7][:, g * 128:(g + 1) * 128], XF[:, 4 + g, cs], IDF)
                return ins
            sc.op("pe", trb)
            sc.op("act", lambda e: e.copy(out=BTM[:].rearrange("p g n -> p (g n)"), in_=ps[7][:, 0:256]))
            def dtmm(e, cs=cs):
                ins = None
                for kc in range(KC):
                    ins = e.matmul(ps[1][:, 0:8], HT[:, kc, cs], WDT[:, kc, :], start=(kc == 0), stop=(kc == KC - 1))
                return ins
            sc.op("pe", dtmm)
            sc.op("dve", lambda e: e.tensor_tensor(out=SM[:, 0:8], in0=ps[1][:, 0:8], in1=SSMB[:, 512:520], op=ALU.add))
            sc.op("act", lambda e: e.activation(out=SM[:, 0:8], in_=SM[:, 0:8], func=AF.Exp))
            sc.op("act", lambda e: e.activation(out=SM[:, 0:8], in_=SM[:, 0:8], func=AF.Ln, bias=1.0))
            sc.op("dve", lambda e: e.tensor_tensor(out=SM[:, 8:16], in0=SM[:, 0:8], in1=AB[:], op=ALU.mult))
            sc.op("pe", lambda e: e.matmul(ps[1][:, 16:24], TRI, SM[:, 8:16], start=True, stop=True))
            sc.op("dve", lambda e: e.tensor_copy(out=SM[:, 16:24], in_=ps[1][:, 16:24]))
            sc.op("act", lambda e: e.activation(out=SM[:, 24:32], in_=SM[:, 16:24], func=AF.Exp))
            sc.op("dve", lambda e: e.tensor_scalar(out=SM[:, 32:40], in0=SM[:, 16:24], scalar1=-1.0, scalar2=None, op0=ALU.mult))
            def xdt(e):
                ins = None
                for h in range(8):
                    ins = e.tensor_scalar(out=XDT[:, h * 64:(h + 1) * 64], in0=XTM[:, h * 64:(h + 1) * 64], scalar1=SM[:, h:h + 1],
                                          scalar2=None, op0=ALU.mult)
                return ins
            sc.op("dve", xdt)
            for g in range(2):
                def adtb(e, g=g):
                    ins = None
                    for hh in range(4):
                        h = g * 4 + hh
                        ins = e.tensor_scalar(out=ADTB[:, hh, :], in0=ONESF, scalar1=SM[:, 8 + h:9 + h], scalar2=None, op0=ALU.mult)
                    return ins
                sc.op("dve", adtb)

                def bc(e):
                    ins = None
                    for hh in range(4):
                        e.matmul(ps[2][:, hh * 128:(hh + 1) * 128], ADTB[:, hh, :], TRI, start=True, stop=False)
                        ins = e.matmul(ps[2][:, hh * 128:(hh + 1) * 128], IDF, NEGM, start=False, stop=True)
                    return ins
                sc.op("pe", bc)

                def lm(e, g=g):
                    ins = None
                    for hh in range(4):
                        h = g * 4 + hh
                        e.activation(out=LM[:, hh, :], in_=ps[2][:, hh * 128:(hh + 1) * 128], func=AF.Exp, bias=SM[:, 32 + h:33 + h])
                        ins = e.activation(out=SM[:, 40 + h:41 + h], in_=ps[2][:, hh * 128 + 127:hh * 128 + 128], func=AF.Exp)
                    return ins
                sc.op("act", lm)
                sc.op("pe", lambda e, g=g, cs=cs: e.matmul(ps[3][:, 0:128], BCT[:, g, cs], BCT[:, 2 + g, cs], start=True, stop=True))

                def wmul(e, g=g):
                    ins = None
                    for hh in range(4):
                        ins = e.tensor_tensor(out=WW[:, g * 4 + hh, :], in0=ps[3][:, 0:128], in1=LM[:, hh, :], op=ALU.mult)
                    return ins
                sc.op("dve", wmul)

                def xds(e, g=g):
                    ins = None
                    for hh in range(4):
                        h = g * 4 + hh
                        ins = e.tensor_scalar(out=XDS[:, h * 64:(h + 1) * 64], in0=XDT[:, h * 64:(h + 1) * 64], scalar1=LM[:, hh, 127:128],
                                              scalar2=None, op0=ALU.mult)
                    return ins
                sc.op("dve", xds)

            def ydiag(e):
                ins = None
                for h in range(8):
                    ins = e.matmul(ps[4][:, h * 64:(h + 1) * 64], WW[:, h, :], XDT[:, h * 64:(h + 1) * 64], start=True, stop=True)
                return ins
            sc.op("pe", ydiag)

            def yoff(e, cs=cs):
                ins = None
                for h in range(8):
                    ins = e.matmul(ps[5][:, h * 64:(h + 1) * 64], BCT[:, 2 + h // 4, cs], STB_C[:, h * 64:(h + 1) * 64], start=True, stop=True)
                return ins
            sc.op("pe", yoff)
            sc.op("act", lambda e: e.copy(out=YTM[:], in_=ps[4][:]))

            def yadd(e):
                ins = None
                for h in range(8):
                    ins = e.scalar_tensor_tensor(out=YTM[:, h * 64:(h + 1) * 64], in0=ps[5][:, h * 64:(h + 1) * 64], scalar=SM[:, 24 + h:25 + h],
                                                 in1=YTM[:, h * 64:(h + 1) * 64], op0=ALU.mult, op1=ALU.add)
                return ins
            sc.op("dve", yadd)
            sc.op("dve", lambda e: e.tensor_tensor(out=TMP[:], in0=XTM[:], in1=SSMB[:, 0:512], op=ALU.mult))
            sc.op("dve", lambda e: e.tensor_tensor(out=YTM[:], in0=YTM[:], in1=TMP[:], op=ALU.add))
            def smm(e):
                ins = None
                for g in range(2):
                    ins = e.matmul(ps[6][:, g * 256:(g + 1) * 256], BTM[:, g, :], XDS[:, g * 256:(g + 1) * 256], start=True, stop=True)
                return ins
            sc.op("pe", smm)

            def stu(e):
                ins = None
                for h in range(8):
                    ins = e.scalar_tensor_tensor(out=ST_C[:, h * 64:(h + 1) * 64], in0=ST_C[:, h * 64:(h + 1) * 64], scalar=SM[:, 40 + h:41 + h],
                                                 in1=ps[6][:, h * 64:(h + 1) * 64], op0=ALU.mult, op1=ALU.add)
                return ins
            sc.op("dve", stu)
            sc.op("act", lambda e: e.copy(out=STB_C[:], in_=ST_C[:]))
            def tr(e):
                ins = None
                for j in range(4):
                    ins = e.transpose(ps[7][:, j * 128:(j + 1) * 128], YTM[:, j * 128:(j + 1) * 128], IDF)
                return ins
            sc.op("pe", tr)
            sc.op("act", lambda e, cs=cs: e.copy(out=YC[:, :, cs], in_=ps[7][:].rearrange("p (j t) -> p j t", t=128)))
        for j in range(4):
            if j % 2 == 0:
                sl = win_slab(l, 14 + j // 2)
            proj_fm(sl[0], j % 2, ps[0], sl[1])
            sc.op("act", lambda e: e.activation(out=TMP[:], in_=ps[0][:], func=AF.Silu))
            sc.op("dve", lambda e, j=j: e.tensor_tensor(out=YC[:, j, :], in0=YC[:, j, :], in1=TMP[:], op=ALU.mult))
        TB2 = Tmp(WDBASE, TBSIZE)
        group_norm_to_yt(YC, [0, 1], 8, pc + 88, TB2)
        group_norm_to_yt(YC, [2, 3], 10, pc + 90, Tmp(WDBASE, TBSIZE))

    S_A = sb("S_A", [128, 4, 128], F32)
    LBL = sb("LBL", [128, DEPTH, 4], F32)
    LB = sb("LB", [128, DEPTH, 4], F32)
    OML = sb("OML", [128, DEPTH, 4], F32)
    LBT = sb("LBT", [128, 16], F32)
    assert off[0] <= 229344, off[0]

    def hgrn_init():
        ld = sc.dma("sp", lambda e: e.dma_start(out=LBL[:].rearrange("p a b -> p (a b)"), in_=lbl_in[:, :]), "d_misc")
        sc.op("dve", lambda e: e.tensor_copy(out=LBT[:, 0:4], in_=LBL[:, 0, :]), waits=[ld])
        for l in range(1, DEPTH):
            sc.op("dve", lambda e, l=l: e.tensor_tensor(out=LBT[:, 0:4], in0=LBT[:, 0:4], in1=LBL[:, l, :], op=ALU.max))
        for l in range(DEPTH):
            sc.op("dve", lambda e, l=l: e.tensor_tensor(out=LBL[:, l, :], in0=LBL[:, l, :], in1=LBT[:, 0:4], op=ALU.subtract))
        sc.op("act", lambda e: e.activation(out=LBL[:], in_=LBL[:], func=AF.Exp))
        sc.op("dve", lambda e: e.tensor_copy(out=LBT[:, 4:8], in_=LBL[:, 0, :]))
        for l in range(1, DEPTH):
            sc.op("dve", lambda e, l=l: e.tensor_tensor(out=LBT[:, 4:8], in0=LBT[:, 4:8], in1=LBL[:, l, :], op=ALU.add))
        sc.op("dve", lambda e: e.reciprocal(out=LBT[:, 8:12], in_=LBT[:, 4:8]))
        for l in range(DEPTH):
            sc.op("dve", lambda e, l=l: e.tensor_tensor(out=LBL[:, l, :], in0=LBL[:, l, :], in1=LBT[:, 8:12], op=ALU.mult))
        sc.op("dve", lambda e: e.memset(LB[:, 0, :], 0.0))
        for l in range(1, DEPTH):
            sc.op("dve", lambda e, l=l: e.tensor_tensor(out=LB[:, l, :], in0=LB[:, l - 1, :], in1=LBL[:, l, :], op=ALU.add))
        sc.op("dve", lambda e: e.tensor_scalar(out=OML[:], in0=LB[:], scalar1=-1.0, scalar2=1.0, op0=ALU.mult, op1=ALU.add))

    def mixer_hgrn(l, ti, TA, TB):
        pc = 128 * l
        O = TA.get([128, 4, T], F32)
        KTb = TA.get([128, 4, T], BF16)
        KTM = TA.get([128, 4, 8, 128], BF16)
        VTM = TA.get([128, 8, 512], BF16)
        QTb = YT
        t1 = TB.get([128, T], F32)
        t2 = TB.get([128, T], F32)
        t3 = TB.get([128, T], F32)
        t4 = TB.get([128, T], F32)
        t5 = TB.get([128, T], F32)
        EB = TB.get([128, 3, 4, 8], F32)
        TMPU = TB.get([128, 4, 128], F32)
        PM = TB.get([128, 4, 64], BF16)
        SMID = TB.get([128, 4, 128], BF16)
        for h in range(4):
            sl = win_slab(l, h // 2) if h % 2 == 0 else sl_q
            sl_q = sl
            proj_fm(sl[0], h % 2, ps[0], sl[1])
            sc.op("act", lambda e: e.activation(out=t1[:], in_=ps[0][:], func=AF.Silu))
            slf = win_slab(l, 2 + h // 2) if h % 2 == 0 else sl_f
            sl_f = slf
            proj_fm(slf[0], h % 2, ps[0], slf[1])
            sc.op("act", lambda e: e.activation(out=t2[:], in_=ps[0][:], func=AF.Sigmoid))
            sc.op("dve", lambda e, h=h: e.tensor_scalar(out=t2[:], in0=t2[:], scalar1=OML[:, l, h:h + 1], scalar2=LB[:, l, h:h + 1],
                                                        op0=ALU.mult, op1=ALU.add))
            sc.op("act", lambda e: e.activation(out=t3[:], in_=t2[:], func=AF.Ln))
            sc.op("dve", lambda e: e.tensor_scalar(out=t2[:], in0=t2[:], scalar1=-1.0, scalar2=1.0, op0=ALU.mult, op1=ALU.add))

            def scan(e):
                ins = None
                for c in range(8):
                    ins = e.tensor_tensor_scan(out=t4[:, c * 64:(c + 1) * 64], data0=ONESF[:, 0:64], data1=t3[:, c * 64:(c + 1) * 64],
                                               initial=0.0, op0=ALU.mult, op1=ALU.add)
                return ins
            sc.op("dve", scan)

            def dsub(e):
                ins = None
                for c in range(8):
                    ins = e.tensor_scalar(out=t3[:, c * 64:(c + 1) * 64], in0=t4[:, c * 64:(c + 1) * 64],
                                          scalar1=t4[:, c * 64 + 31:c * 64 + 32], scalar2=None, op0=ALU.subtract)
                return ins
            sc.op("dve", dsub)
            b3 = t4[:].rearrange("p (c t) -> p c t", t=64)
            sc.op("act", lambda e, h=h, b3=b3: e.activation(out=EB[:, 0, h, :], in_=b3[:, :, 31], func=AF.Exp))
            sc.op("act", lambda e, h=h, b3=b3: e.activation(out=EB[:, 1, h, :], in_=b3[:, :, 63], func=AF.Exp))
            sc.op("dve", lambda e, h=h, b3=b3: e.tensor_tensor(out=EB[:, 2, h, :], in0=b3[:, :, 63], in1=b3[:, :, 31], op=ALU.subtract))
            sc.op("act", lambda e, h=h: e.activation(out=EB[:, 2, h, :], in_=EB[:, 2, h, :], func=AF.Exp))
            sc.op("act", lambda e: e.activation(out=t5[:], in_=t3[:], func=AF.Exp))
            sc.op("dve", lambda e, h=h: e.tensor_tensor(out=QTb[:, h, :], in0=t1[:], in1=t5[:], op=ALU.mult))
            sc.op("act", lambda e: e.activation(out=t5[:], in_=t3[:], func=AF.Exp, scale=-1.0))
            sc.op("dve", lambda e: e.tensor_tensor(out=t1[:], in0=t2[:], in1=t5[:], op=ALU.mult))
            sc.op("act", lambda e, h=h: e.copy(out=KTb[:, h, :], in_=t1[:]))
            for rnd in range(2):
                def trk(e, rnd=rnd):
                    ins = None
                    for cc in range(4):
                        c = rnd * 4 + cc
                        ins = e.transpose(ps[7][0:64, cc * 128:(cc + 1) * 128], t1[:, c * 64:(c + 1) * 64], IDF)
                    return ins
                sc.op("pe", trk)
                sc.op("act", lambda e, h=h, rnd=rnd: e.copy(out=KTM[0:64, h, rnd * 4:rnd * 4 + 4, :].rearrange("p a b -> p (a b)"),
                                                             in_=ps[7][0:64, :]))
        v0 = win_slab(l, 4)
        v1 = win_slab(l, 5)
        for c in range(8):
            def mm(e, c=c):
                ins = None
                for half, (b, _) in enumerate((v0, v1)):
                    for kc in range(KC):
                        ins = e.matmul(ps[0][0:64, half * 256:(half + 1) * 256], HT[:, kc, c * 64:(c + 1) * 64], WGU[b][:, kc, :],
                                       start=(kc == 0), stop=(kc == KC - 1))
                return ins
            sc.op("pe", mm, waits=[v0[1], v1[1]] if c == 0 else [])
            sc.op("act", lambda e, c=c: e.copy(out=VTM[0:64, c, :], in_=ps[0][0:64, :]))
        for c in range(8):
            cs = slice(c * 64, (c + 1) * 64)

            def smid(e, c=c):
                ins = None
                for h in range(4):
                    ins = e.tensor_scalar(out=SMID[:, h, :], in0=S_A[:, h, :], scalar1=EB[:, 0, h, c:c + 1], scalar2=None, op0=ALU.mult)
                return ins
            sc.op("dve", smid)

            def pmm(e, cs=cs):
                ins = None
                for h in range(4):
                    ins = e.matmul(ps[3][0:64, h * 64:(h + 1) * 64], KTb[:, h, cs], QTb[:, h, cs], start=True, stop=True)
                return ins
            sc.op("pe", pmm)

            def pmask(e):
                ins = None
                for h in range(4):
                    ins = e.tensor_tensor(out=PM[0:64, h, :], in0=ps[3][0:64, h * 64:(h + 1) * 64], in1=TRI[0:64, 0:64], op=ALU.mult)
                return ins
            sc.op("dve", pmask)

            def omm(e, c=c, cs=cs):
                ins = None
                for h in range(4):
                    e.matmul(ps[4][:, h * 64:(h + 1) * 64], SMID[:, h, :], QTb[:, h, cs], start=True, stop=False)
                    ins = e.matmul(ps[4][:, h * 64:(h + 1) * 64], VTM[0:64, c, h * 128:(h + 1) * 128], PM[0:64, h, :], start=False, stop=True)
                return ins
            sc.op("pe", omm)

            def umm(e, c=c):
                ins = None
                for h in range(4):
                    ins = e.matmul(ps[5][:, h * 128:(h + 1) * 128], KTM[0:64, h, c, :], VTM[0:64, c, h * 128:(h + 1) * 128], start=True, stop=True)
                return ins
            sc.op("pe", umm)

            def utmp(e, c=c):
                ins = None
                for h in range(4):
                    ins = e.tensor_scalar(out=TMPU[:, h, :], in0=ps[5][:, h * 128:(h + 1) * 128], scalar1=EB[:, 2, h, c:c + 1], scalar2=None, op0=ALU.mult)
                return ins
            sc.op("dve", utmp)

            def supd(e, c=c):
                ins = None
                for h in range(4):
                    ins = e.scalar_tensor_tensor(out=S_A[:, h, :], in0=S_A[:, h, :], scalar=EB[:, 1, h, c:c + 1], in1=TMPU[:, h, :],
                                                 op0=ALU.mult, op1=ALU.add)
                return ins
            sc.op("dve", supd)
            sc.op("act", lambda e, cs=cs: e.copy(out=O[:, :, cs], in_=ps[4][:, 0:256].rearrange("p (h t) -> p h t", t=64)))
        SQh = t1[:].bitcast(BF16) if False else None
        for h in range(4):
            slg = win_slab(l, 6 + h // 2) if h % 2 == 0 else sl_g
            sl_g = slg
            sc.op("act", lambda e, h=h: e.activation(out=KTb[:, 0, :], in_=O[:, h, :], func=AF.Square))
            sc.op("pe", lambda e: e.matmul(ps[2][:], ONESB_t[:], KTb[:, 0, :], start=True, stop=True))
            sc.op("dve", lambda e: e.tensor_scalar(out=t2[:], in0=ps[2][:], scalar1=1.0 / 128, scalar2=EPS, op0=ALU.mult, op1=ALU.add))
            sc.op("act", lambda e: e.sqrt(out=t2[:], in_=t2[:]))
            sc.op("dve", lambda e: e.reciprocal(out=t2[:], in_=t2[:]))
            sc.op("dve", lambda e, h=h: e.scalar_tensor_tensor(out=t3[:], in0=O[:, h, :], scalar=PVM[:, pc + 104 + h:pc + 105 + h], in1=t2[:],
                                                               op0=ALU.mult, op1=ALU.mult))
            proj_fm(slg[0], h % 2, ps[0], slg[1])
            sc.op("act", lambda e: e.activation(out=t4[:], in_=ps[0][:], func=AF.Silu))
            sc.op("dve", lambda e, h=h: e.tensor_tensor(out=YT[:, h, :], in0=t3[:], in1=t4[:], op=ALU.mult))

    def layer_init(l):
        pc = 128 * l
        ssd_layer_init(l)
        sc.op("dve", lambda e: e.memset(S_A[:], 0.0))
        ld = sc.dma("sp", lambda e: e.dma_start(out=XT[:, 0:2, :].rearrange("p a (b c) -> p (a b) c", c=128),
                                                 in_=lbd_in[l]), "d_misc")
        sc.op("dve", lambda e: e.tensor_copy(out=LBD[:].rearrange("p a b c -> p (a b) c"),
                                             in_=XT[:, 0:2, :].rearrange("p a (b c) -> p (a b) c", c=128)), waits=[ld])
        ldw = sc.dma("sp", lambda e: e.dma_start(out=WBA[:], in_=anw_in[l]), "d_misc")
        sc.op("dve", lambda e: e.memset(HALO_D[:], 0.0), waits=[ldw])
        sc.op("dve", lambda e: e.memset(HPREV[:], 0.0))
        sc.op("act", lambda e: e.activation(out=PVM[:, pc + 40:pc + 44], in_=PVM[:, pc + 28:pc + 32], func=AF.Exp, scale=-1.0))
        sc.op("act", lambda e: e.activation(out=PVM[:, pc + 40:pc + 44], in_=PVM[:, pc + 40:pc + 44], func=AF.Ln, bias=1.0))
        sc.op("dve", lambda e: e.tensor_scalar(out=PVM[:, pc + 40:pc + 44], in0=PVM[:, pc + 40:pc + 44], scalar1=-8.0, scalar2=None, op0=ALU.mult))

    def out_proj(l):
        wo = wb[(l, "out")]
        base = sc.last
        pe_t, dv_t, ld, bufs = {}, {}, {}, {}
        obanks = (ps[3], ps[6])

        def issue(oc):
            b = win_i[0] % 4
            win_i[0] += 1
            bufs[oc] = b
            w = [pe_t[oc - 4]] if oc >= 4 else [base]
            ld[oc] = sc.dma("sp", lambda e, b=b, oc=oc: e.dma_start(out=WGU[b][:, :, 0:128], in_=wo[oc].rearrange("p (k c) -> p k c", c=128)),
                            f"d_wgu{b}", waits=w, chain=False)
        issue(0)
        issue(1)
        for oc in range(16):
            if oc >= 1 and oc + 1 < 16:
                issue(oc + 1)
            b = bufs[oc]
            bank = obanks[oc % 2]

            def mm(e, b=b, bank=bank):
                ins = None
                for kc in range(KC):
                    ins = e.matmul(bank[:], WGU[b][:, kc, 0:128], YT[:, kc, :], start=(kc == 0), stop=(kc == KC - 1))
                return ins
            pe_t[oc] = sc.op("pe", mm, waits=[ld[oc], dv_t[oc - 2] if oc >= 2 else base], chain=False)
            dv_t[oc] = sc.op("dve", lambda e, oc=oc, bank=bank: e.tensor_tensor(out=XT[:, oc, :], in0=bank[:], in1=XT[:, oc, :], op=ALU.add),
                             waits=[pe_t[oc]], chain=False)
        sc.last = dv_t[15]

    def mixer(l, ti):
        rmsnorm_to(lambda kc: HT[:, kc, :], 64 * l + 16)
        def mkTA():
            return Tmp(arena0 + 16384, 45056 - 16384)

        def mkTB():
            return Tmp(WDBASE, TBSIZE)
        sc.op("dve", lambda e: e.memset(YT[:, 0:12, :], 0.0))
        if 'A' in MIX:
            mixer_hgrn(l, ti, mkTA(), mkTB())
        if 'B' in MIX:
            mixer_attn(l, ti, mkTA(), mkTB())
        if 'C' in MIX:
            mixer_ssd(l, ti, mkTA(), mkTB())
        mixer_lru(l, ti, mkTA(), mkTB())
        out_proj(l)

    ct = load_consts()
    sc.op("dve", lambda e: e.tensor_copy(out=ONESB_t[:], in_=CON[:, 128:256]), waits=[ct])
    sc.op("dve", lambda e: e.tensor_copy(out=IDB_t[:], in_=CON[:, 0:128]))
    attn_init()
    hgrn_init()
    convert_all()
    for l in range(DEPTH):
        layer_init(l)
        for ti in range(NT):
            load_x_tile(l, ti)
            ffn(l, "ffn1", 64 * l + 0)
            if dbg != "ffn1only":
                mixer(l, ti)
                ffn(l, "ffn2", 64 * l + 32)
            if l == DEPTH - 1:
                final_store(ti)
            else:
                store_x_tile(ti)

    semnames = sorted({tok[0] for (_, _, _, tok, _) in sc.ops})
    sems = {n: nc.alloc_semaphore(n) for n in semnames}
    with nc.Block() as block:
        def emit(engname):
            def body(e):
                for (eng, fn, waits, tok, inc) in sc.ops:
                    if eng != engname:
                        continue
                    for (sn, val) in waits:
                        e.wait_ge(sems[sn], val)
                    ins = fn(e)
                    ins.then_inc(sems[tok[0]], inc)
            return body
        block.tensor(emit("pe"))
        block.scalar(emit("act"))
        block.vector(emit("dve"))
        block.gpsimd(emit("pool"))
        block.sync(emit("sp"))
    return nc


def make_consts():
    c = np.zeros((128, 512), np.float32)
    c[:, 0:128] = np.eye(128, dtype=np.float32)
    c[:, 128:256] = 1.0
    i = np.arange(128)
    c[:, 256:384] = (i[:, None] <= i[None, :]).astype(np.float32)
    c[:, 384:512] = np.where(i[:, None] <= i[None, :], 0.0, -30000.0)
    return c


def make_pvec(inp, DEPTH):
    pv = np.zeros((128, 64 * DEPTH + 16), np.float32)
    for l in range(DEPTH):
        pv[:, 64 * l + 0:64 * l + 16] = inp["ffn1_norm"][l].reshape(16, 128).T
        pv[:, 64 * l + 16:64 * l + 32] = inp["mix_norm"][l].reshape(16, 128).T
        pv[:, 64 * l + 32:64 * l + 48] = inp["ffn2_norm"][l].reshape(16, 128).T
    pv[:, 64 * DEPTH:64 * DEPTH + 16] = inp["final_norm"].reshape(16, 128).T
    return pv


def make_amask():
    m = np.zeros((128, 17, 128), np.float32)
    i = np.arange(128)[:, None]
    j = np.arange(128)[None, :]
    for dl in range(17):
        dist = 128 * dl + j - i
        for (win, dil) in ((128, 1), (512, 4), (2048, 16)):
            m[:, dl, :] += ((dist >= 0) & (dist % dil == 0) & (dist // dil <= 128)).astype(np.float32)
    return m.reshape(128, 17 * 128)


def make_pvm(inp, DEPTH):
    pm = np.zeros((128, 128 * DEPTH), np.float32)
    lbd = np.zeros((DEPTH, 128, 8, 128), np.float32)
    for l in range(DEPTH):
        pc = 128 * l
        cw = inp["lru_conv_w"][l]
        for j in range(4):
            pm[:, pc + j * 4:pc + j * 4 + 4] = cw[:, j * 128:(j + 1) * 128].T
        pm[:, pc + 16:pc + 20] = inp["lru_conv_b"][l].reshape(4, 128).T
        pm[:, pc + 20:pc + 24] = inp["lru_b_a"][l].reshape(4, 128).T
        pm[:, pc + 24:pc + 28] = inp["lru_b_x"][l].reshape(4, 128).T
        pm[:, pc + 28:pc + 32] = inp["lru_a_param"][l].reshape(4, 128).T
        pm[:, pc + 32:pc + 36] = inp["lru_norm"][l].reshape(4, 128).T
        pm[:, pc + 104:pc + 108] = inp["hgrn_norm"][l].reshape(4, 128).T
        scw = inp["ssm_conv_w"][l]
        for j in range(8):
            pm[:, pc + 48 + j * 4:pc + 52 + j * 4] = scw[:, j * 128:(j + 1) * 128].T
        pm[:, pc + 80:pc + 88] = inp["ssm_conv_b"][l].reshape(8, 128).T
        pm[:, pc + 88:pc + 92] = inp["ssm_norm"][l].reshape(4, 128).T
        for wi, nm in enumerate(("lru_w_a", "lru_w_x")):
            w = inp[nm][l]
            for j in range(4):
                for hb in range(2):
                    lbd[l, hb * 64:(hb + 1) * 64, wi * 4 + j, hb * 64:(hb + 1) * 64] = w[2 * j + hb]
    return pm, lbd


def run(inp, S, DEPTH, B, dbg=None):
    nc = build(S, DEPTH, dbg)
    consts = make_consts()
    pv = make_pvec(inp, DEPTH)
    pm, lbd = make_pvm(inp, DEPTH)
    amask = make_amask()
    lbl = np.ascontiguousarray(inp["hgrn_lb_logits"][:DEPTH].reshape(DEPTH, 4, 128).transpose(2, 0, 1).reshape(128, DEPTH * 4)).astype(np.float32)
    ssmb = np.zeros((DEPTH, 128, 528), np.float32)
    for l in range(DEPTH):
        ssmb[l, :, 0:512] = np.repeat(inp["ssm_d"][l], 64)[None, :]
        ssmb[l, :, 512:520] = inp["ssm_dt_bias"][l][None, :]
        ssmb[l, :, 520:528] = inp["ssm_a_log"][l][None, :]
    anw = np.ascontiguousarray(np.broadcast_to(inp["attn_norm"][:DEPTH, None, :], (DEPTH, 128, 512))).astype(np.float32)
    in_maps = []
    for b in range(B):
        m = {"x": np.ascontiguousarray(inp["x"][b]), "consts": consts, "pvec": pv, "pvm": pm, "lbd": lbd, "amask": amask, "anw": anw, "ssmb": ssmb, "lbl": lbl}
        for nm in ("ffn1_w_gate", "ffn1_w_up", "ffn1_w_down", "ffn2_w_gate", "ffn2_w_up", "ffn2_w_down", "w_in", "w_out"):
            m[nm] = np.asarray(inp[nm])
        in_maps.append(m)
    res = run_bass_kernel_spmd(nc, in_maps, core_ids=list(range(B)))
    global LAST
    LAST = res.results
    return np.stack([res.results[b]["out"] for b in range(B)], 0)


def kernel(**inputs):
    inp = {k: np.asarray(v) for k, v in inputs.items()}
    return run(inp, 16384, 4, 2).astype(np.float32)
```
